# Optimizing a Trainium2 kernel written in Bass

```python
import math
import jax, jax.numpy as jnp
from jax import lax
import numpy as np

D_MODEL = 1024
BATCH = 8
SEQ = 2048
DEPTH = 1
DEC_BATCH = 128
DEC_SEQ = 4
PAST_LEN = 16384
PAGE_SIZE = 128

POOL_WINDOWS = (2, 4, 8, 16)
N_POOL_GROUPS = len(POOL_WINDOWS)
POOL_WIDTH = D_MODEL
POOL_GC = POOL_WIDTH // N_POOL_GROUPS
POOL_DG = D_MODEL // N_POOL_GROUPS
POOL_BUF = max(POOL_WINDOWS) - 1
GLA_HEADS = 4
GLA_DK = D_MODEL // 2 // GLA_HEADS
GLA_DV = D_MODEL // GLA_HEADS
QK_WIDTH = GLA_HEADS * GLA_DK
V_WIDTH = GLA_HEADS * GLA_DV
GK_RANK = 16
GATE_NORMALIZER = 16.0
GLA_CHUNK = 64
D_FF = 2816
PLE_DIM = 256
EPS = 1e-6
IN_SPLITS = (POOL_WIDTH, QK_WIDTH, QK_WIDTH, V_WIDTH, GK_RANK, V_WIDTH, D_MODEL, D_MODEL)
IN_WIDTH = sum(IN_SPLITS)

kernel_name = "hybrid_pool_gla_macaron_step"


def _rmsnorm(x, g):
    xf = x.astype(jnp.float32)
    y = xf * lax.rsqrt(jnp.mean(xf * xf, axis=-1, keepdims=True) + EPS)
    return (y * g.astype(jnp.float32)).astype(x.dtype)


def _swiglu(x, w_gate, w_up, w_down):
    return (jax.nn.silu(x @ w_gate) * (x @ w_up)) @ w_down


def _pool_mix(u, buf, pos0, pool_w, pool_scale):
    B, L, _ = u.shape
    ext = jnp.concatenate([buf.astype(jnp.float32), u.astype(jnp.float32)], axis=1)
    cs = jnp.concatenate([jnp.zeros((B, 1, POOL_WIDTH), jnp.float32), jnp.cumsum(ext, axis=1)], axis=1)
    end = cs[:, POOL_BUF + 1:]
    pos = pos0 + jnp.arange(L) + 1
    means = []
    for g, w in enumerate(POOL_WINDOWS):
        sl = slice(g * POOL_GC, (g + 1) * POOL_GC)
        start = cs[:, POOL_BUF + 1 - w: POOL_BUF + 1 - w + L, sl]
        cnt = jnp.minimum(w, pos).astype(jnp.float32)
        means.append((end[..., sl] - start) / cnt[None, :, None])
    pooled = jnp.concatenate(means, axis=-1) - u.astype(jnp.float32)
    pooled = pooled.reshape(B, L, N_POOL_GROUPS, POOL_GC).astype(u.dtype)
    out = jnp.einsum('blgc,gcd->blgd', pooled, pool_w).reshape(B, L, D_MODEL) * pool_scale
    new_buf = ext[:, -POOL_BUF:].astype(buf.dtype)
    return out, new_buf


def _gla(q, k, v, gk, S0):
    B, L, H, DK = q.shape
    DV = v.shape[-1]
    C = math.gcd(L, GLA_CHUNK)
    N = L // C

    def to_chunks(t):
        return t.astype(jnp.float32).reshape(B, N, C, H, t.shape[-1]).transpose(1, 0, 3, 2, 4)

    qc, kc, vc, gc = to_chunks(q), to_chunks(k), to_chunks(v), to_chunks(gk)
    mask = jnp.tril(jnp.ones((C, C), dtype=bool))

    def step(S, inp):
        qi, ki, vi, gi = inp
        b = jnp.cumsum(gi, axis=2)
        o_inter = jnp.einsum('bhic,bhcv->bhiv', qi * jnp.exp(b), S)
        diff = b[:, :, :, None, :] - b[:, :, None, :, :]
        decay = jnp.exp(jnp.where(mask[:, :, None], diff, -jnp.inf))
        A = jnp.einsum('bhic,bhjc,bhijc->bhij', qi, ki, decay)
        o = o_inter + jnp.einsum('bhij,bhjv->bhiv', A, vi)
        b_last = b[:, :, -1:, :]
        k_dec = ki * jnp.exp(b_last - b)
        S_new = S * jnp.exp(b_last[:, :, 0, :])[..., None] + jnp.einsum('bhjc,bhjv->bhcv', k_dec, vi)
        return S_new, o

    S, o = lax.scan(step, S0.astype(jnp.float32), (qc, kc, vc, gc))
    o = o.transpose(1, 0, 3, 2, 4).reshape(B, L, H, DV)
    return o, S


def _layer(x, p, buf, S0, pos0,
           ffn1_norm, ffn1_w_gate, ffn1_w_up, ffn1_w_down,
           mix_norm, w_in, w_gk_up, b_gk, gla_norm, pool_w, pool_scale, w_out,
           ffn2_norm, ffn2_w_gate, ffn2_w_up, ffn2_w_down,
           ple_norm, w_ple_gate, w_ple_proj):
    B, L, _ = x.shape
    h = x + 0.5 * _swiglu(_rmsnorm(x, ffn1_norm), ffn1_w_gate, ffn1_w_up, ffn1_w_down)
    xn = _rmsnorm(h, mix_norm)
    z = xn @ w_in
    offs = [int(o) for o in np.cumsum(IN_SPLITS)[:-1]]
    u, q, k, v, gk_lr, g, ga, gb = jnp.split(z, offs, axis=-1)
    a_out, new_buf = _pool_mix(u, buf, pos0, pool_w, pool_scale)
    gk = jax.nn.log_sigmoid((gk_lr @ w_gk_up + b_gk).astype(jnp.float32)) / GATE_NORMALIZER
    q = q.reshape(B, L, GLA_HEADS, GLA_DK) * (GLA_DK ** -0.5)
    k = k.reshape(B, L, GLA_HEADS, GLA_DK)
    v = v.reshape(B, L, GLA_HEADS, GLA_DV)
    gk = gk.reshape(B, L, GLA_HEADS, GLA_DK)
    o, S_new = _gla(q, k, v, gk, S0)
    o = _rmsnorm(o, gla_norm).reshape(B, L, V_WIDTH).astype(x.dtype)
    b_out = o * jax.nn.silu(g)
    mix = jax.nn.sigmoid(ga) * a_out + jax.nn.sigmoid(gb) * b_out
    h = h + mix @ w_out
    h = h + 0.5 * _swiglu(_rmsnorm(h, ffn2_norm), ffn2_w_gate, ffn2_w_up, ffn2_w_down)
    gate = jax.nn.sigmoid(_rmsnorm(h, ple_norm) @ w_ple_gate)
    h = h + gate * (p.astype(h.dtype) @ w_ple_proj)
    return h, new_buf, S_new.astype(S0.dtype)


def setup_inputs(seed: int = 0) -> dict:
    key = jax.random.key(seed)
    ks = jax.random.split(key, 32)
    f32 = jnp.float32

    def nrm(k, shape, scale=1.0):
        return jax.random.normal(k, shape, f32) * scale

    def gain(k, shape):
        return 1.0 + 0.05 * jax.random.normal(k, shape, f32)

    D, F = D_MODEL, D_FF
    return {
        "x_prompt": nrm(ks[0], (BATCH, SEQ, D)),
        "x_sample": nrm(ks[1], (DEC_BATCH, DEC_SEQ, D)),
        "p_prompt": nrm(ks[2], (DEPTH, BATCH, SEQ, PLE_DIM)),
        "p_sample": nrm(ks[3], (DEPTH, DEC_BATCH, DEC_SEQ, PLE_DIM)),
        "state_pool": nrm(ks[4], (DEPTH, DEC_BATCH, POOL_BUF, POOL_WIDTH)),
        "state_gla": nrm(ks[5], (DEPTH, DEC_BATCH, GLA_HEADS, GLA_DK, GLA_DV)),
        "ffn1_norm": gain(ks[6], (DEPTH, D)),
        "ffn1_w_gate": nrm(ks[7], (DEPTH, D, F), D ** -0.5),
        "ffn1_w_up": nrm(ks[8], (DEPTH, D, F), D ** -0.5),
        "ffn1_w_down": nrm(ks[9], (DEPTH, F, D), F ** -0.5),
        "mix_norm": gain(ks[10], (DEPTH, D)),
        "w_in": nrm(ks[11], (DEPTH, D, IN_WIDTH), D ** -0.5),
        "w_gk_up": nrm(ks[12], (DEPTH, GK_RANK, QK_WIDTH), GK_RANK ** -0.5),
        "b_gk": nrm(ks[13], (DEPTH, QK_WIDTH), 0.1),
        "gla_norm": gain(ks[14], (DEPTH, GLA_DV)),
        "pool_w": nrm(ks[15], (DEPTH, N_POOL_GROUPS, POOL_GC, POOL_DG), POOL_GC ** -0.5),
        "pool_scale": gain(ks[16], (DEPTH, D)),
        "w_out": nrm(ks[17], (DEPTH, D, D), D ** -0.5),
        "ffn2_norm": gain(ks[18], (DEPTH, D)),
        "ffn2_w_gate": nrm(ks[19], (DEPTH, D, F), D ** -0.5),
        "ffn2_w_up": nrm(ks[20], (DEPTH, D, F), D ** -0.5),
        "ffn2_w_down": nrm(ks[21], (DEPTH, F, D), F ** -0.5),
        "ple_norm": gain(ks[22], (DEPTH, D)),
        "w_ple_gate": nrm(ks[23], (DEPTH, D, D), D ** -0.5),
        "w_ple_proj": nrm(ks[24], (DEPTH, PLE_DIM, D), PLE_DIM ** -0.5),
        "final_norm": gain(ks[25], (D,)),
    }


def reference(x_prompt, x_sample, p_prompt, p_sample, state_pool, state_gla,
              ffn1_norm, ffn1_w_gate, ffn1_w_up, ffn1_w_down,
              mix_norm, w_in, w_gk_up, b_gk, gla_norm, pool_w, pool_scale, w_out,
              ffn2_norm, ffn2_w_gate, ffn2_w_up, ffn2_w_down,
              ple_norm, w_ple_gate, w_ple_proj, final_norm):
    B, L = x_prompt.shape[0], x_prompt.shape[1]
    hp, hs = x_prompt, x_sample
    pool_p, gla_p, pool_s, gla_s = [], [], [], []
    for i in range(DEPTH):
        lw = (ffn1_norm[i], ffn1_w_gate[i], ffn1_w_up[i], ffn1_w_down[i],
              mix_norm[i], w_in[i], w_gk_up[i], b_gk[i], gla_norm[i], pool_w[i], pool_scale[i], w_out[i],
              ffn2_norm[i], ffn2_w_gate[i], ffn2_w_up[i], ffn2_w_down[i],
              ple_norm[i], w_ple_gate[i], w_ple_proj[i])
        buf0 = jnp.zeros((B, POOL_BUF, POOL_WIDTH), state_pool.dtype)
        S0 = jnp.zeros((B, GLA_HEADS, GLA_DK, GLA_DV), state_gla.dtype)
        hp, bp, sp = _layer(hp, p_prompt[i], buf0, S0, 0, *lw)
        hs, bs, ss = _layer(hs, p_sample[i], state_pool[i], state_gla[i], PAST_LEN, *lw)
        pool_p.append(bp)
        gla_p.append(sp)
        pool_s.append(bs)
        gla_s.append(ss)
    y_prompt = _rmsnorm(hp, final_norm)
    y_sample = _rmsnorm(hs, final_norm)
    return (y_prompt, y_sample, jnp.stack(pool_p), jnp.stack(gla_p), jnp.stack(pool_s), jnp.stack(gla_s))
```

```python
import numpy as np
import ml_dtypes
from contextlib import ExitStack
import concourse.bass as bass
import concourse.mybir as mybir
from concourse.bass_utils import run_bass_kernel_spmd

F32 = mybir.dt.float32
BF16 = mybir.dt.bfloat16
AF = mybir.ActivationFunctionType
ALU = mybir.AluOpType

NCORES = 8
D = 1024
KC = 8
FF = 2816
FC = 22
PLE = 256
NSB = 2
TP = 1024
NBS = 8
LS = 4
TSM = NBS * LS
TS = TP + TSM
EPS = 1e-6
POOL_W = (2, 4, 8, 16)
QSCALE = 128 ** -0.5
MB = 512
NSTG = 3
BLK_D = [(0, 352), (352, 352), (704, 352)]

V_FFN1, V_MIX, V_FFN2, V_PLE, V_FIN, V_PSC, V_GLA = 0, 8, 16, 24, 32, 40, 48
NVEC = 50


class _Probe:
    def __init__(self):
        self.n = 0

    def __getattr__(self, name):
        def m(*a, **k):
            out = k.get('out', a[0] if a else None)
            try:
                fs = 1
                for d in out.shape[1:]:
                    fs *= d
                self.n = max(self.n, fs)
            except Exception:
                pass
            return self
        return m


class FW:
    ENG = ('pe', 'act', 'dve', 'pool', 'sp')

    def __init__(self, nc, es):
        self.nc = nc
        self.es = es
        self.esem = {e: es.enter_context(nc.semaphore("s_" + e)) for e in self.ENG if e != 'sp'}
        self.ecount = {e: 0 for e in self.esem}
        self.dsem = {}
        self.dcount = {}
        self.ops = []
        self.last_writer = {}
        self.readers = {}
        self.seen = {e: {} for e in self.ENG}
        self.same_engine_sync = True
        self.capture = None
        self.do_schedule = True
        self.sched_window = 80

    def merged(self, builders):
        lists = []
        for b in builders:
            self.capture = []
            b()
            lists.append(self.capture)
        self.capture = None
        n = [len(l) for l in lists]
        idx = [0] * len(lists)
        while True:
            cand = [i for i in range(len(lists)) if idx[i] < n[i]]
            if not cand:
                break
            j = min(cand, key=lambda i: (idx[i] + 1) / n[i])
            self.add(*lists[j][idx[j]])
            idx[j] += 1

    DEFCOST = {'pe': 0.25, 'act': 0.6, 'dve': 0.7, 'pool': 0.2, 'sp': 0.1}

    def add(self, engine, fn, reads=(), writes=(), dma=None, cost=None, tbl=None):
        if self.capture is not None:
            self.capture.append((engine, fn, list(reads), list(writes), dma, cost, tbl))
            return None
        if cost is None and dma is None and engine in ('act', 'dve', 'pool'):
            pr = _Probe()
            try:
                fn(pr)
            except Exception:
                pr.n = 0
            if pr.n > 0:
                if engine == 'act':
                    cost = 0.22 + pr.n / 1500.0
                elif engine == 'dve':
                    cost = 0.08 + pr.n / 850.0
                else:
                    cost = 0.3 + pr.n / 350.0
        op = dict(engine=engine, fn=fn, dma=dma, deps=[], alldeps=[], needs_inc=False, val=None, tbl=tbl,
                  cost=(cost if cost is not None else self.DEFCOST[engine]))
        deps = []
        for k in reads:
            w = self.last_writer.get(k)
            if w is not None:
                deps.append(w)
        for k in writes:
            w = self.last_writer.get(k)
            if w is not None:
                deps.append(w)
            deps.extend(self.readers.get(k, ()))
        for d in deps:
            if d is op:
                continue
            op['alldeps'].append(d)
            if d['dma'] is None:
                if d['engine'] == engine and (engine == 'pe' or not self.same_engine_sync):
                    continue
                d['needs_inc'] = True
            op['deps'].append(d)
        for k in writes:
            self.last_writer[k] = op
            self.readers[k] = []
        for k in reads:
            self.readers.setdefault(k, []).append(op)
        if dma is not None and dma not in self.dsem:
            self.dsem[dma] = self.es.enter_context(self.nc.semaphore("d_" + dma))
            self.dcount[dma] = 0
        self.ops.append(op)
        return op

    def schedule(self, ops):
        import bisect
        n = len(ops)
        idx = {id(op): i for i, op in enumerate(ops)}
        dep_idx = []
        succ = [[] for _ in range(n)]
        indeg = [0] * n
        for i, op in enumerate(ops):
            ds = sorted(set(idx[id(d)] for d in op['alldeps'] if id(d) in idx))
            dep_idx.append(ds)
            indeg[i] = len(ds)
            for d in ds:
                succ[d].append(i)
        fin = [0.0] * n
        efree = {e: 0.0 for e in self.ENG}
        ready = [i for i in range(n) if indeg[i] == 0]
        order = []
        LAT = 0.3
        cur_tbl = None
        while ready:
            best = None
            lim = ready[0] + self.sched_window
            for i in ready:
                if i > lim:
                    break
                op = ops[i]
                e = op['engine']
                t = efree[e]
                for d in dep_idx[i]:
                    td = fin[d] + (LAT if ops[d]['engine'] != e else 0.05)
                    if td > t:
                        t = td
                if op['tbl'] is not None and op['tbl'] != cur_tbl:
                    t += 0.5
                tk = t - (0.2 if e == 'pe' else 0.0)
                if best is None or tk < best[2]:
                    best = (t, i, tk)
            t, i = best[0], best[1]
            ready.remove(i)
            op = ops[i]
            e = op['engine']
            if op['tbl'] is not None:
                cur_tbl = op['tbl']
            if op['dma'] is not None:
                efree[e] = t + 0.15
                fin[i] = t + 2.0 + op['cost']
            else:
                efree[e] = t + op['cost']
                fin[i] = t + op['cost']
            order.append(op)
            for sidx in succ[i]:
                indeg[sidx] -= 1
                if indeg[sidx] == 0:
                    bisect.insort(ready, sidx)
        assert len(order) == n
        self.sim_time = max(fin) if fin else 0.0
        return order

    def flush(self):
        ops = self.ops
        self.ops = []
        if self.do_schedule:
            ops = self.schedule(ops)
        last = {}
        for op in ops:
            if op['dma'] is None:
                last[op['engine']] = op
        for op in last.values():
            op['needs_inc'] = True
        for op in ops:
            if op['dma'] is not None:
                self.dcount[op['dma']] += 16
                op['val'] = self.dcount[op['dma']]
            elif op['needs_inc']:
                self.ecount[op['engine']] += 1
                op['val'] = self.ecount[op['engine']]
        nwait = 0
        for op in ops:
            e = op['engine']
            seen = self.seen[e]
            need = {}
            for d in op['deps']:
                key = ('d', d['dma']) if d['dma'] is not None else ('e', d['engine'])
                if need.get(key, (0, None))[0] < d['val']:
                    need[key] = (d['val'], d)
            waits = []
            for key, (v, d) in sorted(need.items(), key=lambda kv: -kv[1][0]):
                if seen.get(key, 0) >= v:
                    continue
                waits.append((key, v))
                seen[key] = v
                for k2, v2 in d.get('vc', {}).items():
                    if seen.get(k2, 0) < v2:
                        seen[k2] = v2
            op['waits'] = waits
            nwait += len(waits)
            vc = dict(seen)
            if op['dma'] is not None:
                vc[('d', op['dma'])] = max(vc.get(('d', op['dma']), 0), op['val'])
            elif op['val'] is not None:
                vc[('e', e)] = max(vc.get(('e', e), 0), op['val'])
            op['vc'] = vc
        self.n_waits = getattr(self, 'n_waits', 0) + nwait
        per = {e: [] for e in self.ENG}
        for op in ops:
            per[op['engine']].append(op)
        fw = self

        def emit(e, eng):
            seen = fw.seen[e]
            for op in per[e]:
                for key, v in op['waits']:
                    sem = fw.dsem[key[1]] if key[0] == 'd' else fw.esem[key[1]]
                    eng.wait_ge(sem, v)
                ins = op['fn'](eng)
                if op['dma'] is not None:
                    ins.then_inc(fw.dsem[op['dma']], 16)
                elif op['needs_inc']:
                    ins.then_inc(fw.esem[e], 1)
            for f in fw.esem:
                if f == e:
                    continue
                v = fw.ecount[f]
                if seen.get(('e', f), 0) < v:
                    seen[('e', f)] = v
                    eng.wait_ge(fw.esem[f], v)
            for dn, v in fw.dcount.items():
                if seen.get(('d', dn), 0) < v:
                    seen[('d', dn)] = v
                    eng.wait_ge(fw.dsem[dn], v)

        with self.nc.Block() as block:
            @block.tensor
            def _(eng):
                emit('pe', eng)

            @block.scalar
            def _(eng):
                emit('act', eng)

            @block.vector
            def _(eng):
                emit('dve', eng)

            @block.gpsimd
            def _(eng):
                emit('pool', eng)

            @block.sync
            def _(eng):
                emit('sp', eng)
        self.last_writer = {}
        self.readers = {}


def KS(name, *dims):
    out = [(name,)]
    for d in dims:
        if isinstance(d, int):
            d = [d]
        out = [o + (i,) for o in out for i in d]
    return out


class Prog:
    def __init__(self, stop_after=None):
        self.stop_after = stop_after
        self.nc = bass.Bass("TRN2", target_bir_lowering=False)
        self.build()

    def din(self, name, shape, dt=F32):
        return self.nc.dram_tensor(name, list(shape), dt, kind="ExternalInput").ap()

    def dout(self, name, shape, dt=F32):
        return self.nc.dram_tensor(name, list(shape), dt, kind="ExternalOutput").ap()

    def T(self, es, name, shape, dt):
        self.uid = getattr(self, "uid", 0) + 1
        return es.enter_context(self.nc.sbuf_tensor("%s_%d" % (name, self.uid), list(shape), dt))

    def next_ps(self):
        i = self.ps_rot[self.ps_i % len(self.ps_rot)]
        self.ps_i += 1
        return self.ps[i], ('ps', i)

    def load_cast(self, dst, base, src, n, parts=128):
        name = "_".join(str(x) for x in base)
        self.fw.add('pool', lambda e: e.dma_start(out=dst, in_=src), writes=[base], dma=name, cost=parts * n * 4 / 200e3)
        return [base]

    def mm_group(self, out, pairs, reads, writes):
        n = len(pairs)

        def fn(e):
            ins = None
            for i, (l, r) in enumerate(pairs):
                ins = e.matmul(out, lhsT=l, rhs=r, start=(i == 0), stop=(i == n - 1))
            return ins
        cost = 0.0
        for (l, r) in pairs:
            fs = 1
            for d in r.shape[1:]:
                fs *= d
            c = max(fs, 64) / 2400.0 + 0.004
            if r.dtype == F32:
                c *= 4
            cost += c
        self.fw.add('pe', fn, reads=reads, writes=writes, cost=cost)

    def rms_stats(self, src3, size, ones, nk, rstd, src_keys, tag, sq=None, sqkey=('sq',)):
        fw = self.fw
        if sq is None:
            sq = self.sq
        fw.add('act', lambda e: e.activation(out=sq[:, 0:nk, 0:size], in_=src3, func=AF.Square),
               reads=src_keys, writes=[sqkey], cost=0.2 + nk * size / 1200.0)
        ps, pk = self.next_ps()
        self.mm_group(ps[:, 0:size], [(ones[:], sq[:, k, 0:size]) for k in range(nk)],
                      reads=[sqkey, ('const',)], writes=[pk])
        fw.add('act', lambda e: e.activation(out=rstd[:, 0:size], in_=ps[:, 0:size], func=AF.Ln, bias=EPS),
               reads=[pk], writes=[('rstd', tag)], tbl='el')
        fw.add('act', lambda e: e.activation(out=rstd[:, 0:size], in_=rstd[:, 0:size], func=AF.Exp, scale=-0.5),
               reads=[('rstd', tag)], writes=[('rstd', tag)], tbl='el')

    def norm_ops(self, vcol, blocks, nsq, nrs, sub=512, out=None, okeys=None):
        fw = self.fw
        h, xn, vecs = self.h, self.xn, self.vecs
        cnt = 0
        for bi, (st, sz) in enumerate(blocks):
            for s0 in range(0, sz, sub):
                ssz = min(sub, sz - s0)
                a0 = st + s0
                t = cnt % 2
                cnt += 1
                rstd = nrs[t]
                self.rms_stats(h[:, :, a0:a0 + ssz], ssz, self.ones_d, KC, rstd, KS('h', range(KC), bi), ('n', t),
                               sq=nsq, sqkey=('nsq',))
                for k in range(KC):
                    fw.add('dve', lambda e, k=k, a0=a0, ssz=ssz, rstd=rstd: e.scalar_tensor_tensor(
                        out=xn[:, k, a0:a0 + ssz], in0=h[:, k, a0:a0 + ssz], scalar=vecs[:, vcol + k:vcol + k + 1],
                        in1=rstd[:, 0:ssz], op0=ALU.mult, op1=ALU.mult),
                        reads=[('h', k, bi), ('rstd', ('n', t)), ('const',)], writes=[('xn', k, bi)],
                        cost=0.1 + ssz / 900.0)

    def ffn_core(self, which, hid, wgu, wdt, sgt, between=None):
        fw = self.fw
        wg_d, wu_d, wd_d = (self.w1g, self.w1u, self.w1d) if which == 1 else (self.w2g, self.w2u, self.w2d)
        h, xn = self.h, self.xn

        def load_gu(j):
            s = j % 2
            kg = self.load_cast(wgu[:, s, 0, :], ('wg', s), wg_d[j].rearrange("p k n -> p (k n)"), 1024)
            ku = self.load_cast(wgu[:, s, 1, :], ('wu', s), wu_d[j].rearrange("p k n -> p (k n)"), 1024)
            return kg, ku

        def load_d(c):
            s = c % 2
            return self.load_cast(wdt[:, s, :], ('wd', s), wd_d[c].rearrange("p j n -> p (j n)"), FC * 128)

        nxt = load_gu(0)
        cnt = 0
        for j in range(FC):
            kg, ku = nxt
            if j + 1 < FC:
                nxt = load_gu(j + 1)
            s = j % 2
            for bi, (st, sz) in enumerate(BLK_D):
                xk = KS('xn', range(KC), bi)
                psg, pkg = self.next_ps()
                self.mm_group(psg[:, 0:sz], [(wgu[:, s, 0, k * 128:(k + 1) * 128], xn[:, k, st:st + sz]) for k in range(KC)],
                              reads=kg + xk, writes=[pkg])
                psu, pku = self.next_ps()
                self.mm_group(psu[:, 0:sz], [(wgu[:, s, 1, k * 128:(k + 1) * 128], xn[:, k, st:st + sz]) for k in range(KC)],
                              reads=ku + xk, writes=[pku])
                ss = cnt % 2
                cnt += 1
                fw.add('act', lambda e, psg=psg, sz=sz, ss=ss: e.activation(out=sgt[:, ss, 0:sz], in_=psg[:, 0:sz], func=AF.Silu),
                       reads=[pkg], writes=[('sgt', ss)], cost=0.2 + sz / 1200.0, tbl='st')
                fw.add('dve', lambda e, psu=psu, sz=sz, ss=ss, j=j, st=st: e.tensor_tensor(
                    out=hid[:, j, st:st + sz], in0=psu[:, 0:sz], in1=sgt[:, ss, 0:sz], op=ALU.mult),
                    reads=[pku, ('sgt', ss)], writes=[('hid', j, bi)], cost=0.1 + sz / 900.0)
        nxt = load_d(0)
        if between is not None:
            between()
        for c in range(KC):
            kd = nxt
            if c + 1 < KC:
                nxt = load_d(c + 1)
            s = c % 2
            for bi, (st, sz) in enumerate(BLK_D):
                ps, pk = self.next_ps()
                self.mm_group(ps[:, 0:sz], [(wdt[:, s, j * 128:(j + 1) * 128], hid[:, j, st:st + sz]) for j in range(FC)],
                              reads=kd + KS('hid', range(FC), bi), writes=[pk])
                fw.add('dve', lambda e, ps=ps, sz=sz, c=c, st=st: e.scalar_tensor_tensor(
                    out=h[:, c, st:st + sz], in0=ps[:, 0:sz], scalar=0.5, in1=h[:, c, st:st + sz],
                    op0=ALU.mult, op1=ALU.add),
                    reads=[pk, ('h', c, bi)], writes=[('h', c, bi)], cost=0.1 + sz / 900.0)

    def ffn_tiles(self, es):
        hid = self.T(es, "hid", [128, FC, TS], BF16)
        wgu = self.T(es, "wgu", [128, 2, 2, 1024], BF16)
        wdt = self.T(es, "wdt", [128, 2, FC * 128], BF16)
        sgt = self.T(es, "sgt", [128, 2, 512], F32)
        nsq = self.T(es, "nsq", [128, KC, 512], BF16)
        nrs = [self.T(es, "nrs%d" % i, [128, 512], F32) for i in range(2)]
        return hid, wgu, wdt, sgt, nsq, nrs

    def phase_A(self, sb, prefetch):
        fw = self.fw
        h = self.h
        with ExitStack() as es:
            hid, wgu, wdt, sgt, nsq, nrs = self.ffn_tiles(es)
            self.ps_rot = list(range(7))
            self.A_ops(sb, hid, wgu, wdt, sgt, nsq, nrs, prefetch)
            fw.flush()

    def A_ops(self, sb, hid, wgu, wdt, sgt, nsq, nrs, prefetch):
        fw = self.fw
        h = self.h
        for bi, (st, sz) in enumerate(BLK_D):
            fw.add('sp', lambda e, st=st, sz=sz: e.dma_start(out=h[:, :, st:st + sz], in_=self.xT[sb, :, :, st:st + sz]),
                   writes=KS('h', range(KC), bi), dma='hload%d' % bi, cost=8.0)
        self.norm_ops(V_FFN1, BLK_D, nsq, nrs)
        self.ffn_core(1, hid, wgu, wdt, sgt, between=prefetch)

    def phase_C(self, sb, next_A=None):
        fw = self.fw
        h, xn, vecs = self.h, self.xn, self.vecs
        with ExitStack() as es:
            hid, wgu, wdt, sgt, nsq, nrs = self.ffn_tiles(es)
            pb = self.T(es, "pb", [128, 2, TS], BF16)
            wpg = self.T(es, "wpg", [128, 2, 1024], BF16)
            wpp = self.T(es, "wpp", [128, 2, 256], BF16)
            gt = self.T(es, "gt", [128, 2, 512], F32)
            nyt = 1 if next_A is not None else 2
            yt = self.T(es, "yt", [128, nyt, KC, 352], F32)
            self.ps_rot = list(range(7))
            self.norm_ops(V_FFN2, BLK_D, nsq, nrs)
            kp = []
            for k in range(2):
                kp += self.load_cast(pb[:, k, :], ('pb', k), self.pT[sb, :, k, :], TS)
            self.ffn_core(2, hid, wgu, wdt, sgt)
            self.norm_ops(V_PLE, BLK_D, nsq, nrs)

            def load_w(c):
                s = c % 2
                k1 = self.load_cast(wpg[:, s, :], ('wpg', s), self.wpg_d[c].rearrange("p k n -> p (k n)"), 1024)
                k2 = self.load_cast(wpp[:, s, :], ('wpp', s), self.wpp_d[c].rearrange("p k n -> p (k n)"), 256)
                return k1, k2
            nxt = load_w(0)
            cnt = 0
            for c in range(KC):
                k1, k2 = nxt
                if c + 1 < KC:
                    nxt = load_w(c + 1)
                s = c % 2
                for bi, (st, sz) in enumerate(BLK_D):
                    psg, pkg = self.next_ps()
                    self.mm_group(psg[:, 0:sz], [(wpg[:, s, k * 128:(k + 1) * 128], xn[:, k, st:st + sz]) for k in range(KC)],
                                  reads=k1 + KS('xn', range(KC), bi), writes=[pkg])
                    psp, pkp = self.next_ps()
                    self.mm_group(psp[:, 0:sz], [(wpp[:, s, k * 128:(k + 1) * 128], pb[:, k, st:st + sz]) for k in range(2)],
                                  reads=k2 + kp, writes=[pkp])
                    ss = cnt % 2
                    cnt += 1
                    fw.add('act', lambda e, psg=psg, sz=sz, ss=ss: e.activation(out=gt[:, ss, 0:sz], in_=psg[:, 0:sz], func=AF.Sigmoid),
                           reads=[pkg], writes=[('gt', ss)], tbl='sg')
                    fw.add('dve', lambda e, psp=psp, sz=sz, ss=ss: e.tensor_tensor(
                        out=gt[:, ss, 0:sz], in0=psp[:, 0:sz], in1=gt[:, ss, 0:sz], op=ALU.mult),
                        reads=[pkp, ('gt', ss)], writes=[('gt', ss)])
                    fw.add('dve', lambda e, sz=sz, ss=ss, c=c, st=st: e.tensor_tensor(
                        out=h[:, c, st:st + sz], in0=h[:, c, st:st + sz], in1=gt[:, ss, 0:sz], op=ALU.add),
                        reads=[('gt', ss), ('h', c, bi)], writes=[('h', c, bi)])
            for bi, (st, sz) in enumerate(BLK_D):
                t = bi % 2
                rstd = nrs[t]
                self.rms_stats(h[:, :, st:st + sz], sz, self.ones_d, KC, rstd, KS('h', range(KC), bi), ('n', t),
                               sq=nsq, sqkey=('nsq',))
                t = bi % nyt
                for k in range(KC):
                    fw.add('dve', lambda e, k=k, st=st, sz=sz, rstd=rstd, t=t: e.scalar_tensor_tensor(
                        out=yt[:, t, k, 0:sz], in0=h[:, k, st:st + sz], scalar=vecs[:, V_FIN + k:V_FIN + k + 1],
                        in1=rstd[:, 0:sz], op0=ALU.mult, op1=ALU.mult),
                        reads=[('h', k, bi), ('rstd', ('n', bi % 2)), ('const',)], writes=[('yt', t, k)])
                fw.add('sp', lambda e, st=st, sz=sz, t=t: e.dma_start(out=self.yT[sb, :, :, st:st + sz], in_=yt[:, t, :, 0:sz]),
                       reads=KS('yt', t, range(KC)), dma='yout%d' % t, cost=6.0)
            if next_A is not None:
                nsb, prefetch = next_A
                self.A_ops(nsb, hid, wgu, wdt, sgt, nsq, nrs, prefetch)
            fw.flush()

    def raw_out(self, sb):
        fw = self.fw
        for k in range(KC):
            fw.add('sp', lambda e, k=k: e.dma_start(out=self.yT[sb, :, k, :], in_=self.h[:, k, :]),
                   reads=KS('h', k, range(len(BLK_D))), dma='yout%d' % (k % 2))
        fw.flush()

    def phase_mixer(self, sb, pre=(0, 1)):
        fw = self.fw
        h, xn, vecs = self.h, self.xn, self.vecs
        pblocks = [(i * MB, MB) for i in range(TP // MB)]
        ablocks = pblocks + [(TP, TSM)]
        nab = len(ablocks)
        npb = len(pblocks)
        with ExitStack() as es:
            T = lambda name, shape, dt: self.T(es, name, shape, dt)
            nsq = T("nsq", [128, KC, 256], BF16)
            nrs = [T("nrs%d" % i, [128, 256], F32) for i in range(2)]
            self.ps_rot = [3, 4]
            self.norm_ops(V_MIX, ablocks, nsq, nrs, sub=256)
            gkw = T("gkw", [128, 128], BF16)
            gklr = T("gklr", [32, TS], BF16)
            wgk = T("wgk", [32, 512], BF16)
            ML = MB + 16
            uext = T("uext", [128, 2, 1, ML], F32)
            wA = T("wA", [128, 2, 1, ML], F32)
            wB = T("wB", [128, 2, 1, ML], F32)
            sgt = T("sgt", [128, 2, MB], F32)
            aout = T("aout", [128, 2, MB], F32)
            pooled = T("pooled", [128, 2, MB], BF16)
            spt = T("spt", [128, MB], F32)
            Et = T("Et", [128, MB], F32)
            Ei = T("Ei", [128, MB], F32)
            qt = T("qt", [128, MB], BF16)
            kt = T("kt", [128, MB], BF16)
            kTM = T("kTM", [128, MB // 128, 128], BF16)
            vTM = T("vTM", [128, MB // 128, 256], BF16)
            ATm = T("ATm", [128, MB // 128, 128], BF16)
            osb = T("osb", [128, 2, MB], F32)
            self.sq = T("osq", [128, 2, MB], BF16)
            rso = T("rso", [128, MB], F32)
            siga = T("siga", [128, 2, MB], F32)
            sigb = T("sigb", [128, 2, MB], F32)
            mix = T("mix", [128, 2, MB], BF16)
            uext_s = T("uext_s", [128, 2, NBS, LS + 16], F32)
            wA_s = T("wA_s", [128, 2, NBS, LS + 16], F32)
            wB_s = T("wB_s", [128, 2, NBS, LS + 16], F32)
            ustg = T("ustg", [128, 2, NBS, 16], F32)
            S0 = T("S0", [128, NBS, 256], F32)
            S0b = T("S0b", [128, NBS, 256], BF16)
            kmask = T("kmask", [32, NBS, 128], BF16)
            ps = self.ps
            psbf = self.psbf
            self.dense = [0, 1, 2]
            self.di = 0

            def nd():
                i = self.dense[self.di % len(self.dense)]
                self.di += 1
                return ps[i], ('ps', i)
            self.ps_rot = [3, 4]
            self.ps_i = 0
            nsm = self.next_ps
            po = [ps[5], ps[6]]
            pok = [('ps', 5), ('ps', 6)]
            S, Sb = self.S, self.Sb

            kgkw = self.load_cast(gkw[:, :], ('gkw',), self.win_gk.rearrange("p k n -> p (k n)"), 128)
            kwgk = self.load_cast(wgk[:, :], ('wgk',), self.wgk_d[:, :], 512)
            fw.add('dve', lambda e: e.memset(gklr[:, :], 1.0), writes=KS('gklr', range(nab)))
            for bi, (st, sz) in enumerate(ablocks):
                p_, pk = nd()
                self.mm_group(p_[0:16, 0:sz], [(gkw[:, k * 16:(k + 1) * 16], xn[:, k, st:st + sz]) for k in range(KC)],
                              reads=kgkw + KS('xn', range(KC), bi), writes=[pk])
                fw.add('act', lambda e, p_=p_, st=st, sz=sz: e.activation(out=gklr[0:16, st:st + sz], in_=p_[0:16, 0:sz], func=AF.Copy),
                       reads=[pk], writes=[('gklr', bi)])

            load_head = self.load_head

            def proj_fm(hd, wk, u, bi, st, sz):
                s = hd % 2
                p_, pk = nd()
                self.mm_group(p_[:, 0:sz], [(self.mw[s][0][:, u, k * 128:(k + 1) * 128], xn[:, k, st:st + sz]) for k in range(KC)],
                              reads=wk[u] + KS('xn', range(KC), bi), writes=[pk])
                return p_, pk

            def softplus_gk(p_, pk, rows, sz):
                fw.add('act', lambda e: e.activation(out=spt[0:rows, 0:sz], in_=p_[0:rows, 0:sz], func=AF.Exp, scale=-1.0),
                       reads=[pk], writes=[('spt',)], tbl='el')
                fw.add('act', lambda e: e.activation(out=spt[0:rows, 0:sz], in_=spt[0:rows, 0:sz], func=AF.Ln, bias=1.0),
                       reads=[('spt',)], writes=[('spt',)], tbl='el')

            def u_proj(it, ue, nb, L):
                hd, wk, bi, st, sz = it['hd'], it['wk'], it['bi'], it['st'], it['sz']
                for cc in range(2):
                    p_, pk = proj_fm(hd, wk, cc, bi, st, sz)
                    fw.add('act', lambda e, p_=p_, cc=cc: e.activation(
                        out=ue[:, cc, :, 16:16 + L], in_=p_[:, 0:sz].rearrange("p (b l) -> p b l", b=nb), func=AF.Copy),
                        reads=[pk], writes=[('ue', cc)])

            def window_means(it, ue, A, B, nb, L, first):
                hd, sz = it['hd'], it['sz']
                w = POOL_W[hd]
                W = L + 16
                src, sk = ue, KS('ue', range(2))
                lvl = 1
                tog = 0
                while lvl < w:
                    dst, dk = (A, [('wA',)]) if tog == 0 else (B, [('wB',)])
                    lo = 2 * lvl - 1
                    fw.add('dve', lambda e, src=src, dst=dst, lo=lo, lvl=lvl: e.tensor_tensor(
                        out=dst[:, :, :, lo:W], in0=src[:, :, :, lo:W], in1=src[:, :, :, lo - lvl:W - lvl], op=ALU.add),
                        reads=sk, writes=dk, cost=0.1 + 2 * W * nb / 900.0)
                    src, sk = dst, dk
                    lvl *= 2
                    tog ^= 1
                if first:
                    for cc in range(2):
                        fw.add('dve', lambda e, src=src, cc=cc: e.tensor_tensor(
                            out=src[:, cc, 0, 16:32], in0=src[:, cc, 0, 16:32], in1=self.cfix[:, hd, :], op=ALU.mult),
                            reads=sk + [('const',)], writes=sk)
                pl = pooled[:, :, 0:sz].rearrange("p c (b l) -> p c b l", b=nb)
                fw.add('dve', lambda e, src=src: e.scalar_tensor_tensor(
                    out=pl, in0=src[:, :, :, 16:16 + L], scalar=1.0 / w, in1=ue[:, :, :, 16:16 + L],
                    op0=ALU.mult, op1=ALU.subtract),
                    reads=sk + KS('ue', range(2)), writes=[('pooled',)], cost=0.1 + 2 * sz / 900.0)

            def pool_map(it):
                hd, wk, sz = it['hd'], it['wk'], it['sz']
                s = hd % 2
                for dc in range(2):
                    p_, pk = nd()
                    self.mm_group(p_[:, 0:sz], [(self.mw[s][3][:, cc * 256 + dc * 128: cc * 256 + dc * 128 + 128], pooled[:, cc, 0:sz]) for cc in range(2)],
                                  reads=wk['p'] + [('pooled',)], writes=[pk])
                    col = V_PSC + 2 * hd + dc
                    fw.add('dve', lambda e, p_=p_, dc=dc, col=col: e.tensor_scalar_mul(
                        out=aout[:, dc, 0:sz], in0=p_[:, 0:sz], scalar1=vecs[:, col:col + 1]),
                        reads=[pk, ('const',)], writes=[('aout', dc)])

            def qk_proj(it):
                hd, wk, bi, st, sz = it['hd'], it['wk'], it['bi'], it['st'], it['sz']
                pq, pkq = proj_fm(hd, wk, 2, bi, st, sz)
                fw.add('dve', lambda e: e.scalar_tensor_tensor(out=qt[:, 0:sz], in0=pq[:, 0:sz], scalar=QSCALE, in1=Et[:, 0:sz],
                                                               op0=ALU.mult, op1=ALU.mult),
                       reads=[pkq, ('Et',)], writes=[('qt',)])
                pk_, pkk = proj_fm(hd, wk, 3, bi, st, sz)
                fw.add('dve', lambda e: e.tensor_tensor(out=kt[:, 0:sz], in0=pk_[:, 0:sz], in1=Ei[:, 0:sz], op=ALU.mult),
                       reads=[pkk, ('Ei',)], writes=[('kt',)])

            def gate_proj(it, u0, func, dst, dname):
                hd, wk, bi, st, sz = it['hd'], it['wk'], it['bi'], it['st'], it['sz']
                for cc in range(2):
                    p_, pk = proj_fm(hd, wk, u0 + cc, bi, st, sz)
                    if func == AF.Silu:
                        fw.add('act', lambda e, p_=p_, cc=cc: e.activation(out=dst[:, cc, 0:sz], in_=p_[:, 0:sz], func=AF.Silu),
                               reads=[pk], writes=[(dname, cc)], tbl='st')
                    else:
                        fw.add('act', lambda e, p_=p_, cc=cc: e.activation(out=dst[:, cc, 0:sz], in_=p_[:, 0:sz], func=AF.Tanh, scale=0.5),
                               reads=[pk], writes=[(dname, cc)], tbl='st')

            def siga_aout(it):
                sz = it['sz']
                fw.add('dve', lambda e: e.scalar_tensor_tensor(out=siga[:, :, 0:sz], in0=siga[:, :, 0:sz], scalar=1.0, in1=aout[:, :, 0:sz],
                                                               op0=ALU.add, op1=ALU.mult),
                       reads=KS('siga', range(2)) + KS('aout', range(2)), writes=KS('siga', range(2)), cost=0.1 + 2 * sz / 900.0)

            def front_prompt(it):
                hd, wk, bi, st, sz = it['hd'], it['wk'], it['bi'], it['st'], it['sz']
                s = hd % 2
                nt = sz // 128
                first = (sb == 0 and bi == 0)
                p_, pk = nsm()

                def gkmm(e, p_=p_):
                    ins = None
                    for tt in range(nt):
                        ins = e.matmul(p_[:, tt * 128:(tt + 1) * 128], lhsT=gklr[0:17, st + tt * 128: st + (tt + 1) * 128],
                                       rhs=wgk[0:17, hd * 128:(hd + 1) * 128], start=True, stop=True)
                    return ins
                fw.add('pe', gkmm, reads=[('gklr', bi)] + kwgk, writes=[pk], cost=0.07 * nt)
                softplus_gk(p_, pk, 128, sz)
                fw.add('act', lambda e: e.activation(out=uext[:, :, 0, 0:16], in_=self.ucarry[:, hd, :, :], func=AF.Copy),
                       reads=[('ucarry', hd)], writes=KS('ue', range(2)))
                u_proj(it, uext, 1, sz)
                fw.add('act', lambda e: e.activation(out=self.ucarry[:, hd, :, :], in_=uext[:, :, 0, sz:sz + 16], func=AF.Copy),
                       reads=KS('ue', range(2)), writes=[('ucarry', hd)])
                if sb == NSB - 1 and bi == npb - 1:
                    fw.add('sp', lambda e: e.dma_start(out=self.npp[:, 2 * hd:2 * hd + 2, :], in_=uext[:, :, 0, sz + 1:sz + 16]),
                           reads=KS('ue', range(2)), dma='npp')
                window_means(it, uext, wA, wB, 1, sz, first)
                p2, pk2 = nsm()

                def bmm(e, p2=p2):
                    ins = None
                    for tt in range(nt):
                        ins = e.matmul(p2[:, tt * 128:(tt + 1) * 128], lhsT=spt[:, tt * 128:(tt + 1) * 128],
                                       rhs=self.Umat[:, :], start=True, stop=True)
                    return ins
                fw.add('pe', bmm, reads=[('spt',), ('const',)], writes=[pk2], cost=0.35 * nt)
                fw.add('act', lambda e: e.activation(out=Et[:, 0:sz], in_=p2[:, 0:sz], func=AF.Exp), reads=[pk2], writes=[('Et',)], tbl='el')
                fw.add('act', lambda e: e.activation(out=Ei[:, 0:sz], in_=p2[:, 0:sz], func=AF.Exp, scale=-1.0), reads=[pk2], writes=[('Ei',)], tbl='el')
                for t2 in range(0, nt, 2):
                    pv, pkv = nd()
                    ntt = min(2, nt - t2)

                    def vmm(e, pv=pv, t2=t2, ntt=ntt):
                        ins = None
                        for q in range(ntt):
                            tt = t2 + q
                            for k in range(KC):
                                ins = e.matmul(pv[:, q * 256:(q + 1) * 256], lhsT=xn[:, k, st + tt * 128: st + (tt + 1) * 128],
                                               rhs=self.mw[s][1][:, k * 256:(k + 1) * 256], start=(k == 0), stop=(k == KC - 1))
                        return ins
                    fw.add('pe', vmm, reads=wk['v'] + KS('xn', range(KC), bi), writes=[pkv], cost=0.9 * ntt)
                    fw.add('act', lambda e, pv=pv, t2=t2, ntt=ntt: e.activation(
                        out=vTM[:, t2:t2 + ntt, :], in_=pv[:, 0:ntt * 256].rearrange("p (t c) -> p t c", t=ntt), func=AF.Copy),
                        reads=[pkv], writes=[('vTM', t2 // 2)])
                qk_proj(it)
                pool_map(it)

            def middle_prompt(it):
                hd, wk, bi, st, sz = it['hd'], it['wk'], it['bi'], it['st'], it['sz']
                nt = sz // 128
                vk = [('vTM', i) for i in range((nt + 1) // 2)]

                def ktr(e):
                    ins = None
                    for tt in range(nt):
                        ins = e.transpose(psbf[:, tt * 128:(tt + 1) * 128], kt[:, tt * 128:(tt + 1) * 128], self.ident[:, :])
                    return ins
                fw.add('pe', ktr, reads=[('kt',), ('const',)], writes=[('psbf',)], cost=0.1 * nt)
                fw.add('act', lambda e: e.activation(out=kTM[:, 0:nt, :], in_=psbf[:, 0:nt * 128].rearrange("p (t c) -> p t c", t=nt), func=AF.Copy),
                       reads=[('psbf',)], writes=[('kTM',)])
                pa, pka = nsm()

                def amm(e, pa=pa):
                    ins = None
                    for tt in range(nt):
                        ins = e.matmul(pa[:, tt * 128:(tt + 1) * 128], lhsT=kt[:, tt * 128:(tt + 1) * 128],
                                       rhs=qt[:, tt * 128:(tt + 1) * 128], start=True, stop=True)
                    return ins
                fw.add('pe', amm, reads=[('kt',), ('qt',)], writes=[pka], cost=0.07 * nt)
                fw.add('dve', lambda e: e.tensor_tensor(out=ATm[:, 0:nt, :], in0=pa[:, 0:nt * 128].rearrange("p (t i) -> p t i", t=nt),
                                                        in1=self.mask4[:, 0:nt, :], op=ALU.mult),
                       reads=[pka, ('const',)], writes=[('ATm',)])
                fillers = [lambda: gate_proj(it, 4, AF.Silu, sgt, 'sgt'),
                           lambda: gate_proj(it, 6, AF.Sigmoid, siga, 'siga'),
                           lambda: (gate_proj(it, 8, AF.Sigmoid, sigb, 'sigb'), siga_aout(it))]
                for tt in range(nt):
                    c0 = tt * 128

                    def omm(e, tt=tt, c0=c0):
                        ins = None
                        for vc in range(2):
                            e.matmul(po[vc][:, c0:c0 + 128], lhsT=Sb[:, hd, vc * 128:(vc + 1) * 128], rhs=qt[:, c0:c0 + 128],
                                     start=True, stop=False)
                            ins = e.matmul(po[vc][:, c0:c0 + 128], lhsT=vTM[:, tt, vc * 128:(vc + 1) * 128],
                                           rhs=ATm[:, tt, :], start=False, stop=True)
                        return ins
                    fw.add('pe', omm, reads=[('Sb', hd), ('qt',), ('ATm',)] + vk, writes=pok, cost=0.3)
                    pS, pkS = nsm()
                    fw.add('pe', lambda e, pS=pS, tt=tt: e.matmul(pS[:, 0:256], lhsT=kTM[:, tt, :], rhs=vTM[:, tt, :], start=True, stop=True),
                           reads=[('kTM',)] + vk, writes=[pkS])
                    ecol = Et[:, c0 + 127:c0 + 128]
                    fw.add('dve', lambda e, ecol=ecol: e.tensor_scalar_mul(out=S[:, hd, :], in0=S[:, hd, :], scalar1=ecol),
                           reads=[('S', hd), ('Et',)], writes=[('S', hd)])
                    fw.add('dve', lambda e, ecol=ecol, pS=pS: e.scalar_tensor_tensor(
                        out=S[:, hd, :], in0=pS[:, 0:256], scalar=ecol, in1=S[:, hd, :], op0=ALU.mult, op1=ALU.add),
                        reads=[pkS, ('S', hd), ('Et',)], writes=[('S', hd)])
                    fw.add('act', lambda e: e.activation(out=Sb[:, hd, :], in_=S[:, hd, :], func=AF.Copy),
                           reads=[('S', hd)], writes=[('Sb', hd)])
                    if fillers:
                        fillers.pop(0)()
                while fillers:
                    fillers.pop(0)()
                if sb == NSB - 1 and bi == npb - 1:
                    fw.add('sp', lambda e: e.dma_start(out=self.ngp[hd, :, :], in_=S[:, hd, :]), reads=[('S', hd)], dma='ngp')

            def front_sample(it):
                hd, wk, bi, st, sz = it['hd'], it['wk'], it['bi'], it['st'], it['sz']
                s = hd % 2
                fw.add('sp', lambda e: e.dma_start(out=ustg[:, :, :, :], in_=self.spT[sb, :, hd, :, :, :]), writes=[('ustg',)], dma='ustg')
                fw.add('act', lambda e: e.activation(out=uext_s[:, :, :, 0:16], in_=ustg[:, :, :, :], func=AF.Copy),
                       reads=[('ustg',)], writes=KS('ue', range(2)))
                fw.add('sp', lambda e: e.dma_start(out=S0[:, :, :], in_=self.sgla[sb, hd, :, :, :]), writes=KS('S0', range(NBS)), dma='S0')
                fw.add('act', lambda e: e.activation(out=S0b[:, :, :], in_=S0[:, :, :], func=AF.Copy), reads=KS('S0', range(NBS)), writes=[('S0b',)])
                p_, pk = nsm()
                fw.add('pe', lambda e: e.matmul(p_[0:sz, 0:128], lhsT=gklr[0:17, st:st + sz], rhs=wgk[0:17, hd * 128:(hd + 1) * 128],
                                                start=True, stop=True),
                       reads=[('gklr', bi)] + kwgk, writes=[pk])
                softplus_gk(p_, pk, sz, 128)
                u_proj(it, uext_s, NBS, LS)
                fw.add('sp', lambda e: e.dma_start(out=self.nps[sb, :, hd, :, :, :], in_=uext_s[:, :, :, LS + 1:LS + 16]),
                       reads=KS('ue', range(2)), dma='nps')
                window_means(it, uext_s, wA_s, wB_s, NBS, LS, False)
                p2, pk2 = nsm()
                fw.add('pe', lambda e: e.matmul(p2[:, 0:sz], lhsT=spt[0:sz, 0:128], rhs=self.Umat_s[0:sz, 0:sz], start=True, stop=True),
                       reads=[('spt',), ('const',)], writes=[pk2])
                fw.add('act', lambda e: e.activation(out=Et[:, 0:sz], in_=p2[:, 0:sz], func=AF.Exp), reads=[pk2], writes=[('Et',)], tbl='el')
                fw.add('act', lambda e: e.activation(out=Ei[:, 0:sz], in_=p2[:, 0:sz], func=AF.Exp, scale=-1.0), reads=[pk2], writes=[('Ei',)], tbl='el')
                pv, pkv = nd()
                self.mm_group(pv[0:sz, 0:256], [(xn[:, k, st:st + sz], self.mw[s][1][:, k * 256:(k + 1) * 256]) for k in range(KC)],
                              reads=wk['v'] + KS('xn', range(KC), bi), writes=[pkv])
                fw.add('act', lambda e: e.activation(out=vTM[0:sz, 0, :], in_=pv[0:sz, 0:256], func=AF.Copy),
                       reads=[pkv], writes=[('vTM', 0)])
                qk_proj(it)
                pool_map(it)

            def middle_sample(it):
                hd, wk, bi, st, sz = it['hd'], it['wk'], it['bi'], it['st'], it['sz']
                fw.add('pe', lambda e: e.transpose(psbf[0:sz, 0:128], kt[:, 0:sz], self.ident[:, :]),
                       reads=[('kt',), ('const',)], writes=[('psbf',)])
                fw.add('act', lambda e: e.activation(out=kTM[0:sz, 0, :], in_=psbf[0:sz, 0:128], func=AF.Copy),
                       reads=[('psbf',)], writes=[('kTM',)])
                for b in range(NBS):
                    fw.add('dve', lambda e, b=b: e.tensor_scalar_mul(out=kmask[0:sz, b, :], in0=kTM[0:sz, 0, :],
                                                                     scalar1=self.onehot[0:sz, b:b + 1]),
                           reads=[('kTM',), ('const',)], writes=[('kmask', b)])
                pa, pka = nsm()
                fw.add('pe', lambda e: e.matmul(pa[0:sz, 0:sz], lhsT=kt[:, 0:sz], rhs=qt[:, 0:sz], start=True, stop=True),
                       reads=[('kt',), ('qt',)], writes=[pka])
                fw.add('dve', lambda e: e.tensor_tensor(out=ATm[0:sz, 0, 0:sz], in0=pa[0:sz, 0:sz], in1=self.mask_s[0:sz, 0:sz], op=ALU.mult),
                       reads=[pka, ('const',)], writes=[('ATm',)])

                def omm(e):
                    ins = None
                    for vc in range(2):
                        for b in range(NBS):
                            e.matmul(po[vc][:, b * LS:(b + 1) * LS], lhsT=vTM[0:sz, 0, vc * 128:(vc + 1) * 128],
                                     rhs=ATm[0:sz, 0, b * LS:(b + 1) * LS], start=True, stop=False)
                            ins = e.matmul(po[vc][:, b * LS:(b + 1) * LS], lhsT=S0b[:, b, vc * 128:(vc + 1) * 128],
                                           rhs=qt[:, b * LS:(b + 1) * LS], start=False, stop=True)
                    return ins
                fw.add('pe', omm, reads=[('S0b',), ('qt',), ('ATm',), ('vTM', 0)], writes=pok, cost=1.2)
                gate_proj(it, 4, AF.Silu, sgt, 'sgt')
                for b in range(NBS):
                    pS, pkS = nsm()
                    fw.add('pe', lambda e, pS=pS, b=b: e.matmul(pS[:, 0:256], lhsT=kmask[0:sz, b, :], rhs=vTM[0:sz, 0, :], start=True, stop=True),
                           reads=[('kmask', b), ('vTM', 0)], writes=[pkS])
                    ecol = Et[:, b * LS + LS - 1:b * LS + LS]
                    fw.add('dve', lambda e, ecol=ecol, b=b: e.tensor_scalar_mul(out=S0[:, b, :], in0=S0[:, b, :], scalar1=ecol),
                           reads=[('S0', b), ('Et',)], writes=[('S0', b)])
                    fw.add('dve', lambda e, ecol=ecol, pS=pS, b=b: e.scalar_tensor_tensor(
                        out=S0[:, b, :], in0=pS[:, 0:256], scalar=ecol, in1=S0[:, b, :], op0=ALU.mult, op1=ALU.add),
                        reads=[pkS, ('S0', b), ('Et',)], writes=[('S0', b)])
                fw.add('sp', lambda e: e.dma_start(out=self.ngs[sb, hd, :, :, :], in_=S0[:, :, :]),
                       reads=KS('S0', range(NBS)), dma='ngs')
                gate_proj(it, 6, AF.Sigmoid, siga, 'siga')
                gate_proj(it, 8, AF.Sigmoid, sigb, 'sigb')
                siga_aout(it)

            def tail(it):
                hd, wk, bi, st, sz = it['hd'], it['wk'], it['bi'], it['st'], it['sz']
                s = hd % 2
                fw.add('act', lambda e: e.activation(out=osb[:, 0, 0:sz], in_=po[0][:, 0:sz], func=AF.Copy),
                       reads=[pok[0]], writes=[('osb', 0)])
                fw.add('act', lambda e: e.activation(out=osb[:, 1, 0:sz], in_=po[1][:, 0:sz], func=AF.Copy),
                       reads=[pok[1]], writes=[('osb', 1)])
                self.rms_stats(osb[:, :, 0:sz], sz, self.ones_v, 2, rso, KS('osb', range(2)), 'o')
                for vc in range(2):
                    fw.add('dve', lambda e, vc=vc: e.scalar_tensor_tensor(
                        out=osb[:, vc, 0:sz], in0=osb[:, vc, 0:sz], scalar=vecs[:, V_GLA + vc:V_GLA + vc + 1],
                        in1=rso[:, 0:sz], op0=ALU.mult, op1=ALU.mult),
                        reads=[('osb', vc), ('rstd', 'o'), ('const',)], writes=[('osb', vc)])
                fw.add('dve', lambda e: e.tensor_tensor(out=osb[:, :, 0:sz], in0=osb[:, :, 0:sz], in1=sgt[:, :, 0:sz], op=ALU.mult),
                       reads=KS('osb', range(2)) + KS('sgt', range(2)), writes=KS('osb', range(2)), cost=0.1 + 2 * sz / 900.0)
                fw.add('dve', lambda e: e.scalar_tensor_tensor(out=osb[:, :, 0:sz], in0=sigb[:, :, 0:sz], scalar=1.0, in1=osb[:, :, 0:sz],
                                                               op0=ALU.add, op1=ALU.mult),
                       reads=KS('osb', range(2)) + KS('sigb', range(2)), writes=KS('osb', range(2)), cost=0.1 + 2 * sz / 900.0)
                fw.add('dve', lambda e: e.tensor_tensor(out=mix[:, :, 0:sz], in0=osb[:, :, 0:sz], in1=siga[:, :, 0:sz], op=ALU.add),
                       reads=KS('osb', range(2)) + KS('siga', range(2)), writes=[('mix',)], cost=0.1 + 2 * sz / 900.0)
                for c in range(KC):
                    p_, pk = nd()
                    self.mm_group(p_[:, 0:sz], [(self.mw[s][2][:, kc * 1024 + c * 128: kc * 1024 + c * 128 + 128], mix[:, kc, 0:sz]) for kc in range(2)],
                                  reads=wk['o'] + [('mix',)], writes=[pk])
                    fw.add('dve', lambda e, p_=p_, c=c: e.scalar_tensor_tensor(
                        out=h[:, c, st:st + sz], in0=p_[:, 0:sz], scalar=0.5, in1=h[:, c, st:st + sz], op0=ALU.mult, op1=ALU.add),
                        reads=[pk, ('h', c, bi)], writes=[('h', c, bi)])

            def pre_keys(s_):
                d = {u: [('wfm', s_, u)] for u in range(10)}
                d.update({'v': [('wv', s_)], 'o': [('wo', s_)], 'p': [('pw', s_)]})
                return d
            wks = {}
            for hd0 in (0, 1):
                if hd0 in pre:
                    wks[hd0] = pre_keys(hd0)
                else:
                    if self.mw[hd0 % 2] is None:
                        self.alloc_mw(es, hd0 % 2)
                    wks[hd0] = load_head(hd0)
            items = []
            for hd in range(4):
                for bi, (st, sz) in enumerate(pblocks):
                    items.append(dict(hd=hd, bi=bi, st=st, sz=sz, kind='p', last=False))
                items.append(dict(hd=hd, bi=npb, st=TP, sz=TSM, kind='s', last=True))

            def front(it):
                it['wk'] = wks[it['hd']]
                (front_prompt if it['kind'] == 'p' else front_sample)(it)

            def middle(it):
                (middle_prompt if it['kind'] == 'p' else middle_sample)(it)
            def th_front(itn):
                def f():
                    self.dense, self.ps_rot = [0, 1], [3]
                    front(itn)
                return f

            def th_tail(itc):
                def f():
                    self.dense, self.ps_rot = [2], [4]
                    tail(itc)
                return f
            front(items[0])
            for i, it in enumerate(items):
                self.dense, self.ps_rot = [0, 1, 2], [3, 4]
                middle(it)
                if i + 1 < len(items):
                    fw.merged([th_front(items[i + 1]), th_tail(it)])
                else:
                    self.dense, self.ps_rot = [0, 1, 2], [3, 4]
                    tail(it)
                if it['last'] and it['hd'] + 2 < 4:
                    wks[it['hd'] + 2] = load_head(it['hd'] + 2)
            fw.flush()

    def alloc_mw(self, es, s):
        self.mw[s] = (self.T(es, "wfm%d" % s, [128, 10, 1024], BF16), self.T(es, "wv%d" % s, [128, 2048], BF16),
                      self.T(es, "wo%d" % s, [128, 2048], BF16), self.T(es, "pw%d" % s, [128, 512], BF16))

    def load_head(self, hd):
        s = hd % 2
        wfm, wv, wo, pw = self.mw[s]
        keys = {}
        for u in range(10):
            keys[u] = self.load_cast(wfm[:, u, :], ('wfm', s, u), self.win_fm[hd, u].rearrange("p k n -> p (k n)"), 1024)
        keys['v'] = self.load_cast(wv[:, :], ('wv', s), self.win_v[hd].rearrange("p k n -> p (k n)"), 2048)
        keys['o'] = self.load_cast(wo[:, :], ('wo', s), self.wout_d[hd].rearrange("p k n -> p (k n)"), 2048)
        keys['p'] = self.load_cast(pw[:, :], ('pw', s), self.poolw_d[hd].rearrange("p k n -> p (k n)"), 512)
        return keys

    def build(self):
        nc = self.nc
        self.xT = self.din("xT", [NSB, 128, KC, TS])
        self.pT = self.din("pT", [NSB, 128, 2, TS])
        self.spT = self.din("spT", [NSB, 128, 4, 2, NBS, 16])
        self.sgla = self.din("sgla", [NSB, 4, 128, NBS, 256])
        self.w1g = self.din("w1g", [FC, 128, KC, 128])
        self.w1u = self.din("w1u", [FC, 128, KC, 128])
        self.w1d = self.din("w1d", [KC, 128, FC, 128])
        self.w2g = self.din("w2g", [FC, 128, KC, 128])
        self.w2u = self.din("w2u", [FC, 128, KC, 128])
        self.w2d = self.din("w2d", [KC, 128, FC, 128])
        self.win_fm = self.din("win_fm", [4, 10, 128, KC, 128])
        self.win_v = self.din("win_v", [4, 128, KC, 256])
        self.win_gk = self.din("win_gk", [128, KC, 16])
        self.wout_d = self.din("wout", [4, 128, 2, 1024])
        self.poolw_d = self.din("poolw", [4, 128, 2, 256])
        self.wgk_d = self.din("wgk", [32, 512])
        self.wpg_d = self.din("wpg", [KC, 128, KC, 128])
        self.wpp_d = self.din("wpp", [KC, 128, 2, 128])
        self.vecs_d = self.din("vecs", [128, NVEC])
        self.ident_d = self.din("ident", [128, 128], BF16)
        self.ones_d_d = self.din("ones_d", [128, 128], BF16)
        self.ones_v_d = self.din("ones_v", [128, 128], BF16)
        self.Umat_d = self.din("Umat", [128, 128])
        self.Umat_s_d = self.din("Umat_s", [32, 32])
        self.mask4_d = self.din("mask4", [128, 4, 128])
        self.mask_s_d = self.din("mask_s", [32, 32])
        self.onehot_d = self.din("onehot", [32, NBS])
        self.cfix_d = self.din("cfix", [128, 4, 16])
        self.yT = self.dout("yT", [NSB, 128, KC, TS])
        self.npp = self.dout("npp", [128, KC, 15])
        self.ngp = self.dout("ngp", [4, 128, 256])
        self.nps = self.dout("nps", [NSB, 128, 4, 2, NBS, 15])
        self.ngs = self.dout("ngs", [NSB, 4, 128, NBS, 256])
        with ExitStack() as es:
            self.fw = fw = FW(nc, es)
            T = lambda name, shape, dt: self.T(es, name, shape, dt)
            self.h = T("h", [128, KC, TS], F32)
            self.xn = T("xn", [128, KC, TS], BF16)
            self.S = T("S", [128, 4, 256], F32)
            self.Sb = T("Sb", [128, 4, 256], BF16)
            self.ucarry = T("ucarry", [128, 4, 2, 16], F32)
            self.vecs = T("vecs", [128, NVEC], F32)
            self.ident = T("ident", [128, 128], BF16)
            self.ones_d = T("ones_d", [128, 128], BF16)
            self.ones_v = T("ones_v", [128, 128], BF16)
            self.Umat = T("Umat", [128, 128], F32)
            self.Umat_s = T("Umat_s", [32, 32], F32)
            self.mask4 = T("mask4", [128, 4, 128], F32)
            self.mask_s = T("mask_s", [32, 32], F32)
            self.onehot = T("onehot", [32, NBS], F32)
            self.cfix = T("cfix", [128, 4, 16], F32)
            self.ps = [es.enter_context(nc.psum_tensor("ps%d" % i, [128, 512], F32)) for i in range(7)]
            self.psbf = es.enter_context(nc.psum_tensor("psbf", [128, 1024], BF16))
            self.ps_rot = list(range(7))
            self.ps_i = 0
            for i, (t, d) in enumerate([(self.vecs, self.vecs_d), (self.ident, self.ident_d), (self.ones_d, self.ones_d_d),
                                        (self.ones_v, self.ones_v_d), (self.Umat, self.Umat_d), (self.Umat_s, self.Umat_s_d),
                                        (self.mask4, self.mask4_d), (self.mask_s, self.mask_s_d), (self.onehot, self.onehot_d),
                                        (self.cfix, self.cfix_d)]):
                fw.add('sp', lambda e, t=t, d=d: e.dma_start(out=t[:], in_=d), writes=[('const', i)], dma='const')
            fw.add('dve', lambda e: e.memset(self.S[:, :, :], 0.0), writes=KS('S', range(4)))
            fw.add('dve', lambda e: e.memset(self.Sb[:, :, :], 0.0), writes=KS('Sb', range(4)))
            fw.add('dve', lambda e: e.memset(self.ucarry[:, :, :, :], 0.0), writes=KS('ucarry', range(4)))
            fw.flush()
            self.mw = [None, None]
            with ExitStack() as esw:
                self.alloc_mw(esw, 0)
                self.alloc_mw(esw, 1)
                self.phase_A(0, prefetch=lambda: (self.load_head(0), self.load_head(1)))
                self.phase_mixer(0, pre=(0, 1))
            self.mw = [None, None]
            with ExitStack() as esw:
                self.alloc_mw(esw, 0)
                self.phase_C(0, next_A=(1, lambda: self.load_head(0)))
                self.phase_mixer(1, pre=(0,))
            self.mw = [None, None]
            self.phase_C(1)


def _fm_units(W, c0, width=128):
    return np.ascontiguousarray(W[:, c0:c0 + width].reshape(KC, 128, width).transpose(1, 0, 2))


def _consts():
    bf = ml_dtypes.bfloat16
    j = np.arange(128)[:, None]
    i = np.arange(128)[None, :]
    causal = (j <= i)
    Umat = np.where(causal, -1.0 / 16.0, 0.0).astype(np.float32)
    mask4 = np.repeat(causal.astype(np.float32)[:, None, :], 4, axis=1)
    js = np.arange(32)[:, None]
    is_ = np.arange(32)[None, :]
    cs = (js <= is_) & ((js // LS) == (is_ // LS))
    Umat_s = np.where(cs, -1.0 / 16.0, 0.0).astype(np.float32)
    mask_s = cs.astype(np.float32)
    onehot = ((np.arange(32)[:, None] // LS) == np.arange(NBS)[None, :]).astype(np.float32)
    cfix = np.zeros((128, 4, 16), np.float32)
    for g, w in enumerate(POOL_W):
        t = np.arange(16)
        cfix[:, g, :] = (w / np.minimum(w, t + 1))[None, :]
    return dict(
        ident=np.eye(128, dtype=np.float32).astype(bf),
        ones_d=np.full((128, 128), 1.0 / D, np.float32).astype(bf),
        ones_v=np.full((128, 128), 1.0 / 256, np.float32).astype(bf),
        Umat=Umat, Umat_s=Umat_s, mask4=np.ascontiguousarray(mask4), mask_s=mask_s, onehot=onehot, cfix=cfix)


def _prep_shared(inp):
    f = lambda a: np.asarray(a, dtype=np.float32)
    out = {}
    for nm, key in (("w1", "ffn1"), ("w2", "ffn2")):
        Wg, Wu, Wd = f(inp[key + "_w_gate"])[0], f(inp[key + "_w_up"])[0], f(inp[key + "_w_down"])[0]
        out[nm + "g"] = np.ascontiguousarray(Wg.reshape(KC, 128, FC, 128).transpose(2, 1, 0, 3))
        out[nm + "u"] = np.ascontiguousarray(Wu.reshape(KC, 128, FC, 128).transpose(2, 1, 0, 3))
        out[nm + "d"] = np.ascontiguousarray(Wd.reshape(FC, 128, KC, 128).transpose(2, 1, 0, 3))
    Win = f(inp["w_in"])[0]
    OU, OQ, OK_, OV, OGK, OG, OGA, OGB = 0, 1024, 1536, 2048, 3072, 3088, 4112, 5136
    fm = np.zeros((4, 10, 128, KC, 128), np.float32)
    wv = np.zeros((4, 128, KC, 256), np.float32)
    for hd in range(4):
        cols = [OU + hd * 256, OU + hd * 256 + 128, OQ + hd * 128, OK_ + hd * 128,
                OG + hd * 256, OG + hd * 256 + 128, OGA + hd * 256, OGA + hd * 256 + 128,
                OGB + hd * 256, OGB + hd * 256 + 128]
        for u, c0 in enumerate(cols):
            fm[hd, u] = _fm_units(Win, c0)
        wv[hd] = _fm_units(Win, OV + hd * 256, 256)
    out["win_fm"] = fm
    out["win_v"] = wv
    out["win_gk"] = _fm_units(Win, OGK, 16)
    out["wout"] = np.ascontiguousarray(f(inp["w_out"])[0].reshape(4, 2, 128, 1024).transpose(0, 2, 1, 3))
    out["poolw"] = np.ascontiguousarray(f(inp["pool_w"])[0].reshape(4, 2, 128, 256).transpose(0, 2, 1, 3))
    wgk = np.zeros((32, 512), np.float32)
    wgk[0:16] = f(inp["w_gk_up"])[0]
    wgk[16] = f(inp["b_gk"])[0]
    out["wgk"] = wgk
    out["wpg"] = np.ascontiguousarray(f(inp["w_ple_gate"])[0].reshape(KC, 128, KC, 128).transpose(2, 1, 0, 3))
    out["wpp"] = np.ascontiguousarray(f(inp["w_ple_proj"])[0].reshape(2, 128, KC, 128).transpose(2, 1, 0, 3))
    vecs = np.zeros((128, NVEC), np.float32)
    for col, v in ((V_FFN1, inp["ffn1_norm"][0]), (V_MIX, inp["mix_norm"][0]), (V_FFN2, inp["ffn2_norm"][0]),
                   (V_PLE, inp["ple_norm"][0]), (V_FIN, inp["final_norm"]), (V_PSC, inp["pool_scale"][0])):
        vecs[:, col:col + 8] = f(v).reshape(KC, 128).T
    vecs[:, V_GLA:V_GLA + 2] = f(inp["gla_norm"][0]).reshape(2, 128).T
    out["vecs"] = vecs
    out.update(_consts())
    return out


def _prep_core(inp, c):
    f = lambda a: np.asarray(a, dtype=np.float32)
    xp, xs = f(inp["x_prompt"]), f(inp["x_sample"])
    pp, psm = f(inp["p_prompt"])[0], f(inp["p_sample"])[0]
    sp, sg = f(inp["state_pool"])[0], f(inp["state_gla"])[0]
    xT = np.zeros((NSB, 128, KC, TS), np.float32)
    pT = np.zeros((NSB, 128, 2, TS), np.float32)
    spT = np.zeros((NSB, 128, 4, 2, NBS, 16), np.float32)
    sgla = np.zeros((NSB, 4, 128, NBS, 256), np.float32)
    for sb in range(NSB):
        b0 = c * 16 + sb * NBS
        tok = np.concatenate([xp[c, sb * TP:(sb + 1) * TP], xs[b0:b0 + NBS].reshape(TSM, D)], axis=0)
        xT[sb] = tok.T.reshape(KC, 128, TS).transpose(1, 0, 2)
        ptok = np.concatenate([pp[c, sb * TP:(sb + 1) * TP], psm[b0:b0 + NBS].reshape(TSM, PLE)], axis=0)
        pT[sb] = ptok.T.reshape(2, 128, TS).transpose(1, 0, 2)
        st = sp[b0:b0 + NBS]
        st = st.transpose(2, 0, 1).reshape(4, 2, 128, NBS, 15)
        spT[sb, :, :, :, :, 1:16] = st.transpose(2, 0, 1, 3, 4)
        sgla[sb] = sg[b0:b0 + NBS].transpose(1, 2, 0, 3)
    return dict(xT=xT, pT=pT, spT=spT, sgla=sgla)


_PROG = {}


def _get_prog(stop_after=None):
    if stop_after not in _PROG:
        _PROG[stop_after] = Prog(stop_after)
    return _PROG[stop_after]


def kernel(**inputs):
    stop_after = inputs.pop("_stop_after", None)
    prog = _get_prog(stop_after)
    shared = _prep_shared(inputs)
    in_maps = []
    for c in range(NCORES):
        m = dict(shared)
        m.update(_prep_core(inputs, c))
        in_maps.append(m)
    res = run_bass_kernel_spmd(prog.nc, in_maps, core_ids=list(range(NCORES)))
    R = res.results
    B, SEQ = 8, 2048
    y_prompt = np.zeros((B, SEQ, D), np.float32)
    y_sample = np.zeros((128, LS, D), np.float32)
    npool_p = np.zeros((1, B, 15, D), np.float32)
    ngla_p = np.zeros((1, B, 4, 128, 256), np.float32)
    npool_s = np.zeros((1, 128, 15, D), np.float32)
    ngla_s = np.zeros((1, 128, 4, 128, 256), np.float32)
    for c in range(NCORES):
        r = R[c]
        yT = np.asarray(r["yT"], dtype=np.float32)
        for sb in range(NSB):
            tok = yT[sb].transpose(1, 0, 2).reshape(D, TS).T
            y_prompt[c, sb * TP:(sb + 1) * TP] = tok[:TP]
            b0 = c * 16 + sb * NBS
            y_sample[b0:b0 + NBS] = tok[TP:].reshape(NBS, LS, D)
            nps = np.asarray(r["nps"], dtype=np.float32)[sb]
            npool_s[0, b0:b0 + NBS] = nps.transpose(3, 4, 1, 2, 0).reshape(NBS, 15, D)
            ngs = np.asarray(r["ngs"], dtype=np.float32)[sb]
            ngla_s[0, b0:b0 + NBS] = ngs.transpose(2, 0, 1, 3)
        npp = np.asarray(r["npp"], dtype=np.float32)
        npool_p[0, c] = npp.transpose(2, 1, 0).reshape(15, D)
        ngla_p[0, c] = np.asarray(r["ngp"], dtype=np.float32)
    return (y_prompt, y_sample, npool_p, ngla_p, npool_s, ngla_s)
```

```python
import numpy as np
import ml_dtypes
from contextlib import ExitStack
import concourse.bass as bass
import concourse.mybir as mybir
from concourse.bass_utils import run_bass_kernel_spmd

F32 = mybir.dt.float32
BF16 = mybir.dt.bfloat16
AF = mybir.ActivationFunctionType
ALU = mybir.AluOpType

NCORES = 8
D = 1024
KC = 8
FF = 2816
FC = 22
PLE = 256
NSB = 2
TP = 1024
NBS = 8
LS = 4
TSM = NBS * LS
TS = TP + TSM
EPS = 1e-6
POOL_W = (2, 4, 8, 16)
QSCALE = 128 ** -0.5
MB = 512
NSTG = 3
BLK_D = [(0, 352), (352, 352), (704, 352)]

V_FFN1, V_MIX, V_FFN2, V_PLE, V_FIN, V_PSC, V_GLA = 0, 8, 16, 24, 32, 40, 48
NVEC = 50


class _Probe:
    def __init__(self):
        self.n = 0

    def __getattr__(self, name):
        def m(*a, **k):
            out = k.get('out', a[0] if a else None)
            try:
                fs = 1
                for d in out.shape[1:]:
                    fs *= d
                self.n = max(self.n, fs)
            except Exception:
                pass
            return self
        return m


class FW:
    ENG = ('pe', 'act', 'dve', 'pool', 'sp')

    def __init__(self, nc, es):
        self.nc = nc
        self.es = es
        self.esem = {e: es.enter_context(nc.semaphore("s_" + e)) for e in self.ENG if e != 'sp'}
        self.ecount = {e: 0 for e in self.esem}
        self.dsem = {}
        self.dcount = {}
        self.ops = []
        self.last_writer = {}
        self.readers = {}
        self.seen = {e: {} for e in self.ENG}
        self.same_engine_sync = True
        self.capture = None
        self.do_schedule = True
        self.sched_window = 66

    def merged(self, builders):
        lists = []
        for b in builders:
            self.capture = []
            b()
            lists.append(self.capture)
        self.capture = None
        n = [len(l) for l in lists]
        idx = [0] * len(lists)
        while True:
            cand = [i for i in range(len(lists)) if idx[i] < n[i]]
            if not cand:
                break
            j = min(cand, key=lambda i: (idx[i] + 1) / n[i])
            self.add(*lists[j][idx[j]])
            idx[j] += 1

    DEFCOST = {'pe': 0.25, 'act': 0.6, 'dve': 0.7, 'pool': 0.2, 'sp': 0.1}

    def add(self, engine, fn, reads=(), writes=(), dma=None, cost=None, tbl=None):
        if self.capture is not None:
            self.capture.append((engine, fn, list(reads), list(writes), dma, cost, tbl))
            return None
        if cost is None and dma is None and engine in ('act', 'dve', 'pool'):
            pr = _Probe()
            try:
                fn(pr)
            except Exception:
                pr.n = 0
            if pr.n > 0:
                if engine == 'act':
                    cost = 0.22 + pr.n / 1500.0
                elif engine == 'dve':
                    cost = 0.08 + pr.n / 850.0
                else:
                    cost = 0.3 + pr.n / 350.0
        op = dict(engine=engine, fn=fn, dma=dma, deps=[], alldeps=[], needs_inc=False, val=None, tbl=tbl,
                  cost=(cost if cost is not None else self.DEFCOST[engine]))
        deps = []
        for k in reads:
            w = self.last_writer.get(k)
            if w is not None:
                deps.append(w)
        for k in writes:
            w = self.last_writer.get(k)
            if w is not None:
                deps.append(w)
            deps.extend(self.readers.get(k, ()))
        for d in deps:
            if d is op:
                continue
            op['alldeps'].append(d)
            if d['dma'] is None:
                if d['engine'] == engine and (engine == 'pe' or not self.same_engine_sync):
                    continue
                d['needs_inc'] = True
            op['deps'].append(d)
        for k in writes:
            self.last_writer[k] = op
            self.readers[k] = []
        for k in reads:
            self.readers.setdefault(k, []).append(op)
        if dma is not None and dma not in self.dsem:
            self.dsem[dma] = self.es.enter_context(self.nc.semaphore("d_" + dma))
            self.dcount[dma] = 0
        self.ops.append(op)
        return op

    def schedule(self, ops):
        import bisect
        n = len(ops)
        idx = {id(op): i for i, op in enumerate(ops)}
        dep_idx = []
        succ = [[] for _ in range(n)]
        indeg = [0] * n
        for i, op in enumerate(ops):
            ds = sorted(set(idx[id(d)] for d in op['alldeps'] if id(d) in idx))
            dep_idx.append(ds)
            indeg[i] = len(ds)
            for d in ds:
                succ[d].append(i)
        fin = [0.0] * n
        efree = {e: 0.0 for e in self.ENG}
        ready = [i for i in range(n) if indeg[i] == 0]
        order = []
        LAT = 0.3
        cur_tbl = None
        while ready:
            best = None
            lim = ready[0] + self.sched_window
            for i in ready:
                if i > lim:
                    break
                op = ops[i]
                e = op['engine']
                t = efree[e]
                for d in dep_idx[i]:
                    td = fin[d] + (LAT if ops[d]['engine'] != e else 0.05)
                    if td > t:
                        t = td
                if op['tbl'] is not None and op['tbl'] != cur_tbl:
                    t += 0.5
                tk = t - (0.2 if e == 'pe' else 0.0)
                if best is None or tk < best[2]:
                    best = (t, i, tk)
            t, i = best[0], best[1]
            ready.remove(i)
            op = ops[i]
            e = op['engine']
            if op['tbl'] is not None:
                cur_tbl = op['tbl']
            if op['dma'] is not None:
                efree[e] = t + 0.15
                fin[i] = t + 2.0 + op['cost']
            else:
                efree[e] = t + op['cost']
                fin[i] = t + op['cost']
            order.append(op)
            for sidx in succ[i]:
                indeg[sidx] -= 1
                if indeg[sidx] == 0:
                    bisect.insort(ready, sidx)
        assert len(order) == n
        self.sim_time = max(fin) if fin else 0.0
        return order

    def flush(self):
        ops = self.ops
        self.ops = []
        if self.do_schedule:
            ops = self.schedule(ops)
        last = {}
        for op in ops:
            if op['dma'] is None:
                last[op['engine']] = op
        for op in last.values():
            op['needs_inc'] = True
        for op in ops:
            if op['dma'] is not None:
                self.dcount[op['dma']] += 16
                op['val'] = self.dcount[op['dma']]
            elif op['needs_inc']:
                self.ecount[op['engine']] += 1
                op['val'] = self.ecount[op['engine']]
        nwait = 0
        for op in ops:
            e = op['engine']
            seen = self.seen[e]
            need = {}
            for d in op['deps']:
                key = ('d', d['dma']) if d['dma'] is not None else ('e', d['engine'])
                if need.get(key, (0, None))[0] < d['val']:
                    need[key] = (d['val'], d)
            waits = []
            for key, (v, d) in sorted(need.items(), key=lambda kv: -kv[1][0]):
                if seen.get(key, 0) >= v:
                    continue
                waits.append((key, v))
                seen[key] = v
                for k2, v2 in d.get('vc', {}).items():
                    if seen.get(k2, 0) < v2:
                        seen[k2] = v2
            op['waits'] = waits
            nwait += len(waits)
            vc = dict(seen)
            if op['dma'] is not None:
                vc[('d', op['dma'])] = max(vc.get(('d', op['dma']), 0), op['val'])
            elif op['val'] is not None:
                vc[('e', e)] = max(vc.get(('e', e), 0), op['val'])
            op['vc'] = vc
        self.n_waits = getattr(self, 'n_waits', 0) + nwait
        per = {e: [] for e in self.ENG}
        for op in ops:
            per[op['engine']].append(op)
        fw = self

        def emit(e, eng):
            seen = fw.seen[e]
            for op in per[e]:
                for key, v in op['waits']:
                    sem = fw.dsem[key[1]] if key[0] == 'd' else fw.esem[key[1]]
                    eng.wait_ge(sem, v)
                ins = op['fn'](eng)
                if op['dma'] is not None:
                    ins.then_inc(fw.dsem[op['dma']], 16)
                elif op['needs_inc']:
                    ins.then_inc(fw.esem[e], 1)
            for f in fw.esem:
                if f == e:
                    continue
                v = fw.ecount[f]
                if seen.get(('e', f), 0) < v:
                    seen[('e', f)] = v
                    eng.wait_ge(fw.esem[f], v)
            for dn, v in fw.dcount.items():
                if seen.get(('d', dn), 0) < v:
                    seen[('d', dn)] = v
                    eng.wait_ge(fw.dsem[dn], v)

        with self.nc.Block() as block:
            @block.tensor
            def _(eng):
                emit('pe', eng)

            @block.scalar
            def _(eng):
                emit('act', eng)

            @block.vector
            def _(eng):
                emit('dve', eng)

            @block.gpsimd
            def _(eng):
                emit('pool', eng)

            @block.sync
            def _(eng):
                emit('sp', eng)
        self.last_writer = {}
        self.readers = {}


def KS(name, *dims):
    out = [(name,)]
    for d in dims:
        if isinstance(d, int):
            d = [d]
        out = [o + (i,) for o in out for i in d]
    return out


class Prog:
    def __init__(self, stop_after=None):
        self.stop_after = stop_after
        self.nc = bass.Bass("TRN2", target_bir_lowering=False)
        self.build()

    def din(self, name, shape, dt=F32):
        return self.nc.dram_tensor(name, list(shape), dt, kind="ExternalInput").ap()

    def dout(self, name, shape, dt=F32):
        return self.nc.dram_tensor(name, list(shape), dt, kind="ExternalOutput").ap()

    def T(self, es, name, shape, dt):
        self.uid = getattr(self, "uid", 0) + 1
        return es.enter_context(self.nc.sbuf_tensor("%s_%d" % (name, self.uid), list(shape), dt))

    def next_ps(self):
        i = self.ps_rot[self.ps_i % len(self.ps_rot)]
        self.ps_i += 1
        return self.ps[i], ('ps', i)

    def load_cast(self, dst, base, src, n, parts=128):
        name = "_".join(str(x) for x in base)
        self.fw.add('pool', lambda e: e.dma_start(out=dst, in_=src), writes=[base], dma=name, cost=parts * n * 4 / 200e3)
        return [base]

    def mm_group(self, out, pairs, reads, writes):
        n = len(pairs)

        def fn(e):
            ins = None
            for i, (l, r) in enumerate(pairs):
                ins = e.matmul(out, lhsT=l, rhs=r, start=(i == 0), stop=(i == n - 1))
            return ins
        cost = 0.0
        for (l, r) in pairs:
            fs = 1
            for d in r.shape[1:]:
                fs *= d
            c = max(fs, 64) / 2400.0 + 0.004
            if r.dtype == F32:
                c *= 4
            cost += c
        self.fw.add('pe', fn, reads=reads, writes=writes, cost=cost)

    def rms_stats(self, src3, size, ones, nk, rstd, src_keys, tag, sq=None, sqkey=('sq',)):
        fw = self.fw
        if sq is None:
            sq = self.sq
        fw.add('act', lambda e: e.activation(out=sq[:, 0:nk, 0:size], in_=src3, func=AF.Square),
               reads=src_keys, writes=[sqkey], cost=0.2 + nk * size / 1200.0)
        ps, pk = self.next_ps()
        self.mm_group(ps[:, 0:size], [(ones[:], sq[:, k, 0:size]) for k in range(nk)],
                      reads=[sqkey, ('const',)], writes=[pk])
        fw.add('act', lambda e: e.activation(out=rstd[:, 0:size], in_=ps[:, 0:size], func=AF.Ln, bias=EPS),
               reads=[pk], writes=[('rstd', tag)], tbl='el')
        fw.add('act', lambda e: e.activation(out=rstd[:, 0:size], in_=rstd[:, 0:size], func=AF.Exp, scale=-0.5),
               reads=[('rstd', tag)], writes=[('rstd', tag)], tbl='el')

    def norm_ops(self, vcol, blocks, nsq, nrs, sub=512, out=None, okeys=None):
        fw = self.fw
        h, xn, vecs = self.h, self.xn, self.vecs
        cnt = 0
        for bi, (st, sz) in enumerate(blocks):
            for s0 in range(0, sz, sub):
                ssz = min(sub, sz - s0)
                a0 = st + s0
                t = cnt % 2
                cnt += 1
                rstd = nrs[t]
                self.rms_stats(h[:, :, a0:a0 + ssz], ssz, self.ones_d, KC, rstd, KS('h', range(KC), bi), ('n', t),
                               sq=nsq, sqkey=('nsq',))
                for k in range(KC):
                    fw.add('dve', lambda e, k=k, a0=a0, ssz=ssz, rstd=rstd: e.scalar_tensor_tensor(
                        out=xn[:, k, a0:a0 + ssz], in0=h[:, k, a0:a0 + ssz], scalar=vecs[:, vcol + k:vcol + k + 1],
                        in1=rstd[:, 0:ssz], op0=ALU.mult, op1=ALU.mult),
                        reads=[('h', k, bi), ('rstd', ('n', t)), ('const',)], writes=[('xn', k, bi)],
                        cost=0.1 + ssz / 900.0)

    def ffn_core(self, which, hid, wgu, wdt, sgt, between=None):
        fw = self.fw
        wg_d, wu_d, wd_d = (self.w1g, self.w1u, self.w1d) if which == 1 else (self.w2g, self.w2u, self.w2d)
        h, xn = self.h, self.xn

        def load_gu(j):
            s = j % 2
            kg = self.load_cast(wgu[:, s, 0, :], ('wg', s), wg_d[j].rearrange("p k n -> p (k n)"), 1024)
            ku = self.load_cast(wgu[:, s, 1, :], ('wu', s), wu_d[j].rearrange("p k n -> p (k n)"), 1024)
            return kg, ku

        def load_d(c):
            s = c % 2
            return self.load_cast(wdt[:, s, :], ('wd', s), wd_d[c].rearrange("p j n -> p (j n)"), FC * 128)

        nxt = load_gu(0)
        cnt = 0
        for j in range(FC):
            kg, ku = nxt
            if j + 1 < FC:
                nxt = load_gu(j + 1)
            s = j % 2
            for bi, (st, sz) in enumerate(BLK_D):
                xk = KS('xn', range(KC), bi)
                psg, pkg = self.next_ps()
                self.mm_group(psg[:, 0:sz], [(wgu[:, s, 0, k * 128:(k + 1) * 128], xn[:, k, st:st + sz]) for k in range(KC)],
                              reads=kg + xk, writes=[pkg])
                psu, pku = self.next_ps()
                self.mm_group(psu[:, 0:sz], [(wgu[:, s, 1, k * 128:(k + 1) * 128], xn[:, k, st:st + sz]) for k in range(KC)],
                              reads=ku + xk, writes=[pku])
                ss = cnt % 2
                cnt += 1
                fw.add('act', lambda e, psg=psg, sz=sz, ss=ss: e.activation(out=sgt[:, ss, 0:sz], in_=psg[:, 0:sz], func=AF.Silu),
                       reads=[pkg], writes=[('sgt', ss)], cost=0.2 + sz / 1200.0, tbl='st')
                fw.add('dve', lambda e, psu=psu, sz=sz, ss=ss, j=j, st=st: e.tensor_tensor(
                    out=hid[:, j, st:st + sz], in0=psu[:, 0:sz], in1=sgt[:, ss, 0:sz], op=ALU.mult),
                    reads=[pku, ('sgt', ss)], writes=[('hid', j, bi)], cost=0.1 + sz / 900.0)
        nxt = load_d(0)
        if between is not None:
            between()
        for c in range(KC):
            kd = nxt
            if c + 1 < KC:
                nxt = load_d(c + 1)
            s = c % 2
            for bi, (st, sz) in enumerate(BLK_D):
                ps, pk = self.next_ps()
                self.mm_group(ps[:, 0:sz], [(wdt[:, s, j * 128:(j + 1) * 128], hid[:, j, st:st + sz]) for j in range(FC)],
                              reads=kd + KS('hid', range(FC), bi), writes=[pk])
                fw.add('dve', lambda e, ps=ps, sz=sz, c=c, st=st: e.scalar_tensor_tensor(
                    out=h[:, c, st:st + sz], in0=ps[:, 0:sz], scalar=0.5, in1=h[:, c, st:st + sz],
                    op0=ALU.mult, op1=ALU.add),
                    reads=[pk, ('h', c, bi)], writes=[('h', c, bi)], cost=0.1 + sz / 900.0)

    def ffn_tiles(self, es):
        hid = self.T(es, "hid", [128, FC, TS], BF16)
        wgu = self.T(es, "wgu", [128, 2, 2, 1024], BF16)
        wdt = self.T(es, "wdt", [128, 2, FC * 128], BF16)
        sgt = self.T(es, "sgt", [128, 2, 512], F32)
        nsq = self.T(es, "nsq", [128, KC, 512], BF16)
        nrs = [self.T(es, "nrs%d" % i, [128, 512], F32) for i in range(2)]
        return hid, wgu, wdt, sgt, nsq, nrs

    def phase_A(self, sb, prefetch):
        fw = self.fw
        h = self.h
        with ExitStack() as es:
            hid, wgu, wdt, sgt, nsq, nrs = self.ffn_tiles(es)
            self.ps_rot = list(range(7))
            self.A_ops(sb, hid, wgu, wdt, sgt, nsq, nrs, prefetch)
            fw.flush()

    def A_ops(self, sb, hid, wgu, wdt, sgt, nsq, nrs, prefetch):
        fw = self.fw
        h = self.h
        for bi, (st, sz) in enumerate(BLK_D):
            fw.add('sp', lambda e, st=st, sz=sz: e.dma_start(out=h[:, :, st:st + sz], in_=self.xT[sb, :, :, st:st + sz]),
                   writes=KS('h', range(KC), bi), dma='hload%d' % bi, cost=8.0)
        self.norm_ops(V_FFN1, BLK_D, nsq, nrs)
        self.ffn_core(1, hid, wgu, wdt, sgt, between=prefetch)

    def phase_C(self, sb, next_A=None):
        fw = self.fw
        h, xn, vecs = self.h, self.xn, self.vecs
        with ExitStack() as es:
            hid, wgu, wdt, sgt, nsq, nrs = self.ffn_tiles(es)
            pb = self.T(es, "pb", [128, 2, TS], BF16)
            wpg = self.T(es, "wpg", [128, 2, 1024], BF16)
            wpp = self.T(es, "wpp", [128, 2, 256], BF16)
            gt = self.T(es, "gt", [128, 2, 512], F32)
            nyt = 1 if next_A is not None else 2
            yt = self.T(es, "yt", [128, nyt, KC, 352], F32)
            self.ps_rot = list(range(7))
            self.norm_ops(V_FFN2, BLK_D, nsq, nrs)
            kp = []
            for k in range(2):
                kp += self.load_cast(pb[:, k, :], ('pb', k), self.pT[sb, :, k, :], TS)
            self.ffn_core(2, hid, wgu, wdt, sgt)
            self.norm_ops(V_PLE, BLK_D, nsq, nrs)

            def load_w(c):
                s = c % 2
                k1 = self.load_cast(wpg[:, s, :], ('wpg', s), self.wpg_d[c].rearrange("p k n -> p (k n)"), 1024)
                k2 = self.load_cast(wpp[:, s, :], ('wpp', s), self.wpp_d[c].rearrange("p k n -> p (k n)"), 256)
                return k1, k2
            nxt = load_w(0)
            cnt = 0
            for c in range(KC):
                k1, k2 = nxt
                if c + 1 < KC:
                    nxt = load_w(c + 1)
                s = c % 2
                for bi, (st, sz) in enumerate(BLK_D):
                    psg, pkg = self.next_ps()
                    self.mm_group(psg[:, 0:sz], [(wpg[:, s, k * 128:(k + 1) * 128], xn[:, k, st:st + sz]) for k in range(KC)],
                                  reads=k1 + KS('xn', range(KC), bi), writes=[pkg])
                    psp, pkp = self.next_ps()
                    self.mm_group(psp[:, 0:sz], [(wpp[:, s, k * 128:(k + 1) * 128], pb[:, k, st:st + sz]) for k in range(2)],
                                  reads=k2 + kp, writes=[pkp])
                    ss = cnt % 2
                    cnt += 1
                    fw.add('act', lambda e, psg=psg, sz=sz, ss=ss: e.activation(out=gt[:, ss, 0:sz], in_=psg[:, 0:sz], func=AF.Sigmoid),
                           reads=[pkg], writes=[('gt', ss)], tbl='sg')
                    fw.add('dve', lambda e, psp=psp, sz=sz, ss=ss: e.tensor_tensor(
                        out=gt[:, ss, 0:sz], in0=psp[:, 0:sz], in1=gt[:, ss, 0:sz], op=ALU.mult),
                        reads=[pkp, ('gt', ss)], writes=[('gt', ss)])
                    fw.add('dve', lambda e, sz=sz, ss=ss, c=c, st=st: e.tensor_tensor(
                        out=h[:, c, st:st + sz], in0=h[:, c, st:st + sz], in1=gt[:, ss, 0:sz], op=ALU.add),
                        reads=[('gt', ss), ('h', c, bi)], writes=[('h', c, bi)])
            for bi, (st, sz) in enumerate(BLK_D):
                t = bi % 2
                rstd = nrs[t]
                self.rms_stats(h[:, :, st:st + sz], sz, self.ones_d, KC, rstd, KS('h', range(KC), bi), ('n', t),
                               sq=nsq, sqkey=('nsq',))
                t = bi % nyt
                for k in range(KC):
                    fw.add('dve', lambda e, k=k, st=st, sz=sz, rstd=rstd, t=t: e.scalar_tensor_tensor(
                        out=yt[:, t, k, 0:sz], in0=h[:, k, st:st + sz], scalar=vecs[:, V_FIN + k:V_FIN + k + 1],
                        in1=rstd[:, 0:sz], op0=ALU.mult, op1=ALU.mult),
                        reads=[('h', k, bi), ('rstd', ('n', bi % 2)), ('const',)], writes=[('yt', t, k)])
                fw.add('sp', lambda e, st=st, sz=sz, t=t: e.dma_start(out=self.yT[sb, :, :, st:st + sz], in_=yt[:, t, :, 0:sz]),
                       reads=KS('yt', t, range(KC)), dma='yout%d' % t, cost=6.0)
            if next_A is not None:
                nsb, prefetch = next_A
                self.A_ops(nsb, hid, wgu, wdt, sgt, nsq, nrs, prefetch)
            fw.flush()

    def raw_out(self, sb):
        fw = self.fw
        for k in range(KC):
            fw.add('sp', lambda e, k=k: e.dma_start(out=self.yT[sb, :, k, :], in_=self.h[:, k, :]),
                   reads=KS('h', k, range(len(BLK_D))), dma='yout%d' % (k % 2))
        fw.flush()

    def phase_mixer(self, sb, pre=(0, 1)):
        fw = self.fw
        h, xn, vecs = self.h, self.xn, self.vecs
        pblocks = [(i * MB, MB) for i in range(TP // MB)]
        ablocks = pblocks + [(TP, TSM)]
        nab = len(ablocks)
        npb = len(pblocks)
        with ExitStack() as es:
            T = lambda name, shape, dt: self.T(es, name, shape, dt)
            nsq = T("nsq", [128, KC, 256], BF16)
            nrs = [T("nrs%d" % i, [128, 256], F32) for i in range(2)]
            self.ps_rot = [3, 4]
            self.norm_ops(V_MIX, ablocks, nsq, nrs, sub=256)
            gkw = T("gkw", [128, 128], BF16)
            gklr = T("gklr", [32, TS], BF16)
            wgk = T("wgk", [32, 512], BF16)
            ML = MB + 16
            uext = T("uext", [128, 2, 1, ML], F32)
            wA = T("wA", [128, 2, 1, ML], F32)
            wB = T("wB", [128, 2, 1, ML], F32)
            sgt = T("sgt", [128, 2, MB], F32)
            aout = T("aout", [128, 2, MB], F32)
            pooled = T("pooled", [128, 2, MB], BF16)
            spt = T("spt", [128, MB], F32)
            Et = T("Et", [128, MB], F32)
            Ei = T("Ei", [128, MB], F32)
            qt = T("qt", [128, MB], BF16)
            kt = T("kt", [128, MB], BF16)
            kTM = T("kTM", [128, MB // 128, 128], BF16)
            vTM = T("vTM", [128, MB // 128, 256], BF16)
            ATm = T("ATm", [128, MB // 128, 128], BF16)
            osb = T("osb", [128, 2, MB], F32)
            self.sq = T("osq", [128, 2, MB], BF16)
            rso = T("rso", [128, MB], F32)
            siga = T("siga", [128, 2, MB], F32)
            sigb = T("sigb", [128, 2, MB], F32)
            mix = T("mix", [128, 2, MB], BF16)
            uext_s = T("uext_s", [128, 2, NBS, LS + 16], F32)
            wA_s = T("wA_s", [128, 2, NBS, LS + 16], F32)
            wB_s = T("wB_s", [128, 2, NBS, LS + 16], F32)
            ustg = T("ustg", [128, 2, NBS, 16], F32)
            S0 = T("S0", [128, NBS, 256], F32)
            S0b = T("S0b", [128, NBS, 256], BF16)
            kmask = T("kmask", [32, NBS, 128], BF16)
            ps = self.ps
            psbf = self.psbf
            self.dense = [0, 1, 2]
            self.di = 0

            def nd():
                i = self.dense[self.di % len(self.dense)]
                self.di += 1
                return ps[i], ('ps', i)
            self.ps_rot = [3, 4]
            self.ps_i = 0
            nsm = self.next_ps
            po = [ps[5], ps[6]]
            pok = [('ps', 5), ('ps', 6)]
            S, Sb = self.S, self.Sb

            kgkw = self.load_cast(gkw[:, :], ('gkw',), self.win_gk.rearrange("p k n -> p (k n)"), 128)
            kwgk = self.load_cast(wgk[:, :], ('wgk',), self.wgk_d[:, :], 512)
            fw.add('dve', lambda e: e.memset(gklr[:, :], 1.0), writes=KS('gklr', range(nab)))
            for bi, (st, sz) in enumerate(ablocks):
                p_, pk = nd()
                self.mm_group(p_[0:16, 0:sz], [(gkw[:, k * 16:(k + 1) * 16], xn[:, k, st:st + sz]) for k in range(KC)],
                              reads=kgkw + KS('xn', range(KC), bi), writes=[pk])
                fw.add('act', lambda e, p_=p_, st=st, sz=sz: e.activation(out=gklr[0:16, st:st + sz], in_=p_[0:16, 0:sz], func=AF.Copy),
                       reads=[pk], writes=[('gklr', bi)])

            load_head = self.load_head

            def proj_fm(hd, wk, u, bi, st, sz):
                s = hd % 2
                p_, pk = nd()
                self.mm_group(p_[:, 0:sz], [(self.mw[s][0][:, u, k * 128:(k + 1) * 128], xn[:, k, st:st + sz]) for k in range(KC)],
                              reads=wk[u] + KS('xn', range(KC), bi), writes=[pk])
                return p_, pk

            def softplus_gk(p_, pk, rows, sz):
                fw.add('act', lambda e: e.activation(out=spt[0:rows, 0:sz], in_=p_[0:rows, 0:sz], func=AF.Exp, scale=-1.0),
                       reads=[pk], writes=[('spt',)], tbl='el')
                fw.add('act', lambda e: e.activation(out=spt[0:rows, 0:sz], in_=spt[0:rows, 0:sz], func=AF.Ln, bias=1.0),
                       reads=[('spt',)], writes=[('spt',)], tbl='el')

            def u_proj(it, ue, nb, L):
                hd, wk, bi, st, sz = it['hd'], it['wk'], it['bi'], it['st'], it['sz']
                for cc in range(2):
                    p_, pk = proj_fm(hd, wk, cc, bi, st, sz)
                    fw.add('act', lambda e, p_=p_, cc=cc: e.activation(
                        out=ue[:, cc, :, 16:16 + L], in_=p_[:, 0:sz].rearrange("p (b l) -> p b l", b=nb), func=AF.Copy),
                        reads=[pk], writes=[('ue', cc)])

            def window_means(it, ue, A, B, nb, L, first):
                hd, sz = it['hd'], it['sz']
                w = POOL_W[hd]
                W = L + 16
                src, sk = ue, KS('ue', range(2))
                lvl = 1
                tog = 0
                while lvl < w:
                    dst, dk = (A, [('wA',)]) if tog == 0 else (B, [('wB',)])
                    lo = 2 * lvl - 1
                    fw.add('dve', lambda e, src=src, dst=dst, lo=lo, lvl=lvl: e.tensor_tensor(
                        out=dst[:, :, :, lo:W], in0=src[:, :, :, lo:W], in1=src[:, :, :, lo - lvl:W - lvl], op=ALU.add),
                        reads=sk, writes=dk, cost=0.1 + 2 * W * nb / 900.0)
                    src, sk = dst, dk
                    lvl *= 2
                    tog ^= 1
                if first:
                    for cc in range(2):
                        fw.add('dve', lambda e, src=src, cc=cc: e.tensor_tensor(
                            out=src[:, cc, 0, 16:32], in0=src[:, cc, 0, 16:32], in1=self.cfix[:, hd, :], op=ALU.mult),
                            reads=sk + [('const',)], writes=sk)
                pl = pooled[:, :, 0:sz].rearrange("p c (b l) -> p c b l", b=nb)
                fw.add('dve', lambda e, src=src: e.scalar_tensor_tensor(
                    out=pl, in0=src[:, :, :, 16:16 + L], scalar=1.0 / w, in1=ue[:, :, :, 16:16 + L],
                    op0=ALU.mult, op1=ALU.subtract),
                    reads=sk + KS('ue', range(2)), writes=[('pooled',)], cost=0.1 + 2 * sz / 900.0)

            def pool_map(it):
                hd, wk, sz = it['hd'], it['wk'], it['sz']
                s = hd % 2
                for dc in range(2):
                    p_, pk = nd()
                    self.mm_group(p_[:, 0:sz], [(self.mw[s][3][:, cc * 256 + dc * 128: cc * 256 + dc * 128 + 128], pooled[:, cc, 0:sz]) for cc in range(2)],
                                  reads=wk['p'] + [('pooled',)], writes=[pk])
                    col = V_PSC + 2 * hd + dc
                    fw.add('dve', lambda e, p_=p_, dc=dc, col=col: e.tensor_scalar_mul(
                        out=aout[:, dc, 0:sz], in0=p_[:, 0:sz], scalar1=vecs[:, col:col + 1]),
                        reads=[pk, ('const',)], writes=[('aout', dc)])

            def qk_proj(it):
                hd, wk, bi, st, sz = it['hd'], it['wk'], it['bi'], it['st'], it['sz']
                pq, pkq = proj_fm(hd, wk, 2, bi, st, sz)
                fw.add('dve', lambda e: e.scalar_tensor_tensor(out=qt[:, 0:sz], in0=pq[:, 0:sz], scalar=QSCALE, in1=Et[:, 0:sz],
                                                               op0=ALU.mult, op1=ALU.mult),
                       reads=[pkq, ('Et',)], writes=[('qt',)])
                pk_, pkk = proj_fm(hd, wk, 3, bi, st, sz)
                fw.add('dve', lambda e: e.tensor_tensor(out=kt[:, 0:sz], in0=pk_[:, 0:sz], in1=Ei[:, 0:sz], op=ALU.mult),
                       reads=[pkk, ('Ei',)], writes=[('kt',)])

            def gate_proj(it, u0, func, dst, dname):
                hd, wk, bi, st, sz = it['hd'], it['wk'], it['bi'], it['st'], it['sz']
                for cc in range(2):
                    p_, pk = proj_fm(hd, wk, u0 + cc, bi, st, sz)
                    if func == AF.Silu:
                        fw.add('act', lambda e, p_=p_, cc=cc: e.activation(out=dst[:, cc, 0:sz], in_=p_[:, 0:sz], func=AF.Silu),
                               reads=[pk], writes=[(dname, cc)], tbl='st')
                    else:
                        fw.add('act', lambda e, p_=p_, cc=cc: e.activation(out=dst[:, cc, 0:sz], in_=p_[:, 0:sz], func=AF.Tanh, scale=0.5),
                               reads=[pk], writes=[(dname, cc)], tbl='st')

            def siga_aout(it):
                sz = it['sz']
                fw.add('dve', lambda e: e.scalar_tensor_tensor(out=siga[:, :, 0:sz], in0=siga[:, :, 0:sz], scalar=1.0, in1=aout[:, :, 0:sz],
                                                               op0=ALU.add, op1=ALU.mult),
                       reads=KS('siga', range(2)) + KS('aout', range(2)), writes=KS('siga', range(2)), cost=0.1 + 2 * sz / 900.0)

            def front_prompt(it):
                hd, wk, bi, st, sz = it['hd'], it['wk'], it['bi'], it['st'], it['sz']
                s = hd % 2
                nt = sz // 128
                first = (sb == 0 and bi == 0)
                p_, pk = nsm()

                def gkmm(e, p_=p_):
                    ins = None
                    for tt in range(nt):
                        ins = e.matmul(p_[:, tt * 128:(tt + 1) * 128], lhsT=gklr[0:17, st + tt * 128: st + (tt + 1) * 128],
                                       rhs=wgk[0:17, hd * 128:(hd + 1) * 128], start=True, stop=True)
                    return ins
                fw.add('pe', gkmm, reads=[('gklr', bi)] + kwgk, writes=[pk], cost=0.07 * nt)
                softplus_gk(p_, pk, 128, sz)
                fw.add('act', lambda e: e.activation(out=uext[:, :, 0, 0:16], in_=self.ucarry[:, hd, :, :], func=AF.Copy),
                       reads=[('ucarry', hd)], writes=KS('ue', range(2)))
                u_proj(it, uext, 1, sz)
                fw.add('act', lambda e: e.activation(out=self.ucarry[:, hd, :, :], in_=uext[:, :, 0, sz:sz + 16], func=AF.Copy),
                       reads=KS('ue', range(2)), writes=[('ucarry', hd)])
                if sb == NSB - 1 and bi == npb - 1:
                    fw.add('sp', lambda e: e.dma_start(out=self.npp[:, 2 * hd:2 * hd + 2, :], in_=uext[:, :, 0, sz + 1:sz + 16]),
                           reads=KS('ue', range(2)), dma='npp')
                window_means(it, uext, wA, wB, 1, sz, first)
                p2, pk2 = nsm()

                def bmm(e, p2=p2):
                    ins = None
                    for tt in range(nt):
                        ins = e.matmul(p2[:, tt * 128:(tt + 1) * 128], lhsT=spt[:, tt * 128:(tt + 1) * 128],
                                       rhs=self.Umat[:, :], start=True, stop=True)
                    return ins
                fw.add('pe', bmm, reads=[('spt',), ('const',)], writes=[pk2], cost=0.35 * nt)
                fw.add('act', lambda e: e.activation(out=Et[:, 0:sz], in_=p2[:, 0:sz], func=AF.Exp), reads=[pk2], writes=[('Et',)], tbl='el')
                fw.add('act', lambda e: e.activation(out=Ei[:, 0:sz], in_=p2[:, 0:sz], func=AF.Exp, scale=-1.0), reads=[pk2], writes=[('Ei',)], tbl='el')
                for t2 in range(0, nt, 2):
                    pv, pkv = nd()
                    ntt = min(2, nt - t2)

                    def vmm(e, pv=pv, t2=t2, ntt=ntt):
                        ins = None
                        for q in range(ntt):
                            tt = t2 + q
                            for k in range(KC):
                                ins = e.matmul(pv[:, q * 256:(q + 1) * 256], lhsT=xn[:, k, st + tt * 128: st + (tt + 1) * 128],
                                               rhs=self.mw[s][1][:, k * 256:(k + 1) * 256], start=(k == 0), stop=(k == KC - 1))
                        return ins
                    fw.add('pe', vmm, reads=wk['v'] + KS('xn', range(KC), bi), writes=[pkv], cost=0.9 * ntt)
                    fw.add('act', lambda e, pv=pv, t2=t2, ntt=ntt: e.activation(
                        out=vTM[:, t2:t2 + ntt, :], in_=pv[:, 0:ntt * 256].rearrange("p (t c) -> p t c", t=ntt), func=AF.Copy),
                        reads=[pkv], writes=[('vTM', t2 // 2)])
                qk_proj(it)
                pool_map(it)

            def middle_prompt(it):
                hd, wk, bi, st, sz = it['hd'], it['wk'], it['bi'], it['st'], it['sz']
                nt = sz // 128
                vk = [('vTM', i) for i in range((nt + 1) // 2)]

                def ktr(e):
                    ins = None
                    for tt in range(nt):
                        ins = e.transpose(psbf[:, tt * 128:(tt + 1) * 128], kt[:, tt * 128:(tt + 1) * 128], self.ident[:, :])
                    return ins
                fw.add('pe', ktr, reads=[('kt',), ('const',)], writes=[('psbf',)], cost=0.1 * nt)
                fw.add('act', lambda e: e.activation(out=kTM[:, 0:nt, :], in_=psbf[:, 0:nt * 128].rearrange("p (t c) -> p t c", t=nt), func=AF.Copy),
                       reads=[('psbf',)], writes=[('kTM',)])
                pa, pka = nsm()

                def amm(e, pa=pa):
                    ins = None
                    for tt in range(nt):
                        ins = e.matmul(pa[:, tt * 128:(tt + 1) * 128], lhsT=kt[:, tt * 128:(tt + 1) * 128],
                                       rhs=qt[:, tt * 128:(tt + 1) * 128], start=True, stop=True)
                    return ins
                fw.add('pe', amm, reads=[('kt',), ('qt',)], writes=[pka], cost=0.07 * nt)
                fw.add('dve', lambda e: e.tensor_tensor(out=ATm[:, 0:nt, :], in0=pa[:, 0:nt * 128].rearrange("p (t i) -> p t i", t=nt),
                                                        in1=self.mask4[:, 0:nt, :], op=ALU.mult),
                       reads=[pka, ('const',)], writes=[('ATm',)])
                fillers = [lambda: gate_proj(it, 4, AF.Silu, sgt, 'sgt'),
                           lambda: gate_proj(it, 6, AF.Sigmoid, siga, 'siga'),
                           lambda: (gate_proj(it, 8, AF.Sigmoid, sigb, 'sigb'), siga_aout(it))]
                for tt in range(nt):
                    c0 = tt * 128

                    def omm(e, tt=tt, c0=c0):
                        ins = None
                        for vc in range(2):
                            e.matmul(po[vc][:, c0:c0 + 128], lhsT=Sb[:, hd, vc * 128:(vc + 1) * 128], rhs=qt[:, c0:c0 + 128],
                                     start=True, stop=False)
                            ins = e.matmul(po[vc][:, c0:c0 + 128], lhsT=vTM[:, tt, vc * 128:(vc + 1) * 128],
                                           rhs=ATm[:, tt, :], start=False, stop=True)
                        return ins
                    fw.add('pe', omm, reads=[('Sb', hd), ('qt',), ('ATm',)] + vk, writes=pok, cost=0.3)
                    pS, pkS = nsm()
                    fw.add('pe', lambda e, pS=pS, tt=tt: e.matmul(pS[:, 0:256], lhsT=kTM[:, tt, :], rhs=vTM[:, tt, :], start=True, stop=True),
                           reads=[('kTM',)] + vk, writes=[pkS])
                    ecol = Et[:, c0 + 127:c0 + 128]
                    fw.add('dve', lambda e, ecol=ecol: e.tensor_scalar_mul(out=S[:, hd, :], in0=S[:, hd, :], scalar1=ecol),
                           reads=[('S', hd), ('Et',)], writes=[('S', hd)])
                    fw.add('dve', lambda e, ecol=ecol, pS=pS: e.scalar_tensor_tensor(
                        out=S[:, hd, :], in0=pS[:, 0:256], scalar=ecol, in1=S[:, hd, :], op0=ALU.mult, op1=ALU.add),
                        reads=[pkS, ('S', hd), ('Et',)], writes=[('S', hd)])
                    fw.add('act', lambda e: e.activation(out=Sb[:, hd, :], in_=S[:, hd, :], func=AF.Copy),
                           reads=[('S', hd)], writes=[('Sb', hd)])
                    if fillers:
                        fillers.pop(0)()
                while fillers:
                    fillers.pop(0)()
                if sb == NSB - 1 and bi == npb - 1:
                    fw.add('sp', lambda e: e.dma_start(out=self.ngp[hd, :, :], in_=S[:, hd, :]), reads=[('S', hd)], dma='ngp')

            def front_sample(it):
                hd, wk, bi, st, sz = it['hd'], it['wk'], it['bi'], it['st'], it['sz']
                s = hd % 2
                fw.add('sp', lambda e: e.dma_start(out=ustg[:, :, :, :], in_=self.spT[sb, :, hd, :, :, :]), writes=[('ustg',)], dma='ustg')
                fw.add('act', lambda e: e.activation(out=uext_s[:, :, :, 0:16], in_=ustg[:, :, :, :], func=AF.Copy),
                       reads=[('ustg',)], writes=KS('ue', range(2)))
                fw.add('sp', lambda e: e.dma_start(out=S0[:, :, :], in_=self.sgla[sb, hd, :, :, :]), writes=KS('S0', range(NBS)), dma='S0')
                fw.add('act', lambda e: e.activation(out=S0b[:, :, :], in_=S0[:, :, :], func=AF.Copy), reads=KS('S0', range(NBS)), writes=[('S0b',)])
                p_, pk = nsm()
                fw.add('pe', lambda e: e.matmul(p_[0:sz, 0:128], lhsT=gklr[0:17, st:st + sz], rhs=wgk[0:17, hd * 128:(hd + 1) * 128],
                                                start=True, stop=True),
                       reads=[('gklr', bi)] + kwgk, writes=[pk])
                softplus_gk(p_, pk, sz, 128)
                u_proj(it, uext_s, NBS, LS)
                fw.add('sp', lambda e: e.dma_start(out=self.nps[sb, :, hd, :, :, :], in_=uext_s[:, :, :, LS + 1:LS + 16]),
                       reads=KS('ue', range(2)), dma='nps')
                window_means(it, uext_s, wA_s, wB_s, NBS, LS, False)
                p2, pk2 = nsm()
                fw.add('pe', lambda e: e.matmul(p2[:, 0:sz], lhsT=spt[0:sz, 0:128], rhs=self.Umat_s[0:sz, 0:sz], start=True, stop=True),
                       reads=[('spt',), ('const',)], writes=[pk2])
                fw.add('act', lambda e: e.activation(out=Et[:, 0:sz], in_=p2[:, 0:sz], func=AF.Exp), reads=[pk2], writes=[('Et',)], tbl='el')
                fw.add('act', lambda e: e.activation(out=Ei[:, 0:sz], in_=p2[:, 0:sz], func=AF.Exp, scale=-1.0), reads=[pk2], writes=[('Ei',)], tbl='el')
                pv, pkv = nd()
                self.mm_group(pv[0:sz, 0:256], [(xn[:, k, st:st + sz], self.mw[s][1][:, k * 256:(k + 1) * 256]) for k in range(KC)],
                              reads=wk['v'] + KS('xn', range(KC), bi), writes=[pkv])
                fw.add('act', lambda e: e.activation(out=vTM[0:sz, 0, :], in_=pv[0:sz, 0:256], func=AF.Copy),
                       reads=[pkv], writes=[('vTM', 0)])
                qk_proj(it)
                pool_map(it)

            def middle_sample(it):
                hd, wk, bi, st, sz = it['hd'], it['wk'], it['bi'], it['st'], it['sz']
                fw.add('pe', lambda e: e.transpose(psbf[0:sz, 0:128], kt[:, 0:sz], self.ident[:, :]),
                       reads=[('kt',), ('const',)], writes=[('psbf',)])
                fw.add('act', lambda e: e.activation(out=kTM[0:sz, 0, :], in_=psbf[0:sz, 0:128], func=AF.Copy),
                       reads=[('psbf',)], writes=[('kTM',)])
                for b in range(NBS):
                    fw.add('dve', lambda e, b=b: e.tensor_scalar_mul(out=kmask[0:sz, b, :], in0=kTM[0:sz, 0, :],
                                                                     scalar1=self.onehot[0:sz, b:b + 1]),
                           reads=[('kTM',), ('const',)], writes=[('kmask', b)])
                pa, pka = nsm()
                fw.add('pe', lambda e: e.matmul(pa[0:sz, 0:sz], lhsT=kt[:, 0:sz], rhs=qt[:, 0:sz], start=True, stop=True),
                       reads=[('kt',), ('qt',)], writes=[pka])
                fw.add('dve', lambda e: e.tensor_tensor(out=ATm[0:sz, 0, 0:sz], in0=pa[0:sz, 0:sz], in1=self.mask_s[0:sz, 0:sz], op=ALU.mult),
                       reads=[pka, ('const',)], writes=[('ATm',)])

                def omm(e):
                    ins = None
                    for vc in range(2):
                        for b in range(NBS):
                            e.matmul(po[vc][:, b * LS:(b + 1) * LS], lhsT=vTM[0:sz, 0, vc * 128:(vc + 1) * 128],
                                     rhs=ATm[0:sz, 0, b * LS:(b + 1) * LS], start=True, stop=False)
                            ins = e.matmul(po[vc][:, b * LS:(b + 1) * LS], lhsT=S0b[:, b, vc * 128:(vc + 1) * 128],
                                           rhs=qt[:, b * LS:(b + 1) * LS], start=False, stop=True)
                    return ins
                fw.add('pe', omm, reads=[('S0b',), ('qt',), ('ATm',), ('vTM', 0)], writes=pok, cost=1.2)
                gate_proj(it, 4, AF.Silu, sgt, 'sgt')
                for b in range(NBS):
                    pS, pkS = nsm()
                    fw.add('pe', lambda e, pS=pS, b=b: e.matmul(pS[:, 0:256], lhsT=kmask[0:sz, b, :], rhs=vTM[0:sz, 0, :], start=True, stop=True),
                           reads=[('kmask', b), ('vTM', 0)], writes=[pkS])
                    ecol = Et[:, b * LS + LS - 1:b * LS + LS]
                    fw.add('dve', lambda e, ecol=ecol, b=b: e.tensor_scalar_mul(out=S0[:, b, :], in0=S0[:, b, :], scalar1=ecol),
                           reads=[('S0', b), ('Et',)], writes=[('S0', b)])
                    fw.add('dve', lambda e, ecol=ecol, pS=pS, b=b: e.scalar_tensor_tensor(
                        out=S0[:, b, :], in0=pS[:, 0:256], scalar=ecol, in1=S0[:, b, :], op0=ALU.mult, op1=ALU.add),
                        reads=[pkS, ('S0', b), ('Et',)], writes=[('S0', b)])
                fw.add('sp', lambda e: e.dma_start(out=self.ngs[sb, hd, :, :, :], in_=S0[:, :, :]),
                       reads=KS('S0', range(NBS)), dma='ngs')
                gate_proj(it, 6, AF.Sigmoid, siga, 'siga')
                gate_proj(it, 8, AF.Sigmoid, sigb, 'sigb')
                siga_aout(it)

            def tail(it):
                hd, wk, bi, st, sz = it['hd'], it['wk'], it['bi'], it['st'], it['sz']
                s = hd % 2
                fw.add('act', lambda e: e.activation(out=osb[:, 0, 0:sz], in_=po[0][:, 0:sz], func=AF.Copy),
                       reads=[pok[0]], writes=[('osb', 0)])
                fw.add('act', lambda e: e.activation(out=osb[:, 1, 0:sz], in_=po[1][:, 0:sz], func=AF.Copy),
                       reads=[pok[1]], writes=[('osb', 1)])
                self.rms_stats(osb[:, :, 0:sz], sz, self.ones_v, 2, rso, KS('osb', range(2)), 'o')
                for vc in range(2):
                    fw.add('dve', lambda e, vc=vc: e.scalar_tensor_tensor(
                        out=osb[:, vc, 0:sz], in0=osb[:, vc, 0:sz], scalar=vecs[:, V_GLA + vc:V_GLA + vc + 1],
                        in1=rso[:, 0:sz], op0=ALU.mult, op1=ALU.mult),
                        reads=[('osb', vc), ('rstd', 'o'), ('const',)], writes=[('osb', vc)])
                fw.add('dve', lambda e: e.tensor_tensor(out=osb[:, :, 0:sz], in0=osb[:, :, 0:sz], in1=sgt[:, :, 0:sz], op=ALU.mult),
                       reads=KS('osb', range(2)) + KS('sgt', range(2)), writes=KS('osb', range(2)), cost=0.1 + 2 * sz / 900.0)
                fw.add('dve', lambda e: e.scalar_tensor_tensor(out=osb[:, :, 0:sz], in0=sigb[:, :, 0:sz], scalar=1.0, in1=osb[:, :, 0:sz],
                                                               op0=ALU.add, op1=ALU.mult),
                       reads=KS('osb', range(2)) + KS('sigb', range(2)), writes=KS('osb', range(2)), cost=0.1 + 2 * sz / 900.0)
                fw.add('dve', lambda e: e.tensor_tensor(out=mix[:, :, 0:sz], in0=osb[:, :, 0:sz], in1=siga[:, :, 0:sz], op=ALU.add),
                       reads=KS('osb', range(2)) + KS('siga', range(2)), writes=[('mix',)], cost=0.1 + 2 * sz / 900.0)
                for c in range(KC):
                    p_, pk = nd()
                    self.mm_group(p_[:, 0:sz], [(self.mw[s][2][:, kc * 1024 + c * 128: kc * 1024 + c * 128 + 128], mix[:, kc, 0:sz]) for kc in range(2)],
                                  reads=wk['o'] + [('mix',)], writes=[pk])
                    fw.add('dve', lambda e, p_=p_, c=c: e.scalar_tensor_tensor(
                        out=h[:, c, st:st + sz], in0=p_[:, 0:sz], scalar=0.5, in1=h[:, c, st:st + sz], op0=ALU.mult, op1=ALU.add),
                        reads=[pk, ('h', c, bi)], writes=[('h', c, bi)])

            def pre_keys(s_):
                d = {u: [('wfm', s_, u)] for u in range(10)}
                d.update({'v': [('wv', s_)], 'o': [('wo', s_)], 'p': [('pw', s_)]})
                return d
            wks = {}
            for hd0 in (0, 1):
                if hd0 in pre:
                    wks[hd0] = pre_keys(hd0)
                else:
                    if self.mw[hd0 % 2] is None:
                        self.alloc_mw(es, hd0 % 2)
                    wks[hd0] = load_head(hd0)
            items = []
            for hd in range(4):
                for bi, (st, sz) in enumerate(pblocks):
                    items.append(dict(hd=hd, bi=bi, st=st, sz=sz, kind='p', last=False))
                items.append(dict(hd=hd, bi=npb, st=TP, sz=TSM, kind='s', last=True))

            def front(it):
                it['wk'] = wks[it['hd']]
                (front_prompt if it['kind'] == 'p' else front_sample)(it)

            def middle(it):
                (middle_prompt if it['kind'] == 'p' else middle_sample)(it)
            def th_front(itn):
                def f():
                    self.dense, self.ps_rot = [0, 1], [3]
                    front(itn)
                return f

            def th_tail(itc):
                def f():
                    self.dense, self.ps_rot = [2], [4]
                    tail(itc)
                return f
            front(items[0])
            for i, it in enumerate(items):
                self.dense, self.ps_rot = [0, 1, 2], [3, 4]
                middle(it)
                if i + 1 < len(items):
                    fw.merged([th_front(items[i + 1]), th_tail(it)])
                else:
                    self.dense, self.ps_rot = [0, 1, 2], [3, 4]
                    tail(it)
                if it['last'] and it['hd'] + 2 < 4:
                    wks[it['hd'] + 2] = load_head(it['hd'] + 2)
            fw.flush()

    def alloc_mw(self, es, s):
        self.mw[s] = (self.T(es, "wfm%d" % s, [128, 10, 1024], BF16), self.T(es, "wv%d" % s, [128, 2048], BF16),
                      self.T(es, "wo%d" % s, [128, 2048], BF16), self.T(es, "pw%d" % s, [128, 512], BF16))

    def load_head(self, hd):
        s = hd % 2
        wfm, wv, wo, pw = self.mw[s]
        keys = {}
        for u in range(10):
            keys[u] = self.load_cast(wfm[:, u, :], ('wfm', s, u), self.win_fm[hd, u].rearrange("p k n -> p (k n)"), 1024)
        keys['v'] = self.load_cast(wv[:, :], ('wv', s), self.win_v[hd].rearrange("p k n -> p (k n)"), 2048)
        keys['o'] = self.load_cast(wo[:, :], ('wo', s), self.wout_d[hd].rearrange("p k n -> p (k n)"), 2048)
        keys['p'] = self.load_cast(pw[:, :], ('pw', s), self.poolw_d[hd].rearrange("p k n -> p (k n)"), 512)
        return keys

    def build(self):
        nc = self.nc
        self.xT = self.din("xT", [NSB, 128, KC, TS])
        self.pT = self.din("pT", [NSB, 128, 2, TS])
        self.spT = self.din("spT", [NSB, 128, 4, 2, NBS, 16])
        self.sgla = self.din("sgla", [NSB, 4, 128, NBS, 256])
        self.w1g = self.din("w1g", [FC, 128, KC, 128])
        self.w1u = self.din("w1u", [FC, 128, KC, 128])
        self.w1d = self.din("w1d", [KC, 128, FC, 128])
        self.w2g = self.din("w2g", [FC, 128, KC, 128])
        self.w2u = self.din("w2u", [FC, 128, KC, 128])
        self.w2d = self.din("w2d", [KC, 128, FC, 128])
        self.win_fm = self.din("win_fm", [4, 10, 128, KC, 128])
        self.win_v = self.din("win_v", [4, 128, KC, 256])
        self.win_gk = self.din("win_gk", [128, KC, 16])
        self.wout_d = self.din("wout", [4, 128, 2, 1024])
        self.poolw_d = self.din("poolw", [4, 128, 2, 256])
        self.wgk_d = self.din("wgk", [32, 512])
        self.wpg_d = self.din("wpg", [KC, 128, KC, 128])
        self.wpp_d = self.din("wpp", [KC, 128, 2, 128])
        self.vecs_d = self.din("vecs", [128, NVEC])
        self.ident_d = self.din("ident", [128, 128], BF16)
        self.ones_d_d = self.din("ones_d", [128, 128], BF16)
        self.ones_v_d = self.din("ones_v", [128, 128], BF16)
        self.Umat_d = self.din("Umat", [128, 128])
        self.Umat_s_d = self.din("Umat_s", [32, 32])
        self.mask4_d = self.din("mask4", [128, 4, 128])
        self.mask_s_d = self.din("mask_s", [32, 32])
        self.onehot_d = self.din("onehot", [32, NBS])
        self.cfix_d = self.din("cfix", [128, 4, 16])
        self.yT = self.dout("yT", [NSB, 128, KC, TS])
        self.npp = self.dout("npp", [128, KC, 15])
        self.ngp = self.dout("ngp", [4, 128, 256])
        self.nps = self.dout("nps", [NSB, 128, 4, 2, NBS, 15])
        self.ngs = self.dout("ngs", [NSB, 4, 128, NBS, 256])
        with ExitStack() as es:
            self.fw = fw = FW(nc, es)
            T = lambda name, shape, dt: self.T(es, name, shape, dt)
            self.h = T("h", [128, KC, TS], F32)
            self.xn = T("xn", [128, KC, TS], BF16)
            self.S = T("S", [128, 4, 256], F32)
            self.Sb = T("Sb", [128, 4, 256], BF16)
            self.ucarry = T("ucarry", [128, 4, 2, 16], F32)
            self.vecs = T("vecs", [128, NVEC], F32)
            self.ident = T("ident", [128, 128], BF16)
            self.ones_d = T("ones_d", [128, 128], BF16)
            self.ones_v = T("ones_v", [128, 128], BF16)
            self.Umat = T("Umat", [128, 128], F32)
            self.Umat_s = T("Umat_s", [32, 32], F32)
            self.mask4 = T("mask4", [128, 4, 128], F32)
            self.mask_s = T("mask_s", [32, 32], F32)
            self.onehot = T("onehot", [32, NBS], F32)
            self.cfix = T("cfix", [128, 4, 16], F32)
            self.ps = [es.enter_context(nc.psum_tensor("ps%d" % i, [128, 512], F32)) for i in range(7)]
            self.psbf = es.enter_context(nc.psum_tensor("psbf", [128, 1024], BF16))
            self.ps_rot = list(range(7))
            self.ps_i = 0
            for i, (t, d) in enumerate([(self.vecs, self.vecs_d), (self.ident, self.ident_d), (self.ones_d, self.ones_d_d),
                                        (self.ones_v, self.ones_v_d), (self.Umat, self.Umat_d), (self.Umat_s, self.Umat_s_d),
                                        (self.mask4, self.mask4_d), (self.mask_s, self.mask_s_d), (self.onehot, self.onehot_d),
                                        (self.cfix, self.cfix_d)]):
                fw.add('sp', lambda e, t=t, d=d: e.dma_start(out=t[:], in_=d), writes=[('const', i)], dma='const')
            fw.add('dve', lambda e: e.memset(self.S[:, :, :], 0.0), writes=KS('S', range(4)))
            fw.add('dve', lambda e: e.memset(self.Sb[:, :, :], 0.0), writes=KS('Sb', range(4)))
            fw.add('dve', lambda e: e.memset(self.ucarry[:, :, :, :], 0.0), writes=KS('ucarry', range(4)))
            fw.flush()
            self.mw = [None, None]
            with ExitStack() as esw:
                self.alloc_mw(esw, 0)
                self.alloc_mw(esw, 1)
                self.phase_A(0, prefetch=lambda: (self.load_head(0), self.load_head(1)))
                self.phase_mixer(0, pre=(0, 1))
            self.mw = [None, None]
            with ExitStack() as esw:
                self.alloc_mw(esw, 0)
                self.phase_C(0, next_A=(1, lambda: self.load_head(0)))
                self.phase_mixer(1, pre=(0,))
            self.mw = [None, None]
            self.phase_C(1)


def _fm_units(W, c0, width=128):
    return np.ascontiguousarray(W[:, c0:c0 + width].reshape(KC, 128, width).transpose(1, 0, 2))


def _consts():
    bf = ml_dtypes.bfloat16
    j = np.arange(128)[:, None]
    i = np.arange(128)[None, :]
    causal = (j <= i)
    Umat = np.where(causal, -1.0 / 16.0, 0.0).astype(np.float32)
    mask4 = np.repeat(causal.astype(np.float32)[:, None, :], 4, axis=1)
    js = np.arange(32)[:, None]
    is_ = np.arange(32)[None, :]
    cs = (js <= is_) & ((js // LS) == (is_ // LS))
    Umat_s = np.where(cs, -1.0 / 16.0, 0.0).astype(np.float32)
    mask_s = cs.astype(np.float32)
    onehot = ((np.arange(32)[:, None] // LS) == np.arange(NBS)[None, :]).astype(np.float32)
    cfix = np.zeros((128, 4, 16), np.float32)
    for g, w in enumerate(POOL_W):
        t = np.arange(16)
        cfix[:, g, :] = (w / np.minimum(w, t + 1))[None, :]
    return dict(
        ident=np.eye(128, dtype=np.float32).astype(bf),
        ones_d=np.full((128, 128), 1.0 / D, np.float32).astype(bf),
        ones_v=np.full((128, 128), 1.0 / 256, np.float32).astype(bf),
        Umat=Umat, Umat_s=Umat_s, mask4=np.ascontiguousarray(mask4), mask_s=mask_s, onehot=onehot, cfix=cfix)


def _prep_shared(inp):
    f = lambda a: np.asarray(a, dtype=np.float32)
    out = {}
    for nm, key in (("w1", "ffn1"), ("w2", "ffn2")):
        Wg, Wu, Wd = f(inp[key + "_w_gate"])[0], f(inp[key + "_w_up"])[0], f(inp[key + "_w_down"])[0]
        out[nm + "g"] = np.ascontiguousarray(Wg.reshape(KC, 128, FC, 128).transpose(2, 1, 0, 3))
        out[nm + "u"] = np.ascontiguousarray(Wu.reshape(KC, 128, FC, 128).transpose(2, 1, 0, 3))
        out[nm + "d"] = np.ascontiguousarray(Wd.reshape(FC, 128, KC, 128).transpose(2, 1, 0, 3))
    Win = f(inp["w_in"])[0]
    OU, OQ, OK_, OV, OGK, OG, OGA, OGB = 0, 1024, 1536, 2048, 3072, 3088, 4112, 5136
    fm = np.zeros((4, 10, 128, KC, 128), np.float32)
    wv = np.zeros((4, 128, KC, 256), np.float32)
    for hd in range(4):
        cols = [OU + hd * 256, OU + hd * 256 + 128, OQ + hd * 128, OK_ + hd * 128,
                OG + hd * 256, OG + hd * 256 + 128, OGA + hd * 256, OGA + hd * 256 + 128,
                OGB + hd * 256, OGB + hd * 256 + 128]
        for u, c0 in enumerate(cols):
            fm[hd, u] = _fm_units(Win, c0)
        wv[hd] = _fm_units(Win, OV + hd * 256, 256)
    out["win_fm"] = fm
    out["win_v"] = wv
    out["win_gk"] = _fm_units(Win, OGK, 16)
    out["wout"] = np.ascontiguousarray(f(inp["w_out"])[0].reshape(4, 2, 128, 1024).transpose(0, 2, 1, 3))
    out["poolw"] = np.ascontiguousarray(f(inp["pool_w"])[0].reshape(4, 2, 128, 256).transpose(0, 2, 1, 3))
    wgk = np.zeros((32, 512), np.float32)
    wgk[0:16] = f(inp["w_gk_up"])[0]
    wgk[16] = f(inp["b_gk"])[0]
    out["wgk"] = wgk
    out["wpg"] = np.ascontiguousarray(f(inp["w_ple_gate"])[0].reshape(KC, 128, KC, 128).transpose(2, 1, 0, 3))
    out["wpp"] = np.ascontiguousarray(f(inp["w_ple_proj"])[0].reshape(2, 128, KC, 128).transpose(2, 1, 0, 3))
    vecs = np.zeros((128, NVEC), np.float32)
    for col, v in ((V_FFN1, inp["ffn1_norm"][0]), (V_MIX, inp["mix_norm"][0]), (V_FFN2, inp["ffn2_norm"][0]),
                   (V_PLE, inp["ple_norm"][0]), (V_FIN, inp["final_norm"]), (V_PSC, inp["pool_scale"][0])):
        vecs[:, col:col + 8] = f(v).reshape(KC, 128).T
    vecs[:, V_GLA:V_GLA + 2] = f(inp["gla_norm"][0]).reshape(2, 128).T
    out["vecs"] = vecs
    out.update(_consts())
    return out


def _prep_core(inp, c):
    f = lambda a: np.asarray(a, dtype=np.float32)
    xp, xs = f(inp["x_prompt"]), f(inp["x_sample"])
    pp, psm = f(inp["p_prompt"])[0], f(inp["p_sample"])[0]
    sp, sg = f(inp["state_pool"])[0], f(inp["state_gla"])[0]
    xT = np.zeros((NSB, 128, KC, TS), np.float32)
    pT = np.zeros((NSB, 128, 2, TS), np.float32)
    spT = np.zeros((NSB, 128, 4, 2, NBS, 16), np.float32)
    sgla = np.zeros((NSB, 4, 128, NBS, 256), np.float32)
    for sb in range(NSB):
        b0 = c * 16 + sb * NBS
        tok = np.concatenate([xp[c, sb * TP:(sb + 1) * TP], xs[b0:b0 + NBS].reshape(TSM, D)], axis=0)
        xT[sb] = tok.T.reshape(KC, 128, TS).transpose(1, 0, 2)
        ptok = np.concatenate([pp[c, sb * TP:(sb + 1) * TP], psm[b0:b0 + NBS].reshape(TSM, PLE)], axis=0)
        pT[sb] = ptok.T.reshape(2, 128, TS).transpose(1, 0, 2)
        st = sp[b0:b0 + NBS]
        st = st.transpose(2, 0, 1).reshape(4, 2, 128, NBS, 15)
        spT[sb, :, :, :, :, 1:16] = st.transpose(2, 0, 1, 3, 4)
        sgla[sb] = sg[b0:b0 + NBS].transpose(1, 2, 0, 3)
    return dict(xT=xT, pT=pT, spT=spT, sgla=sgla)


_PROG = {}


def _get_prog(stop_after=None):
    if stop_after not in _PROG:
        _PROG[stop_after] = Prog(stop_after)
    return _PROG[stop_after]


def kernel(**inputs):
    stop_after = inputs.pop("_stop_after", None)
    prog = _get_prog(stop_after)
    shared = _prep_shared(inputs)
    in_maps = []
    for c in range(NCORES):
        m = dict(shared)
        m.update(_prep_core(inputs, c))
        in_maps.append(m)
    res = run_bass_kernel_spmd(prog.nc, in_maps, core_ids=list(range(NCORES)))
    R = res.results
    B, SEQ = 8, 2048
    y_prompt = np.zeros((B, SEQ, D), np.float32)
    y_sample = np.zeros((128, LS, D), np.float32)
    npool_p = np.zeros((1, B, 15, D), np.float32)
    ngla_p = np.zeros((1, B, 4, 128, 256), np.float32)
    npool_s = np.zeros((1, 128, 15, D), np.float32)
    ngla_s = np.zeros((1, 128, 4, 128, 256), np.float32)
    for c in range(NCORES):
        r = R[c]
        yT = np.asarray(r["yT"], dtype=np.float32)
        for sb in range(NSB):
            tok = yT[sb].transpose(1, 0, 2).reshape(D, TS).T
            y_prompt[c, sb * TP:(sb + 1) * TP] = tok[:TP]
            b0 = c * 16 + sb * NBS
            y_sample[b0:b0 + NBS] = tok[TP:].reshape(NBS, LS, D)
            nps = np.asarray(r["nps"], dtype=np.float32)[sb]
            npool_s[0, b0:b0 + NBS] = nps.transpose(3, 4, 1, 2, 0).reshape(NBS, 15, D)
            ngs = np.asarray(r["ngs"], dtype=np.float32)[sb]
            ngla_s[0, b0:b0 + NBS] = ngs.transpose(2, 0, 1, 3)
        npp = np.asarray(r["npp"], dtype=np.float32)
        npool_p[0, c] = npp.transpose(2, 1, 0).reshape(15, D)
        ngla_p[0, c] = np.asarray(r["ngp"], dtype=np.float32)
    return (y_prompt, y_sample, npool_p, ngla_p, npool_s, ngla_s)
```

```python
import numpy as np
import ml_dtypes
from contextlib import ExitStack
import concourse.bass as bass
import concourse.mybir as mybir
from concourse.bass_utils import run_bass_kernel_spmd

F32 = mybir.dt.float32
BF16 = mybir.dt.bfloat16
AF = mybir.ActivationFunctionType
ALU = mybir.AluOpType

NCORES = 8
D = 1024
KC = 8
FF = 2816
FC = 22
PLE = 256
NSB = 2
TP = 1024
NBS = 8
LS = 4
TSM = NBS * LS
TS = TP + TSM
EPS = 1e-6
POOL_W = (2, 4, 8, 16)
QSCALE = 128 ** -0.5
MB = 512
NSTG = 3
BLK_D = [(0, 352), (352, 352), (704, 352)]

V_FFN1, V_MIX, V_FFN2, V_PLE, V_FIN, V_PSC, V_GLA = 0, 8, 16, 24, 32, 40, 48
NVEC = 50


class _Probe:
    def __init__(self):
        self.n = 0

    def __getattr__(self, name):
        def m(*a, **k):
            out = k.get('out', a[0] if a else None)
            try:
                fs = 1
                for d in out.shape[1:]:
                    fs *= d
                self.n = max(self.n, fs)
            except Exception:
                pass
            return self
        return m


class FW:
    ENG = ('pe', 'act', 'dve', 'pool', 'sp')

    def __init__(self, nc, es):
        self.nc = nc
        self.es = es
        self.esem = {e: es.enter_context(nc.semaphore("s_" + e)) for e in self.ENG if e != 'sp'}
        self.ecount = {e: 0 for e in self.esem}
        self.dsem = {}
        self.dcount = {}
        self.ops = []
        self.last_writer = {}
        self.readers = {}
        self.seen = {e: {} for e in self.ENG}
        self.same_engine_sync = True
        self.capture = None
        self.do_schedule = True
        self.sched_window = 73

    def merged(self, builders):
        lists = []
        for b in builders:
            self.capture = []
            b()
            lists.append(self.capture)
        self.capture = None
        n = [len(l) for l in lists]
        idx = [0] * len(lists)
        while True:
            cand = [i for i in range(len(lists)) if idx[i] < n[i]]
            if not cand:
                break
            j = min(cand, key=lambda i: (idx[i] + 1) / n[i])
            self.add(*lists[j][idx[j]])
            idx[j] += 1

    DEFCOST = {'pe': 0.25, 'act': 0.6, 'dve': 0.7, 'pool': 0.2, 'sp': 0.1}

    def add(self, engine, fn, reads=(), writes=(), dma=None, cost=None, tbl=None):
        if self.capture is not None:
            self.capture.append((engine, fn, list(reads), list(writes), dma, cost, tbl))
            return None
        if cost is None and dma is None and engine in ('act', 'dve', 'pool'):
            pr = _Probe()
            try:
                fn(pr)
            except Exception:
                pr.n = 0
            if pr.n > 0:
                if engine == 'act':
                    cost = 0.22 + pr.n / 1500.0
                elif engine == 'dve':
                    cost = 0.08 + pr.n / 850.0
                else:
                    cost = 0.3 + pr.n / 350.0
        op = dict(engine=engine, fn=fn, dma=dma, deps=[], alldeps=[], needs_inc=False, val=None, tbl=tbl,
                  cost=(cost if cost is not None else self.DEFCOST[engine]))
        deps = []
        for k in reads:
            w = self.last_writer.get(k)
            if w is not None:
                deps.append(w)
        for k in writes:
            w = self.last_writer.get(k)
            if w is not None:
                deps.append(w)
            deps.extend(self.readers.get(k, ()))
        for d in deps:
            if d is op:
                continue
            op['alldeps'].append(d)
            if d['dma'] is None:
                if d['engine'] == engine and (engine == 'pe' or not self.same_engine_sync):
                    continue
                d['needs_inc'] = True
            op['deps'].append(d)
        for k in writes:
            self.last_writer[k] = op
            self.readers[k] = []
        for k in reads:
            self.readers.setdefault(k, []).append(op)
        if dma is not None and dma not in self.dsem:
            self.dsem[dma] = self.es.enter_context(self.nc.semaphore("d_" + dma))
            self.dcount[dma] = 0
        self.ops.append(op)
        return op

    def schedule(self, ops):
        import bisect
        n = len(ops)
        idx = {id(op): i for i, op in enumerate(ops)}
        dep_idx = []
        succ = [[] for _ in range(n)]
        indeg = [0] * n
        for i, op in enumerate(ops):
            ds = sorted(set(idx[id(d)] for d in op['alldeps'] if id(d) in idx))
            dep_idx.append(ds)
            indeg[i] = len(ds)
            for d in ds:
                succ[d].append(i)
        fin = [0.0] * n
        efree = {e: 0.0 for e in self.ENG}
        ready = [i for i in range(n) if indeg[i] == 0]
        order = []
        LAT = 0.3
        cur_tbl = None
        while ready:
            best = None
            lim = ready[0] + self.sched_window
            for i in ready:
                if i > lim:
                    break
                op = ops[i]
                e = op['engine']
                t = efree[e]
                for d in dep_idx[i]:
                    td = fin[d] + (LAT if ops[d]['engine'] != e else 0.05)
                    if td > t:
                        t = td
                if op['tbl'] is not None and op['tbl'] != cur_tbl:
                    t += 0.5
                tk = t - (0.2 if e == 'pe' else 0.0)
                if best is None or tk < best[2]:
                    best = (t, i, tk)
            t, i = best[0], best[1]
            ready.remove(i)
            op = ops[i]
            e = op['engine']
            if op['tbl'] is not None:
                cur_tbl = op['tbl']
            if op['dma'] is not None:
                efree[e] = t + 0.15
                fin[i] = t + 2.0 + op['cost']
            else:
                efree[e] = t + op['cost']
                fin[i] = t + op['cost']
            order.append(op)
            for sidx in succ[i]:
                indeg[sidx] -= 1
                if indeg[sidx] == 0:
                    bisect.insort(ready, sidx)
        assert len(order) == n
        self.sim_time = max(fin) if fin else 0.0
        return order

    def flush(self):
        ops = self.ops
        self.ops = []
        if self.do_schedule:
            ops = self.schedule(ops)
        last = {}
        for op in ops:
            if op['dma'] is None:
                last[op['engine']] = op
        for op in last.values():
            op['needs_inc'] = True
        for op in ops:
            if op['dma'] is not None:
                self.dcount[op['dma']] += 16
                op['val'] = self.dcount[op['dma']]
            elif op['needs_inc']:
                self.ecount[op['engine']] += 1
                op['val'] = self.ecount[op['engine']]
        nwait = 0
        for op in ops:
            e = op['engine']
            seen = self.seen[e]
            need = {}
            for d in op['deps']:
                key = ('d', d['dma']) if d['dma'] is not None else ('e', d['engine'])
                if need.get(key, (0, None))[0] < d['val']:
                    need[key] = (d['val'], d)
            waits = []
            for key, (v, d) in sorted(need.items(), key=lambda kv: -kv[1][0]):
                if seen.get(key, 0) >= v:
                    continue
                waits.append((key, v))
                seen[key] = v
                for k2, v2 in d.get('vc', {}).items():
                    if seen.get(k2, 0) < v2:
                        seen[k2] = v2
            op['waits'] = waits
            nwait += len(waits)
            vc = dict(seen)
            if op['dma'] is not None:
                vc[('d', op['dma'])] = max(vc.get(('d', op['dma']), 0), op['val'])
            elif op['val'] is not None:
                vc[('e', e)] = max(vc.get(('e', e), 0), op['val'])
            op['vc'] = vc
        self.n_waits = getattr(self, 'n_waits', 0) + nwait
        per = {e: [] for e in self.ENG}
        for op in ops:
            per[op['engine']].append(op)
        fw = self

        def emit(e, eng):
            seen = fw.seen[e]
            for op in per[e]:
                for key, v in op['waits']:
                    sem = fw.dsem[key[1]] if key[0] == 'd' else fw.esem[key[1]]
                    eng.wait_ge(sem, v)
                ins = op['fn'](eng)
                if op['dma'] is not None:
                    ins.then_inc(fw.dsem[op['dma']], 16)
                elif op['needs_inc']:
                    ins.then_inc(fw.esem[e], 1)
            for f in fw.esem:
                if f == e:
                    continue
                v = fw.ecount[f]
                if seen.get(('e', f), 0) < v:
                    seen[('e', f)] = v
                    eng.wait_ge(fw.esem[f], v)
            for dn, v in fw.dcount.items():
                if seen.get(('d', dn), 0) < v:
                    seen[('d', dn)] = v
                    eng.wait_ge(fw.dsem[dn], v)

        with self.nc.Block() as block:
            @block.tensor
            def _(eng):
                emit('pe', eng)

            @block.scalar
            def _(eng):
                emit('act', eng)

            @block.vector
            def _(eng):
                emit('dve', eng)

            @block.gpsimd
            def _(eng):
                emit('pool', eng)

            @block.sync
            def _(eng):
                emit('sp', eng)
        self.last_writer = {}
        self.readers = {}


def KS(name, *dims):
    out = [(name,)]
    for d in dims:
        if isinstance(d, int):
            d = [d]
        out = [o + (i,) for o in out for i in d]
    return out


class Prog:
    def __init__(self, stop_after=None):
        self.stop_after = stop_after
        self.nc = bass.Bass("TRN2", target_bir_lowering=False)
        self.build()

    def din(self, name, shape, dt=F32):
        return self.nc.dram_tensor(name, list(shape), dt, kind="ExternalInput").ap()

    def dout(self, name, shape, dt=F32):
        return self.nc.dram_tensor(name, list(shape), dt, kind="ExternalOutput").ap()

    def T(self, es, name, shape, dt):
        self.uid = getattr(self, "uid", 0) + 1
        return es.enter_context(self.nc.sbuf_tensor("%s_%d" % (name, self.uid), list(shape), dt))

    def next_ps(self):
        i = self.ps_rot[self.ps_i % len(self.ps_rot)]
        self.ps_i += 1
        return self.ps[i], ('ps', i)

    def load_cast(self, dst, base, src, n, parts=128):
        name = "_".join(str(x) for x in base)
        self.fw.add('pool', lambda e: e.dma_start(out=dst, in_=src), writes=[base], dma=name, cost=parts * n * 4 / 200e3)
        return [base]

    def mm_group(self, out, pairs, reads, writes):
        n = len(pairs)

        def fn(e):
            ins = None
            for i, (l, r) in enumerate(pairs):
                ins = e.matmul(out, lhsT=l, rhs=r, start=(i == 0), stop=(i == n - 1))
            return ins
        cost = 0.0
        for (l, r) in pairs:
            fs = 1
            for d in r.shape[1:]:
                fs *= d
            c = max(fs, 64) / 2400.0 + 0.004
            if r.dtype == F32:
                c *= 4
            cost += c
        self.fw.add('pe', fn, reads=reads, writes=writes, cost=cost)

    def rms_stats(self, src3, size, ones, nk, rstd, src_keys, tag, sq=None, sqkey=('sq',)):
        fw = self.fw
        if sq is None:
            sq = self.sq
        fw.add('act', lambda e: e.activation(out=sq[:, 0:nk, 0:size], in_=src3, func=AF.Square),
               reads=src_keys, writes=[sqkey], cost=0.2 + nk * size / 1200.0)
        ps, pk = self.next_ps()
        self.mm_group(ps[:, 0:size], [(ones[:], sq[:, k, 0:size]) for k in range(nk)],
                      reads=[sqkey, ('const',)], writes=[pk])
        fw.add('act', lambda e: e.activation(out=rstd[:, 0:size], in_=ps[:, 0:size], func=AF.Ln, bias=EPS),
               reads=[pk], writes=[('rstd', tag)], tbl='el')
        fw.add('act', lambda e: e.activation(out=rstd[:, 0:size], in_=rstd[:, 0:size], func=AF.Exp, scale=-0.5),
               reads=[('rstd', tag)], writes=[('rstd', tag)], tbl='el')

    def norm_ops(self, vcol, blocks, nsq, nrs, sub=512, out=None, okeys=None):
        fw = self.fw
        h, xn, vecs = self.h, self.xn, self.vecs
        cnt = 0
        for bi, (st, sz) in enumerate(blocks):
            for s0 in range(0, sz, sub):
                ssz = min(sub, sz - s0)
                a0 = st + s0
                t = cnt % 2
                cnt += 1
                rstd = nrs[t]
                self.rms_stats(h[:, :, a0:a0 + ssz], ssz, self.ones_d, KC, rstd, KS('h', range(KC), bi), ('n', t),
                               sq=nsq, sqkey=('nsq',))
                for k in range(KC):
                    fw.add('dve', lambda e, k=k, a0=a0, ssz=ssz, rstd=rstd: e.scalar_tensor_tensor(
                        out=xn[:, k, a0:a0 + ssz], in0=h[:, k, a0:a0 + ssz], scalar=vecs[:, vcol + k:vcol + k + 1],
                        in1=rstd[:, 0:ssz], op0=ALU.mult, op1=ALU.mult),
                        reads=[('h', k, bi), ('rstd', ('n', t)), ('const',)], writes=[('xn', k, bi)],
                        cost=0.1 + ssz / 900.0)

    def ffn_core(self, which, hid, wgu, wdt, sgt, between=None):
        fw = self.fw
        wg_d, wu_d, wd_d = (self.w1g, self.w1u, self.w1d) if which == 1 else (self.w2g, self.w2u, self.w2d)
        h, xn = self.h, self.xn

        def load_gu(j):
            s = j % 2
            kg = self.load_cast(wgu[:, s, 0, :], ('wg', s), wg_d[j].rearrange("p k n -> p (k n)"), 1024)
            ku = self.load_cast(wgu[:, s, 1, :], ('wu', s), wu_d[j].rearrange("p k n -> p (k n)"), 1024)
            return kg, ku

        def load_d(c):
            s = c % 2
            return self.load_cast(wdt[:, s, :], ('wd', s), wd_d[c].rearrange("p j n -> p (j n)"), FC * 128)

        nxt = load_gu(0)
        cnt = 0
        for j in range(FC):
            kg, ku = nxt
            if j + 1 < FC:
                nxt = load_gu(j + 1)
            s = j % 2
            for bi, (st, sz) in enumerate(BLK_D):
                xk = KS('xn', range(KC), bi)
                psg, pkg = self.next_ps()
                self.mm_group(psg[:, 0:sz], [(wgu[:, s, 0, k * 128:(k + 1) * 128], xn[:, k, st:st + sz]) for k in range(KC)],
                              reads=kg + xk, writes=[pkg])
                psu, pku = self.next_ps()
                self.mm_group(psu[:, 0:sz], [(wgu[:, s, 1, k * 128:(k + 1) * 128], xn[:, k, st:st + sz]) for k in range(KC)],
                              reads=ku + xk, writes=[pku])
                ss = cnt % 2
                cnt += 1
                fw.add('act', lambda e, psg=psg, sz=sz, ss=ss: e.activation(out=sgt[:, ss, 0:sz], in_=psg[:, 0:sz], func=AF.Silu),
                       reads=[pkg], writes=[('sgt', ss)], cost=0.2 + sz / 1200.0, tbl='st')
                fw.add('dve', lambda e, psu=psu, sz=sz, ss=ss, j=j, st=st: e.tensor_tensor(
                    out=hid[:, j, st:st + sz], in0=psu[:, 0:sz], in1=sgt[:, ss, 0:sz], op=ALU.mult),
                    reads=[pku, ('sgt', ss)], writes=[('hid', j, bi)], cost=0.1 + sz / 900.0)
        nxt = load_d(0)
        if between is not None:
            between()
        for c in range(KC):
            kd = nxt
            if c + 1 < KC:
                nxt = load_d(c + 1)
            s = c % 2
            for bi, (st, sz) in enumerate(BLK_D):
                ps, pk = self.next_ps()
                self.mm_group(ps[:, 0:sz], [(wdt[:, s, j * 128:(j + 1) * 128], hid[:, j, st:st + sz]) for j in range(FC)],
                              reads=kd + KS('hid', range(FC), bi), writes=[pk])
                fw.add('dve', lambda e, ps=ps, sz=sz, c=c, st=st: e.scalar_tensor_tensor(
                    out=h[:, c, st:st + sz], in0=ps[:, 0:sz], scalar=0.5, in1=h[:, c, st:st + sz],
                    op0=ALU.mult, op1=ALU.add),
                    reads=[pk, ('h', c, bi)], writes=[('h', c, bi)], cost=0.1 + sz / 900.0)

    def ffn_tiles(self, es):
        hid = self.T(es, "hid", [128, FC, TS], BF16)
        wgu = self.T(es, "wgu", [128, 2, 2, 1024], BF16)
        wdt = self.T(es, "wdt", [128, 2, FC * 128], BF16)
        sgt = self.T(es, "sgt", [128, 2, 512], F32)
        nsq = self.T(es, "nsq", [128, KC, 512], BF16)
        nrs = [self.T(es, "nrs%d" % i, [128, 512], F32) for i in range(2)]
        return hid, wgu, wdt, sgt, nsq, nrs

    def phase_A(self, sb, prefetch):
        fw = self.fw
        h = self.h
        with ExitStack() as es:
            hid, wgu, wdt, sgt, nsq, nrs = self.ffn_tiles(es)
            self.ps_rot = list(range(7))
            self.A_ops(sb, hid, wgu, wdt, sgt, nsq, nrs, prefetch)
            fw.flush()

    def A_ops(self, sb, hid, wgu, wdt, sgt, nsq, nrs, prefetch):
        fw = self.fw
        h = self.h
        for bi, (st, sz) in enumerate(BLK_D):
            fw.add('sp', lambda e, st=st, sz=sz: e.dma_start(out=h[:, :, st:st + sz], in_=self.xT[sb, :, :, st:st + sz]),
                   writes=KS('h', range(KC), bi), dma='hload%d' % bi, cost=8.0)
        self.norm_ops(V_FFN1, BLK_D, nsq, nrs)
        self.ffn_core(1, hid, wgu, wdt, sgt, between=prefetch)

    def phase_C(self, sb, next_A=None):
        fw = self.fw
        h, xn, vecs = self.h, self.xn, self.vecs
        with ExitStack() as es:
            hid, wgu, wdt, sgt, nsq, nrs = self.ffn_tiles(es)
            pb = self.T(es, "pb", [128, 2, TS], BF16)
            wpg = self.T(es, "wpg", [128, 2, 1024], BF16)
            wpp = self.T(es, "wpp", [128, 2, 256], BF16)
            gt = self.T(es, "gt", [128, 2, 512], F32)
            nyt = 1 if next_A is not None else 2
            yt = self.T(es, "yt", [128, nyt, KC, 352], F32)
            self.ps_rot = list(range(7))
            self.norm_ops(V_FFN2, BLK_D, nsq, nrs)
            kp = []
            for k in range(2):
                kp += self.load_cast(pb[:, k, :], ('pb', k), self.pT[sb, :, k, :], TS)
            self.ffn_core(2, hid, wgu, wdt, sgt)
            self.norm_ops(V_PLE, BLK_D, nsq, nrs)

            def load_w(c):
                s = c % 2
                k1 = self.load_cast(wpg[:, s, :], ('wpg', s), self.wpg_d[c].rearrange("p k n -> p (k n)"), 1024)
                k2 = self.load_cast(wpp[:, s, :], ('wpp', s), self.wpp_d[c].rearrange("p k n -> p (k n)"), 256)
                return k1, k2
            nxt = load_w(0)
            cnt = 0
            for c in range(KC):
                k1, k2 = nxt
                if c + 1 < KC:
                    nxt = load_w(c + 1)
                s = c % 2
                for bi, (st, sz) in enumerate(BLK_D):
                    psg, pkg = self.next_ps()
                    self.mm_group(psg[:, 0:sz], [(wpg[:, s, k * 128:(k + 1) * 128], xn[:, k, st:st + sz]) for k in range(KC)],
                                  reads=k1 + KS('xn', range(KC), bi), writes=[pkg])
                    psp, pkp = self.next_ps()
                    self.mm_group(psp[:, 0:sz], [(wpp[:, s, k * 128:(k + 1) * 128], pb[:, k, st:st + sz]) for k in range(2)],
                                  reads=k2 + kp, writes=[pkp])
                    ss = cnt % 2
                    cnt += 1
                    fw.add('act', lambda e, psg=psg, sz=sz, ss=ss: e.activation(out=gt[:, ss, 0:sz], in_=psg[:, 0:sz], func=AF.Sigmoid),
                           reads=[pkg], writes=[('gt', ss)], tbl='sg')
                    fw.add('dve', lambda e, psp=psp, sz=sz, ss=ss: e.tensor_tensor(
                        out=gt[:, ss, 0:sz], in0=psp[:, 0:sz], in1=gt[:, ss, 0:sz], op=ALU.mult),
                        reads=[pkp, ('gt', ss)], writes=[('gt', ss)])
                    fw.add('dve', lambda e, sz=sz, ss=ss, c=c, st=st: e.tensor_tensor(
                        out=h[:, c, st:st + sz], in0=h[:, c, st:st + sz], in1=gt[:, ss, 0:sz], op=ALU.add),
                        reads=[('gt', ss), ('h', c, bi)], writes=[('h', c, bi)])
            for bi, (st, sz) in enumerate(BLK_D):
                t = bi % 2
                rstd = nrs[t]
                self.rms_stats(h[:, :, st:st + sz], sz, self.ones_d, KC, rstd, KS('h', range(KC), bi), ('n', t),
                               sq=nsq, sqkey=('nsq',))
                t = bi % nyt
                for k in range(KC):
                    fw.add('dve', lambda e, k=k, st=st, sz=sz, rstd=rstd, t=t: e.scalar_tensor_tensor(
                        out=yt[:, t, k, 0:sz], in0=h[:, k, st:st + sz], scalar=vecs[:, V_FIN + k:V_FIN + k + 1],
                        in1=rstd[:, 0:sz], op0=ALU.mult, op1=ALU.mult),
                        reads=[('h', k, bi), ('rstd', ('n', bi % 2)), ('const',)], writes=[('yt', t, k)])
                fw.add('sp', lambda e, st=st, sz=sz, t=t: e.dma_start(out=self.yT[sb, :, :, st:st + sz], in_=yt[:, t, :, 0:sz]),
                       reads=KS('yt', t, range(KC)), dma='yout%d' % t, cost=6.0)
            if next_A is not None:
                nsb, prefetch = next_A
                self.A_ops(nsb, hid, wgu, wdt, sgt, nsq, nrs, prefetch)
            fw.flush()

    def raw_out(self, sb):
        fw = self.fw
        for k in range(KC):
            fw.add('sp', lambda e, k=k: e.dma_start(out=self.yT[sb, :, k, :], in_=self.h[:, k, :]),
                   reads=KS('h', k, range(len(BLK_D))), dma='yout%d' % (k % 2))
        fw.flush()

    def phase_mixer(self, sb, pre=(0, 1)):
        fw = self.fw
        h, xn, vecs = self.h, self.xn, self.vecs
        pblocks = [(i * MB, MB) for i in range(TP // MB)]
        ablocks = pblocks + [(TP, TSM)]
        nab = len(ablocks)
        npb = len(pblocks)
        with ExitStack() as es:
            T = lambda name, shape, dt: self.T(es, name, shape, dt)
            nsq = T("nsq", [128, KC, 256], BF16)
            nrs = [T("nrs%d" % i, [128, 256], F32) for i in range(2)]
            self.ps_rot = [3, 4]
            self.norm_ops(V_MIX, ablocks, nsq, nrs, sub=256)
            gkw = T("gkw", [128, 128], BF16)
            gklr = T("gklr", [32, TS], BF16)
            wgk = T("wgk", [32, 512], BF16)
            ML = MB + 16
            uext = T("uext", [128, 2, 1, ML], F32)
            wA = T("wA", [128, 2, 1, ML], F32)
            wB = T("wB", [128, 2, 1, ML], F32)
            sgt = T("sgt", [128, 2, MB], F32)
            aout = T("aout", [128, 2, MB], F32)
            pooled = T("pooled", [128, 2, MB], BF16)
            spt = T("spt", [128, MB], F32)
            Et = T("Et", [128, MB], F32)
            Ei = T("Ei", [128, MB], F32)
            qt = T("qt", [128, MB], BF16)
            kt = T("kt", [128, MB], BF16)
            kTM = T("kTM", [128, MB // 128, 128], BF16)
            vTM = T("vTM", [128, MB // 128, 256], BF16)
            ATm = T("ATm", [128, MB // 128, 128], BF16)
            osb = T("osb", [128, 2, MB], F32)
            self.sq = T("osq", [128, 2, MB], BF16)
            rso = T("rso", [128, MB], F32)
            siga = T("siga", [128, 2, MB], F32)
            sigb = T("sigb", [128, 2, MB], F32)
            mix = T("mix", [128, 2, MB], BF16)
            uext_s = T("uext_s", [128, 2, NBS, LS + 16], F32)
            wA_s = T("wA_s", [128, 2, NBS, LS + 16], F32)
            wB_s = T("wB_s", [128, 2, NBS, LS + 16], F32)
            ustg = T("ustg", [128, 2, NBS, 16], F32)
            S0 = T("S0", [128, NBS, 256], F32)
            S0b = T("S0b", [128, NBS, 256], BF16)
            kmask = T("kmask", [32, NBS, 128], BF16)
            ps = self.ps
            psbf = self.psbf
            self.dense = [0, 1, 2]
            self.di = 0

            def nd():
                i = self.dense[self.di % len(self.dense)]
                self.di += 1
                return ps[i], ('ps', i)
            self.ps_rot = [3, 4]
            self.ps_i = 0
            nsm = self.next_ps
            po = [ps[5], ps[6]]
            pok = [('ps', 5), ('ps', 6)]
            S, Sb = self.S, self.Sb

            kgkw = self.load_cast(gkw[:, :], ('gkw',), self.win_gk.rearrange("p k n -> p (k n)"), 128)
            kwgk = self.load_cast(wgk[:, :], ('wgk',), self.wgk_d[:, :], 512)
            fw.add('dve', lambda e: e.memset(gklr[:, :], 1.0), writes=KS('gklr', range(nab)))
            for bi, (st, sz) in enumerate(ablocks):
                p_, pk = nd()
                self.mm_group(p_[0:16, 0:sz], [(gkw[:, k * 16:(k + 1) * 16], xn[:, k, st:st + sz]) for k in range(KC)],
                              reads=kgkw + KS('xn', range(KC), bi), writes=[pk])
                fw.add('act', lambda e, p_=p_, st=st, sz=sz: e.activation(out=gklr[0:16, st:st + sz], in_=p_[0:16, 0:sz], func=AF.Copy),
                       reads=[pk], writes=[('gklr', bi)])

            load_head = self.load_head

            def proj_fm(hd, wk, u, bi, st, sz):
                s = hd % 2
                p_, pk = nd()
                self.mm_group(p_[:, 0:sz], [(self.mw[s][0][:, u, k * 128:(k + 1) * 128], xn[:, k, st:st + sz]) for k in range(KC)],
                              reads=wk[u] + KS('xn', range(KC), bi), writes=[pk])
                return p_, pk

            def softplus_gk(p_, pk, rows, sz):
                fw.add('act', lambda e: e.activation(out=spt[0:rows, 0:sz], in_=p_[0:rows, 0:sz], func=AF.Exp, scale=-1.0),
                       reads=[pk], writes=[('spt',)], tbl='el')
                fw.add('act', lambda e: e.activation(out=spt[0:rows, 0:sz], in_=spt[0:rows, 0:sz], func=AF.Ln, bias=1.0),
                       reads=[('spt',)], writes=[('spt',)], tbl='el')

            def u_proj(it, ue, nb, L):
                hd, wk, bi, st, sz = it['hd'], it['wk'], it['bi'], it['st'], it['sz']
                for cc in range(2):
                    p_, pk = proj_fm(hd, wk, cc, bi, st, sz)
                    fw.add('act', lambda e, p_=p_, cc=cc: e.activation(
                        out=ue[:, cc, :, 16:16 + L], in_=p_[:, 0:sz].rearrange("p (b l) -> p b l", b=nb), func=AF.Copy),
                        reads=[pk], writes=[('ue', cc)])

            def window_means(it, ue, A, B, nb, L, first):
                hd, sz = it['hd'], it['sz']
                w = POOL_W[hd]
                W = L + 16
                src, sk = ue, KS('ue', range(2))
                lvl = 1
                tog = 0
                while lvl < w:
                    dst, dk = (A, [('wA',)]) if tog == 0 else (B, [('wB',)])
                    lo = 2 * lvl - 1
                    fw.add('dve', lambda e, src=src, dst=dst, lo=lo, lvl=lvl: e.tensor_tensor(
                        out=dst[:, :, :, lo:W], in0=src[:, :, :, lo:W], in1=src[:, :, :, lo - lvl:W - lvl], op=ALU.add),
                        reads=sk, writes=dk, cost=0.1 + 2 * W * nb / 900.0)
                    src, sk = dst, dk
                    lvl *= 2
                    tog ^= 1
                if first:
                    for cc in range(2):
                        fw.add('dve', lambda e, src=src, cc=cc: e.tensor_tensor(
                            out=src[:, cc, 0, 16:32], in0=src[:, cc, 0, 16:32], in1=self.cfix[:, hd, :], op=ALU.mult),
                            reads=sk + [('const',)], writes=sk)
                pl = pooled[:, :, 0:sz].rearrange("p c (b l) -> p c b l", b=nb)
                fw.add('dve', lambda e, src=src: e.scalar_tensor_tensor(
                    out=pl, in0=src[:, :, :, 16:16 + L], scalar=1.0 / w, in1=ue[:, :, :, 16:16 + L],
                    op0=ALU.mult, op1=ALU.subtract),
                    reads=sk + KS('ue', range(2)), writes=[('pooled',)], cost=0.1 + 2 * sz / 900.0)

            def pool_map(it):
                hd, wk, sz = it['hd'], it['wk'], it['sz']
                s = hd % 2
                for dc in range(2):
                    p_, pk = nd()
                    self.mm_group(p_[:, 0:sz], [(self.mw[s][3][:, cc * 256 + dc * 128: cc * 256 + dc * 128 + 128], pooled[:, cc, 0:sz]) for cc in range(2)],
                                  reads=wk['p'] + [('pooled',)], writes=[pk])
                    col = V_PSC + 2 * hd + dc
                    fw.add('dve', lambda e, p_=p_, dc=dc, col=col: e.tensor_scalar_mul(
                        out=aout[:, dc, 0:sz], in0=p_[:, 0:sz], scalar1=vecs[:, col:col + 1]),
                        reads=[pk, ('const',)], writes=[('aout', dc)])

            def qk_proj(it):
                hd, wk, bi, st, sz = it['hd'], it['wk'], it['bi'], it['st'], it['sz']
                pq, pkq = proj_fm(hd, wk, 2, bi, st, sz)
                fw.add('dve', lambda e: e.scalar_tensor_tensor(out=qt[:, 0:sz], in0=pq[:, 0:sz], scalar=QSCALE, in1=Et[:, 0:sz],
                                                               op0=ALU.mult, op1=ALU.mult),
                       reads=[pkq, ('Et',)], writes=[('qt',)])
                pk_, pkk = proj_fm(hd, wk, 3, bi, st, sz)
                fw.add('dve', lambda e: e.tensor_tensor(out=kt[:, 0:sz], in0=pk_[:, 0:sz], in1=Ei[:, 0:sz], op=ALU.mult),
                       reads=[pkk, ('Ei',)], writes=[('kt',)])

            def gate_proj(it, u0, func, dst, dname):
                hd, wk, bi, st, sz = it['hd'], it['wk'], it['bi'], it['st'], it['sz']
                for cc in range(2):
                    p_, pk = proj_fm(hd, wk, u0 + cc, bi, st, sz)
                    if func == AF.Silu:
                        fw.add('act', lambda e, p_=p_, cc=cc: e.activation(out=dst[:, cc, 0:sz], in_=p_[:, 0:sz], func=AF.Silu),
                               reads=[pk], writes=[(dname, cc)], tbl='st')
                    else:
                        fw.add('act', lambda e, p_=p_, cc=cc: e.activation(out=dst[:, cc, 0:sz], in_=p_[:, 0:sz], func=AF.Tanh, scale=0.5),
                               reads=[pk], writes=[(dname, cc)], tbl='st')

            def siga_aout(it):
                sz = it['sz']
                fw.add('dve', lambda e: e.scalar_tensor_tensor(out=siga[:, :, 0:sz], in0=siga[:, :, 0:sz], scalar=1.0, in1=aout[:, :, 0:sz],
                                                               op0=ALU.add, op1=ALU.mult),
                       reads=KS('siga', range(2)) + KS('aout', range(2)), writes=KS('siga', range(2)), cost=0.1 + 2 * sz / 900.0)

            def front_prompt(it):
                hd, wk, bi, st, sz = it['hd'], it['wk'], it['bi'], it['st'], it['sz']
                s = hd % 2
                nt = sz // 128
                first = (sb == 0 and bi == 0)
                p_, pk = nsm()

                def gkmm(e, p_=p_):
                    ins = None
                    for tt in range(nt):
                        ins = e.matmul(p_[:, tt * 128:(tt + 1) * 128], lhsT=gklr[0:17, st + tt * 128: st + (tt + 1) * 128],
                                       rhs=wgk[0:17, hd * 128:(hd + 1) * 128], start=True, stop=True)
                    return ins
                fw.add('pe', gkmm, reads=[('gklr', bi)] + kwgk, writes=[pk], cost=0.07 * nt)
                softplus_gk(p_, pk, 128, sz)
                fw.add('act', lambda e: e.activation(out=uext[:, :, 0, 0:16], in_=self.ucarry[:, hd, :, :], func=AF.Copy),
                       reads=[('ucarry', hd)], writes=KS('ue', range(2)))
                u_proj(it, uext, 1, sz)
                fw.add('act', lambda e: e.activation(out=self.ucarry[:, hd, :, :], in_=uext[:, :, 0, sz:sz + 16], func=AF.Copy),
                       reads=KS('ue', range(2)), writes=[('ucarry', hd)])
                if sb == NSB - 1 and bi == npb - 1:
                    fw.add('sp', lambda e: e.dma_start(out=self.npp[:, 2 * hd:2 * hd + 2, :], in_=uext[:, :, 0, sz + 1:sz + 16]),
                           reads=KS('ue', range(2)), dma='npp')
                window_means(it, uext, wA, wB, 1, sz, first)
                p2, pk2 = nsm()

                def bmm(e, p2=p2):
                    ins = None
                    for tt in range(nt):
                        ins = e.matmul(p2[:, tt * 128:(tt + 1) * 128], lhsT=spt[:, tt * 128:(tt + 1) * 128],
                                       rhs=self.Umat[:, :], start=True, stop=True)
                    return ins
                fw.add('pe', bmm, reads=[('spt',), ('const',)], writes=[pk2], cost=0.35 * nt)
                fw.add('act', lambda e: e.activation(out=Et[:, 0:sz], in_=p2[:, 0:sz], func=AF.Exp), reads=[pk2], writes=[('Et',)], tbl='el')
                fw.add('act', lambda e: e.activation(out=Ei[:, 0:sz], in_=p2[:, 0:sz], func=AF.Exp, scale=-1.0), reads=[pk2], writes=[('Ei',)], tbl='el')
                for t2 in range(0, nt, 2):
                    pv, pkv = nd()
                    ntt = min(2, nt - t2)

                    def vmm(e, pv=pv, t2=t2, ntt=ntt):
                        ins = None
                        for q in range(ntt):
                            tt = t2 + q
                            for k in range(KC):
                                ins = e.matmul(pv[:, q * 256:(q + 1) * 256], lhsT=xn[:, k, st + tt * 128: st + (tt + 1) * 128],
                                               rhs=self.mw[s][1][:, k * 256:(k + 1) * 256], start=(k == 0), stop=(k == KC - 1))
                        return ins
                    fw.add('pe', vmm, reads=wk['v'] + KS('xn', range(KC), bi), writes=[pkv], cost=0.9 * ntt)
                    fw.add('act', lambda e, pv=pv, t2=t2, ntt=ntt: e.activation(
                        out=vTM[:, t2:t2 + ntt, :], in_=pv[:, 0:ntt * 256].rearrange("p (t c) -> p t c", t=ntt), func=AF.Copy),
                        reads=[pkv], writes=[('vTM', t2 // 2)])
                qk_proj(it)
                pool_map(it)

            def middle_prompt(it):
                hd, wk, bi, st, sz = it['hd'], it['wk'], it['bi'], it['st'], it['sz']
                nt = sz // 128
                vk = [('vTM', i) for i in range((nt + 1) // 2)]

                def ktr(e):
                    ins = None
                    for tt in range(nt):
                        ins = e.transpose(psbf[:, tt * 128:(tt + 1) * 128], kt[:, tt * 128:(tt + 1) * 128], self.ident[:, :])
                    return ins
                fw.add('pe', ktr, reads=[('kt',), ('const',)], writes=[('psbf',)], cost=0.1 * nt)
                fw.add('act', lambda e: e.activation(out=kTM[:, 0:nt, :], in_=psbf[:, 0:nt * 128].rearrange("p (t c) -> p t c", t=nt), func=AF.Copy),
                       reads=[('psbf',)], writes=[('kTM',)])
                pa, pka = nsm()

                def amm(e, pa=pa):
                    ins = None
                    for tt in range(nt):
                        ins = e.matmul(pa[:, tt * 128:(tt + 1) * 128], lhsT=kt[:, tt * 128:(tt + 1) * 128],
                                       rhs=qt[:, tt * 128:(tt + 1) * 128], start=True, stop=True)
                    return ins
                fw.add('pe', amm, reads=[('kt',), ('qt',)], writes=[pka], cost=0.07 * nt)
                fw.add('dve', lambda e: e.tensor_tensor(out=ATm[:, 0:nt, :], in0=pa[:, 0:nt * 128].rearrange("p (t i) -> p t i", t=nt),
                                                        in1=self.mask4[:, 0:nt, :], op=ALU.mult),
                       reads=[pka, ('const',)], writes=[('ATm',)])
                fillers = [lambda: gate_proj(it, 4, AF.Silu, sgt, 'sgt'),
                           lambda: gate_proj(it, 6, AF.Sigmoid, siga, 'siga'),
                           lambda: (gate_proj(it, 8, AF.Sigmoid, sigb, 'sigb'), siga_aout(it))]
                for tt in range(nt):
                    c0 = tt * 128

                    def omm(e, tt=tt, c0=c0):
                        ins = None
                        for vc in range(2):
                            e.matmul(po[vc][:, c0:c0 + 128], lhsT=Sb[:, hd, vc * 128:(vc + 1) * 128], rhs=qt[:, c0:c0 + 128],
                                     start=True, stop=False)
                            ins = e.matmul(po[vc][:, c0:c0 + 128], lhsT=vTM[:, tt, vc * 128:(vc + 1) * 128],
                                           rhs=ATm[:, tt, :], start=False, stop=True)
                        return ins
                    fw.add('pe', omm, reads=[('Sb', hd), ('qt',), ('ATm',)] + vk, writes=pok, cost=0.3)
                    pS, pkS = nsm()
                    fw.add('pe', lambda e, pS=pS, tt=tt: e.matmul(pS[:, 0:256], lhsT=kTM[:, tt, :], rhs=vTM[:, tt, :], start=True, stop=True),
                           reads=[('kTM',)] + vk, writes=[pkS])
                    ecol = Et[:, c0 + 127:c0 + 128]
                    fw.add('dve', lambda e, ecol=ecol: e.tensor_scalar_mul(out=S[:, hd, :], in0=S[:, hd, :], scalar1=ecol),
                           reads=[('S', hd), ('Et',)], writes=[('S', hd)])
                    fw.add('dve', lambda e, ecol=ecol, pS=pS: e.scalar_tensor_tensor(
                        out=S[:, hd, :], in0=pS[:, 0:256], scalar=ecol, in1=S[:, hd, :], op0=ALU.mult, op1=ALU.add),
                        reads=[pkS, ('S', hd), ('Et',)], writes=[('S', hd)])
                    fw.add('act', lambda e: e.activation(out=Sb[:, hd, :], in_=S[:, hd, :], func=AF.Copy),
                           reads=[('S', hd)], writes=[('Sb', hd)])
                    if fillers:
                        fillers.pop(0)()
                while fillers:
                    fillers.pop(0)()
                if sb == NSB - 1 and bi == npb - 1:
                    fw.add('sp', lambda e: e.dma_start(out=self.ngp[hd, :, :], in_=S[:, hd, :]), reads=[('S', hd)], dma='ngp')

            def front_sample(it):
                hd, wk, bi, st, sz = it['hd'], it['wk'], it['bi'], it['st'], it['sz']
                s = hd % 2
                fw.add('sp', lambda e: e.dma_start(out=ustg[:, :, :, :], in_=self.spT[sb, :, hd, :, :, :]), writes=[('ustg',)], dma='ustg')
                fw.add('act', lambda e: e.activation(out=uext_s[:, :, :, 0:16], in_=ustg[:, :, :, :], func=AF.Copy),
                       reads=[('ustg',)], writes=KS('ue', range(2)))
                fw.add('sp', lambda e: e.dma_start(out=S0[:, :, :], in_=self.sgla[sb, hd, :, :, :]), writes=KS('S0', range(NBS)), dma='S0')
                fw.add('act', lambda e: e.activation(out=S0b[:, :, :], in_=S0[:, :, :], func=AF.Copy), reads=KS('S0', range(NBS)), writes=[('S0b',)])
                p_, pk = nsm()
                fw.add('pe', lambda e: e.matmul(p_[0:sz, 0:128], lhsT=gklr[0:17, st:st + sz], rhs=wgk[0:17, hd * 128:(hd + 1) * 128],
                                                start=True, stop=True),
                       reads=[('gklr', bi)] + kwgk, writes=[pk])
                softplus_gk(p_, pk, sz, 128)
                u_proj(it, uext_s, NBS, LS)
                fw.add('sp', lambda e: e.dma_start(out=self.nps[sb, :, hd, :, :, :], in_=uext_s[:, :, :, LS + 1:LS + 16]),
                       reads=KS('ue', range(2)), dma='nps')
                window_means(it, uext_s, wA_s, wB_s, NBS, LS, False)
                p2, pk2 = nsm()
                fw.add('pe', lambda e: e.matmul(p2[:, 0:sz], lhsT=spt[0:sz, 0:128], rhs=self.Umat_s[0:sz, 0:sz], start=True, stop=True),
                       reads=[('spt',), ('const',)], writes=[pk2])
                fw.add('act', lambda e: e.activation(out=Et[:, 0:sz], in_=p2[:, 0:sz], func=AF.Exp), reads=[pk2], writes=[('Et',)], tbl='el')
                fw.add('act', lambda e: e.activation(out=Ei[:, 0:sz], in_=p2[:, 0:sz], func=AF.Exp, scale=-1.0), reads=[pk2], writes=[('Ei',)], tbl='el')
                pv, pkv = nd()
                self.mm_group(pv[0:sz, 0:256], [(xn[:, k, st:st + sz], self.mw[s][1][:, k * 256:(k + 1) * 256]) for k in range(KC)],
                              reads=wk['v'] + KS('xn', range(KC), bi), writes=[pkv])
                fw.add('act', lambda e: e.activation(out=vTM[0:sz, 0, :], in_=pv[0:sz, 0:256], func=AF.Copy),
                       reads=[pkv], writes=[('vTM', 0)])
                qk_proj(it)
                pool_map(it)

            def middle_sample(it):
                hd, wk, bi, st, sz = it['hd'], it['wk'], it['bi'], it['st'], it['sz']
                fw.add('pe', lambda e: e.transpose(psbf[0:sz, 0:128], kt[:, 0:sz], self.ident[:, :]),
                       reads=[('kt',), ('const',)], writes=[('psbf',)])
                fw.add('act', lambda e: e.activation(out=kTM[0:sz, 0, :], in_=psbf[0:sz, 0:128], func=AF.Copy),
                       reads=[('psbf',)], writes=[('kTM',)])
                for b in range(NBS):
                    fw.add('dve', lambda e, b=b: e.tensor_scalar_mul(out=kmask[0:sz, b, :], in0=kTM[0:sz, 0, :],
                                                                     scalar1=self.onehot[0:sz, b:b + 1]),
                           reads=[('kTM',), ('const',)], writes=[('kmask', b)])
                pa, pka = nsm()
                fw.add('pe', lambda e: e.matmul(pa[0:sz, 0:sz], lhsT=kt[:, 0:sz], rhs=qt[:, 0:sz], start=True, stop=True),
                       reads=[('kt',), ('qt',)], writes=[pka])
                fw.add('dve', lambda e: e.tensor_tensor(out=ATm[0:sz, 0, 0:sz], in0=pa[0:sz, 0:sz], in1=self.mask_s[0:sz, 0:sz], op=ALU.mult),
                       reads=[pka, ('const',)], writes=[('ATm',)])

                def omm(e):
                    ins = None
                    for vc in range(2):
                        for b in range(NBS):
                            e.matmul(po[vc][:, b * LS:(b + 1) * LS], lhsT=vTM[0:sz, 0, vc * 128:(vc + 1) * 128],
                                     rhs=ATm[0:sz, 0, b * LS:(b + 1) * LS], start=True, stop=False)
                            ins = e.matmul(po[vc][:, b * LS:(b + 1) * LS], lhsT=S0b[:, b, vc * 128:(vc + 1) * 128],
                                           rhs=qt[:, b * LS:(b + 1) * LS], start=False, stop=True)
                    return ins
                fw.add('pe', omm, reads=[('S0b',), ('qt',), ('ATm',), ('vTM', 0)], writes=pok, cost=1.2)
                gate_proj(it, 4, AF.Silu, sgt, 'sgt')
                for b in range(NBS):
                    pS, pkS = nsm()
                    fw.add('pe', lambda e, pS=pS, b=b: e.matmul(pS[:, 0:256], lhsT=kmask[0:sz, b, :], rhs=vTM[0:sz, 0, :], start=True, stop=True),
                           reads=[('kmask', b), ('vTM', 0)], writes=[pkS])
                    ecol = Et[:, b * LS + LS - 1:b * LS + LS]
                    fw.add('dve', lambda e, ecol=ecol, b=b: e.tensor_scalar_mul(out=S0[:, b, :], in0=S0[:, b, :], scalar1=ecol),
                           reads=[('S0', b), ('Et',)], writes=[('S0', b)])
                    fw.add('dve', lambda e, ecol=ecol, pS=pS, b=b: e.scalar_tensor_tensor(
                        out=S0[:, b, :], in0=pS[:, 0:256], scalar=ecol, in1=S0[:, b, :], op0=ALU.mult, op1=ALU.add),
                        reads=[pkS, ('S0', b), ('Et',)], writes=[('S0', b)])
                fw.add('sp', lambda e: e.dma_start(out=self.ngs[sb, hd, :, :, :], in_=S0[:, :, :]),
                       reads=KS('S0', range(NBS)), dma='ngs')
                gate_proj(it, 6, AF.Sigmoid, siga, 'siga')
                gate_proj(it, 8, AF.Sigmoid, sigb, 'sigb')
                siga_aout(it)

            def tail(it):
                hd, wk, bi, st, sz = it['hd'], it['wk'], it['bi'], it['st'], it['sz']
                s = hd % 2
                fw.add('act', lambda e: e.activation(out=osb[:, 0, 0:sz], in_=po[0][:, 0:sz], func=AF.Copy),
                       reads=[pok[0]], writes=[('osb', 0)])
                fw.add('act', lambda e: e.activation(out=osb[:, 1, 0:sz], in_=po[1][:, 0:sz], func=AF.Copy),
                       reads=[pok[1]], writes=[('osb', 1)])
                self.rms_stats(osb[:, :, 0:sz], sz, self.ones_v, 2, rso, KS('osb', range(2)), 'o')
                for vc in range(2):
                    fw.add('dve', lambda e, vc=vc: e.scalar_tensor_tensor(
                        out=osb[:, vc, 0:sz], in0=osb[:, vc, 0:sz], scalar=vecs[:, V_GLA + vc:V_GLA + vc + 1],
                        in1=rso[:, 0:sz], op0=ALU.mult, op1=ALU.mult),
                        reads=[('osb', vc), ('rstd', 'o'), ('const',)], writes=[('osb', vc)])
                fw.add('dve', lambda e: e.tensor_tensor(out=osb[:, :, 0:sz], in0=osb[:, :, 0:sz], in1=sgt[:, :, 0:sz], op=ALU.mult),
                       reads=KS('osb', range(2)) + KS('sgt', range(2)), writes=KS('osb', range(2)), cost=0.1 + 2 * sz / 900.0)
                fw.add('dve', lambda e: e.scalar_tensor_tensor(out=osb[:, :, 0:sz], in0=sigb[:, :, 0:sz], scalar=1.0, in1=osb[:, :, 0:sz],
                                                               op0=ALU.add, op1=ALU.mult),
                       reads=KS('osb', range(2)) + KS('sigb', range(2)), writes=KS('osb', range(2)), cost=0.1 + 2 * sz / 900.0)
                fw.add('dve', lambda e: e.tensor_tensor(out=mix[:, :, 0:sz], in0=osb[:, :, 0:sz], in1=siga[:, :, 0:sz], op=ALU.add),
                       reads=KS('osb', range(2)) + KS('siga', range(2)), writes=[('mix',)], cost=0.1 + 2 * sz / 900.0)
                for c in range(KC):
                    p_, pk = nd()
                    self.mm_group(p_[:, 0:sz], [(self.mw[s][2][:, kc * 1024 + c * 128: kc * 1024 + c * 128 + 128], mix[:, kc, 0:sz]) for kc in range(2)],
                                  reads=wk['o'] + [('mix',)], writes=[pk])
                    fw.add('dve', lambda e, p_=p_, c=c: e.scalar_tensor_tensor(
                        out=h[:, c, st:st + sz], in0=p_[:, 0:sz], scalar=0.5, in1=h[:, c, st:st + sz], op0=ALU.mult, op1=ALU.add),
                        reads=[pk, ('h', c, bi)], writes=[('h', c, bi)])

            def pre_keys(s_):
                d = {u: [('wfm', s_, u)] for u in range(10)}
                d.update({'v': [('wv', s_)], 'o': [('wo', s_)], 'p': [('pw', s_)]})
                return d
            wks = {}
            for hd0 in (0, 1):
                if hd0 in pre:
                    wks[hd0] = pre_keys(hd0)
                else:
                    if self.mw[hd0 % 2] is None:
                        self.alloc_mw(es, hd0 % 2)
                    wks[hd0] = load_head(hd0)
            items = []
            for hd in range(4):
                for bi, (st, sz) in enumerate(pblocks):
                    items.append(dict(hd=hd, bi=bi, st=st, sz=sz, kind='p', last=False))
                items.append(dict(hd=hd, bi=npb, st=TP, sz=TSM, kind='s', last=True))

            def front(it):
                it['wk'] = wks[it['hd']]
                (front_prompt if it['kind'] == 'p' else front_sample)(it)

            def middle(it):
                (middle_prompt if it['kind'] == 'p' else middle_sample)(it)
            def th_front(itn):
                def f():
                    self.dense, self.ps_rot = [0, 1], [3]
                    front(itn)
                return f

            def th_tail(itc):
                def f():
                    self.dense, self.ps_rot = [2], [4]
                    tail(itc)
                return f
            front(items[0])
            for i, it in enumerate(items):
                self.dense, self.ps_rot = [0, 1, 2], [3, 4]
                middle(it)
                if i + 1 < len(items):
                    fw.merged([th_front(items[i + 1]), th_tail(it)])
                else:
                    self.dense, self.ps_rot = [0, 1, 2], [3, 4]
                    tail(it)
                if it['last'] and it['hd'] + 2 < 4:
                    wks[it['hd'] + 2] = load_head(it['hd'] + 2)
            fw.flush()

    def alloc_mw(self, es, s):
        self.mw[s] = (self.T(es, "wfm%d" % s, [128, 10, 1024], BF16), self.T(es, "wv%d" % s, [128, 2048], BF16),
                      self.T(es, "wo%d" % s, [128, 2048], BF16), self.T(es, "pw%d" % s, [128, 512], BF16))

    def load_head(self, hd):
        s = hd % 2
        wfm, wv, wo, pw = self.mw[s]
        keys = {}
        for u in range(10):
            keys[u] = self.load_cast(wfm[:, u, :], ('wfm', s, u), self.win_fm[hd, u].rearrange("p k n -> p (k n)"), 1024)
        keys['v'] = self.load_cast(wv[:, :], ('wv', s), self.win_v[hd].rearrange("p k n -> p (k n)"), 2048)
        keys['o'] = self.load_cast(wo[:, :], ('wo', s), self.wout_d[hd].rearrange("p k n -> p (k n)"), 2048)
        keys['p'] = self.load_cast(pw[:, :], ('pw', s), self.poolw_d[hd].rearrange("p k n -> p (k n)"), 512)
        return keys

    def build(self):
        nc = self.nc
        self.xT = self.din("xT", [NSB, 128, KC, TS])
        self.pT = self.din("pT", [NSB, 128, 2, TS])
        self.spT = self.din("spT", [NSB, 128, 4, 2, NBS, 16])
        self.sgla = self.din("sgla", [NSB, 4, 128, NBS, 256])
        self.w1g = self.din("w1g", [FC, 128, KC, 128])
        self.w1u = self.din("w1u", [FC, 128, KC, 128])
        self.w1d = self.din("w1d", [KC, 128, FC, 128])
        self.w2g = self.din("w2g", [FC, 128, KC, 128])
        self.w2u = self.din("w2u", [FC, 128, KC, 128])
        self.w2d = self.din("w2d", [KC, 128, FC, 128])
        self.win_fm = self.din("win_fm", [4, 10, 128, KC, 128])
        self.win_v = self.din("win_v", [4, 128, KC, 256])
        self.win_gk = self.din("win_gk", [128, KC, 16])
        self.wout_d = self.din("wout", [4, 128, 2, 1024])
        self.poolw_d = self.din("poolw", [4, 128, 2, 256])
        self.wgk_d = self.din("wgk", [32, 512])
        self.wpg_d = self.din("wpg", [KC, 128, KC, 128])
        self.wpp_d = self.din("wpp", [KC, 128, 2, 128])
        self.vecs_d = self.din("vecs", [128, NVEC])
        self.ident_d = self.din("ident", [128, 128], BF16)
        self.ones_d_d = self.din("ones_d", [128, 128], BF16)
        self.ones_v_d = self.din("ones_v", [128, 128], BF16)
        self.Umat_d = self.din("Umat", [128, 128])
        self.Umat_s_d = self.din("Umat_s", [32, 32])
        self.mask4_d = self.din("mask4", [128, 4, 128])
        self.mask_s_d = self.din("mask_s", [32, 32])
        self.onehot_d = self.din("onehot", [32, NBS])
        self.cfix_d = self.din("cfix", [128, 4, 16])
        self.yT = self.dout("yT", [NSB, 128, KC, TS])
        self.npp = self.dout("npp", [128, KC, 15])
        self.ngp = self.dout("ngp", [4, 128, 256])
        self.nps = self.dout("nps", [NSB, 128, 4, 2, NBS, 15])
        self.ngs = self.dout("ngs", [NSB, 4, 128, NBS, 256])
        with ExitStack() as es:
            self.fw = fw = FW(nc, es)
            T = lambda name, shape, dt: self.T(es, name, shape, dt)
            self.h = T("h", [128, KC, TS], F32)
            self.xn = T("xn", [128, KC, TS], BF16)
            self.S = T("S", [128, 4, 256], F32)
            self.Sb = T("Sb", [128, 4, 256], BF16)
            self.ucarry = T("ucarry", [128, 4, 2, 16], F32)
            self.vecs = T("vecs", [128, NVEC], F32)
            self.ident = T("ident", [128, 128], BF16)
            self.ones_d = T("ones_d", [128, 128], BF16)
            self.ones_v = T("ones_v", [128, 128], BF16)
            self.Umat = T("Umat", [128, 128], F32)
            self.Umat_s = T("Umat_s", [32, 32], F32)
            self.mask4 = T("mask4", [128, 4, 128], F32)
            self.mask_s = T("mask_s", [32, 32], F32)
            self.onehot = T("onehot", [32, NBS], F32)
            self.cfix = T("cfix", [128, 4, 16], F32)
            self.ps = [es.enter_context(nc.psum_tensor("ps%d" % i, [128, 512], F32)) for i in range(7)]
            self.psbf = es.enter_context(nc.psum_tensor("psbf", [128, 1024], BF16))
            self.ps_rot = list(range(7))
            self.ps_i = 0
            for i, (t, d) in enumerate([(self.vecs, self.vecs_d), (self.ident, self.ident_d), (self.ones_d, self.ones_d_d),
                                        (self.ones_v, self.ones_v_d), (self.Umat, self.Umat_d), (self.Umat_s, self.Umat_s_d),
                                        (self.mask4, self.mask4_d), (self.mask_s, self.mask_s_d), (self.onehot, self.onehot_d),
                                        (self.cfix, self.cfix_d)]):
                fw.add('sp', lambda e, t=t, d=d: e.dma_start(out=t[:], in_=d), writes=[('const', i)], dma='const')
            fw.add('dve', lambda e: e.memset(self.S[:, :, :], 0.0), writes=KS('S', range(4)))
            fw.add('dve', lambda e: e.memset(self.Sb[:, :, :], 0.0), writes=KS('Sb', range(4)))
            fw.add('dve', lambda e: e.memset(self.ucarry[:, :, :, :], 0.0), writes=KS('ucarry', range(4)))
            fw.flush()
            self.mw = [None, None]
            with ExitStack() as esw:
                self.alloc_mw(esw, 0)
                self.alloc_mw(esw, 1)
                self.phase_A(0, prefetch=lambda: (self.load_head(0), self.load_head(1)))
                self.phase_mixer(0, pre=(0, 1))
            self.mw = [None, None]
            with ExitStack() as esw:
                self.alloc_mw(esw, 0)
                self.phase_C(0, next_A=(1, lambda: self.load_head(0)))
                self.phase_mixer(1, pre=(0,))
            self.mw = [None, None]
            self.phase_C(1)


def _fm_units(W, c0, width=128):
    return np.ascontiguousarray(W[:, c0:c0 + width].reshape(KC, 128, width).transpose(1, 0, 2))


def _consts():
    bf = ml_dtypes.bfloat16
    j = np.arange(128)[:, None]
    i = np.arange(128)[None, :]
    causal = (j <= i)
    Umat = np.where(causal, -1.0 / 16.0, 0.0).astype(np.float32)
    mask4 = np.repeat(causal.astype(np.float32)[:, None, :], 4, axis=1)
    js = np.arange(32)[:, None]
    is_ = np.arange(32)[None, :]
    cs = (js <= is_) & ((js // LS) == (is_ // LS))
    Umat_s = np.where(cs, -1.0 / 16.0, 0.0).astype(np.float32)
    mask_s = cs.astype(np.float32)
    onehot = ((np.arange(32)[:, None] // LS) == np.arange(NBS)[None, :]).astype(np.float32)
    cfix = np.zeros((128, 4, 16), np.float32)
    for g, w in enumerate(POOL_W):
        t = np.arange(16)
        cfix[:, g, :] = (w / np.minimum(w, t + 1))[None, :]
    return dict(
        ident=np.eye(128, dtype=np.float32).astype(bf),
        ones_d=np.full((128, 128), 1.0 / D, np.float32).astype(bf),
        ones_v=np.full((128, 128), 1.0 / 256, np.float32).astype(bf),
        Umat=Umat, Umat_s=Umat_s, mask4=np.ascontiguousarray(mask4), mask_s=mask_s, onehot=onehot, cfix=cfix)


def _prep_shared(inp):
    f = lambda a: np.asarray(a, dtype=np.float32)
    out = {}
    for nm, key in (("w1", "ffn1"), ("w2", "ffn2")):
        Wg, Wu, Wd = f(inp[key + "_w_gate"])[0], f(inp[key + "_w_up"])[0], f(inp[key + "_w_down"])[0]
        out[nm + "g"] = np.ascontiguousarray(Wg.reshape(KC, 128, FC, 128).transpose(2, 1, 0, 3))
        out[nm + "u"] = np.ascontiguousarray(Wu.reshape(KC, 128, FC, 128).transpose(2, 1, 0, 3))
        out[nm + "d"] = np.ascontiguousarray(Wd.reshape(FC, 128, KC, 128).transpose(2, 1, 0, 3))
    Win = f(inp["w_in"])[0]
    OU, OQ, OK_, OV, OGK, OG, OGA, OGB = 0, 1024, 1536, 2048, 3072, 3088, 4112, 5136
    fm = np.zeros((4, 10, 128, KC, 128), np.float32)
    wv = np.zeros((4, 128, KC, 256), np.float32)
    for hd in range(4):
        cols = [OU + hd * 256, OU + hd * 256 + 128, OQ + hd * 128, OK_ + hd * 128,
                OG + hd * 256, OG + hd * 256 + 128, OGA + hd * 256, OGA + hd * 256 + 128,
                OGB + hd * 256, OGB + hd * 256 + 128]
        for u, c0 in enumerate(cols):
            fm[hd, u] = _fm_units(Win, c0)
        wv[hd] = _fm_units(Win, OV + hd * 256, 256)
    out["win_fm"] = fm
    out["win_v"] = wv
    out["win_gk"] = _fm_units(Win, OGK, 16)
    out["wout"] = np.ascontiguousarray(f(inp["w_out"])[0].reshape(4, 2, 128, 1024).transpose(0, 2, 1, 3))
    out["poolw"] = np.ascontiguousarray(f(inp["pool_w"])[0].reshape(4, 2, 128, 256).transpose(0, 2, 1, 3))
    wgk = np.zeros((32, 512), np.float32)
    wgk[0:16] = f(inp["w_gk_up"])[0]
    wgk[16] = f(inp["b_gk"])[0]
    out["wgk"] = wgk
    out["wpg"] = np.ascontiguousarray(f(inp["w_ple_gate"])[0].reshape(KC, 128, KC, 128).transpose(2, 1, 0, 3))
    out["wpp"] = np.ascontiguousarray(f(inp["w_ple_proj"])[0].reshape(2, 128, KC, 128).transpose(2, 1, 0, 3))
    vecs = np.zeros((128, NVEC), np.float32)
    for col, v in ((V_FFN1, inp["ffn1_norm"][0]), (V_MIX, inp["mix_norm"][0]), (V_FFN2, inp["ffn2_norm"][0]),
                   (V_PLE, inp["ple_norm"][0]), (V_FIN, inp["final_norm"]), (V_PSC, inp["pool_scale"][0])):
        vecs[:, col:col + 8] = f(v).reshape(KC, 128).T
    vecs[:, V_GLA:V_GLA + 2] = f(inp["gla_norm"][0]).reshape(2, 128).T
    out["vecs"] = vecs
    out.update(_consts())
    return out


def _prep_core(inp, c):
    f = lambda a: np.asarray(a, dtype=np.float32)
    xp, xs = f(inp["x_prompt"]), f(inp["x_sample"])
    pp, psm = f(inp["p_prompt"])[0], f(inp["p_sample"])[0]
    sp, sg = f(inp["state_pool"])[0], f(inp["state_gla"])[0]
    xT = np.zeros((NSB, 128, KC, TS), np.float32)
    pT = np.zeros((NSB, 128, 2, TS), np.float32)
    spT = np.zeros((NSB, 128, 4, 2, NBS, 16), np.float32)
    sgla = np.zeros((NSB, 4, 128, NBS, 256), np.float32)
    for sb in range(NSB):
        b0 = c * 16 + sb * NBS
        tok = np.concatenate([xp[c, sb * TP:(sb + 1) * TP], xs[b0:b0 + NBS].reshape(TSM, D)], axis=0)
        xT[sb] = tok.T.reshape(KC, 128, TS).transpose(1, 0, 2)
        ptok = np.concatenate([pp[c, sb * TP:(sb + 1) * TP], psm[b0:b0 + NBS].reshape(TSM, PLE)], axis=0)
        pT[sb] = ptok.T.reshape(2, 128, TS).transpose(1, 0, 2)
        st = sp[b0:b0 + NBS]
        st = st.transpose(2, 0, 1).reshape(4, 2, 128, NBS, 15)
        spT[sb, :, :, :, :, 1:16] = st.transpose(2, 0, 1, 3, 4)
        sgla[sb] = sg[b0:b0 + NBS].transpose(1, 2, 0, 3)
    return dict(xT=xT, pT=pT, spT=spT, sgla=sgla)


_PROG = {}


def _get_prog(stop_after=None):
    if stop_after not in _PROG:
        _PROG[stop_after] = Prog(stop_after)
    return _PROG[stop_after]


def kernel(**inputs):
    stop_after = inputs.pop("_stop_after", None)
    prog = _get_prog(stop_after)
    shared = _prep_shared(inputs)
    in_maps = []
    for c in range(NCORES):
        m = dict(shared)
        m.update(_prep_core(inputs, c))
        in_maps.append(m)
    res = run_bass_kernel_spmd(prog.nc, in_maps, core_ids=list(range(NCORES)))
    R = res.results
    B, SEQ = 8, 2048
    y_prompt = np.zeros((B, SEQ, D), np.float32)
    y_sample = np.zeros((128, LS, D), np.float32)
    npool_p = np.zeros((1, B, 15, D), np.float32)
    ngla_p = np.zeros((1, B, 4, 128, 256), np.float32)
    npool_s = np.zeros((1, 128, 15, D), np.float32)
    ngla_s = np.zeros((1, 128, 4, 128, 256), np.float32)
    for c in range(NCORES):
        r = R[c]
        yT = np.asarray(r["yT"], dtype=np.float32)
        for sb in range(NSB):
            tok = yT[sb].transpose(1, 0, 2).reshape(D, TS).T
            y_prompt[c, sb * TP:(sb + 1) * TP] = tok[:TP]
            b0 = c * 16 + sb * NBS
            y_sample[b0:b0 + NBS] = tok[TP:].reshape(NBS, LS, D)
            nps = np.asarray(r["nps"], dtype=np.float32)[sb]
            npool_s[0, b0:b0 + NBS] = nps.transpose(3, 4, 1, 2, 0).reshape(NBS, 15, D)
            ngs = np.asarray(r["ngs"], dtype=np.float32)[sb]
            ngla_s[0, b0:b0 + NBS] = ngs.transpose(2, 0, 1, 3)
        npp = np.asarray(r["npp"], dtype=np.float32)
        npool_p[0, c] = npp.transpose(2, 1, 0).reshape(15, D)
        ngla_p[0, c] = np.asarray(r["ngp"], dtype=np.float32)
    return (y_prompt, y_sample, npool_p, ngla_p, npool_s, ngla_s)
```
